# Optimizing a Trainium2 kernel written in Bass

```python
import jax
import jax.numpy as jnp
from jax import lax
import numpy as np

D_MODEL = 1024
BATCH = 8
SEQ = 2048
DEPTH = 1

CTX_LEN = 256
GRID_W = 64
CHUNK = 64
EPS = 1e-6

RET_HEADS = 4
RET_DK = 256
RET_DV = 512
RET_QK = RET_HEADS * RET_DK
RET_V = RET_HEADS * RET_DV
ROPE_THETA = 10000.0
ROPE_FREQS = RET_DK // 4

SSD_INNER = 2 * D_MODEL
SSD_HEADDIM = 64
SSD_HEADS = SSD_INNER // SSD_HEADDIM
SSD_GROUPS = 8
SSD_HPG = SSD_HEADS // SSD_GROUPS
SSD_STATE = 128
SSD_BC = SSD_GROUPS * SSD_STATE
SSD_CONV_W = 5
SSD_CONV_DIM = SSD_INNER + 2 * SSD_BC
SSD_NORM_GROUP = SSD_INNER // SSD_GROUPS

N_BRANCH = 2
IN_SIZES = (RET_QK, RET_QK, RET_V, RET_V, SSD_INNER, SSD_CONV_DIM, 2 * SSD_HEADS, N_BRANCH * D_MODEL)
IN_OFFSETS = tuple(int(o) for o in np.cumsum(IN_SIZES)[:-1])
IN_DIM = int(sum(IN_SIZES))

kernel_name = 'hybrid_retention_ssd_dit_layer'


def rmsnorm(x, w):
    xf = x.astype(jnp.float32)
    return xf * lax.rsqrt(jnp.mean(xf * xf, axis=-1, keepdims=True) + EPS) * w.astype(jnp.float32)


def grid_rope_tables(n_tokens):
    rows = n_tokens // GRID_W
    row = jnp.repeat(jnp.arange(rows, dtype=jnp.float32), GRID_W)
    col = jnp.tile(jnp.arange(GRID_W, dtype=jnp.float32), rows)
    inv_freq = ROPE_THETA ** (-jnp.arange(ROPE_FREQS, dtype=jnp.float32) / ROPE_FREQS)
    ang = jnp.stack([row[:, None] * inv_freq, col[:, None] * inv_freq], axis=1)
    return jnp.cos(ang), jnp.sin(ang)


def apply_grid_rope(x, cos, sin):
    b, L, h, d = x.shape
    xb = x.reshape(b, L, h, 2, 2, ROPE_FREQS)
    x1, x2 = xb[..., 0, :], xb[..., 1, :]
    cs, sn = cos[None, :, None], sin[None, :, None]
    return jnp.stack([x1 * cs - x2 * sn, x1 * sn + x2 * cs], axis=-2).reshape(b, L, h, d)


def dwconv_centred(x, w, bias):
    ch = x.shape[-1]
    y = lax.conv_general_dilated(x, w[:, None, :].astype(x.dtype), (1,),
                                 [(SSD_CONV_W // 2, SSD_CONV_W // 2)],
                                 dimension_numbers=('NWC', 'WIO', 'NWC'), feature_group_count=ch)
    return y + bias


def chunk_scan(q, k, v, log_a, s0, include_diag):
    f32 = jnp.float32
    b, L, g, n = q.shape
    r, p = v.shape[-2:]
    nc = L // CHUNK
    qc = q.astype(f32).reshape(b, nc, CHUNK, g, n)
    kc = k.astype(f32).reshape(b, nc, CHUNK, g, n)
    vc = v.astype(f32).reshape(b, nc, CHUNK, g, r, p)
    acum = jnp.cumsum(log_a.astype(f32).reshape(b, nc, CHUNK, g, r), axis=2)
    idx = jnp.arange(CHUNK)
    mask = (idx[:, None] >= idx[None, :]) if include_diag else (idx[:, None] > idx[None, :])
    seg = acum[:, :, :, None] - acum[:, :, None, :]
    decay = jnp.exp(jnp.where(mask[:, :, None, None], seg, -jnp.inf))
    scores = jnp.einsum('bcign,bcjgn->bcijg', qc, kc)
    y_intra = jnp.einsum('bcijgr,bcjgrp->bcigrp', scores[..., None] * decay, vc)

    def step(s, inp):
        q_ch, k_ch, v_ch, a_ch = inp
        y_inter = jnp.einsum('bign,bgrpn->bigrp', q_ch, s) * jnp.exp(a_ch)[..., None]
        to_end = jnp.exp(a_ch[:, -1:] - a_ch)
        s_new = (jnp.exp(a_ch[:, -1])[..., None, None] * s
                 + jnp.einsum('bjgn,bjgrp->bgrpn', k_ch, to_end[..., None] * v_ch))
        return s_new, y_inter

    chunk_major = lambda t: jnp.moveaxis(t, 1, 0)
    s_final, y_inter = lax.scan(step, s0, (chunk_major(qc), chunk_major(kc), chunk_major(vc), chunk_major(acum)))
    y = y_intra + jnp.moveaxis(y_inter, 0, 1)
    return y.reshape(b, L, g, r, p), s_final


def bidir_scan(q_c, k_c, v_c, a_c, q_l, k_l, v_l, a_l):
    b, _, g, n = q_c.shape
    r, p = v_c[0].shape[-2:]
    s0 = jnp.zeros((b, g, r, p, n), jnp.float32)
    flip = lambda t: jnp.flip(t, axis=1)
    yc_f, sc_f = chunk_scan(q_c, k_c, v_c[0], a_c[0], s0, True)
    yl_f, _ = chunk_scan(q_l, k_l, v_l[0], a_l[0], sc_f, True)
    yc_b, sc_b = chunk_scan(flip(q_c), flip(k_c), flip(v_c[1]), flip(a_c[1]), s0, False)
    yl_b, _ = chunk_scan(flip(q_l), flip(k_l), flip(v_l[1]), flip(a_l[1]), sc_b, False)
    return yc_f + flip(yc_b), yl_f + flip(yl_b)


def retention_inputs(q, k, v, log_gamma, rope):
    b, L, _ = q.shape
    q = q.reshape(b, L, RET_HEADS, RET_DK) * (RET_DK ** -0.5)
    k = k.reshape(b, L, RET_HEADS, RET_DK)
    if rope is not None:
        q = apply_grid_rope(q, rope[0], rope[1])
        k = apply_grid_rope(k, rope[0], rope[1])
    v = v.reshape(b, L, RET_HEADS, 1, RET_DV)
    a = tuple(jnp.broadcast_to(log_gamma[d][None, None, :, None], (b, L, RET_HEADS, 1)) for d in range(2))
    return q, k, (v, v), a


def ssd_inputs(xbc, dt_raw, p):
    f32 = jnp.float32
    b, L, _ = xbc.shape
    xbc = jax.nn.silu(dwconv_centred(xbc, p['ssd_conv_w'], p['ssd_conv_b']))
    xs, bm, cm = jnp.split(xbc, [SSD_INNER, SSD_INNER + SSD_BC], axis=-1)
    xs = xs.reshape(b, L, SSD_GROUPS, SSD_HPG, SSD_HEADDIM)
    bm = bm.reshape(b, L, SSD_GROUPS, SSD_STATE)
    cm = cm.reshape(b, L, SSD_GROUPS, SSD_STATE)
    dt = jax.nn.softplus(dt_raw.astype(f32).reshape(b, L, 2, SSD_HEADS) + p['ssd_dt_bias'].astype(f32))
    A = -jnp.exp(p['ssd_a_log'].astype(f32))
    a = tuple((dt[:, :, d] * A[d]).reshape(b, L, SSD_GROUPS, SSD_HPG) for d in range(2))
    v = tuple(xs * dt[:, :, d].reshape(b, L, SSD_GROUPS, SSD_HPG)[..., None] for d in range(2))
    return cm, bm, v, a, xs


def merge_branches(y_ret, g_ret, y_ssd, x_ssd, z, gates, p):
    b, L = g_ret.shape[:2]
    yr = y_ret.reshape(b, L, RET_HEADS, RET_DV)
    mu = jnp.mean(yr, axis=-1, keepdims=True)
    var = jnp.mean(jnp.square(yr - mu), axis=-1, keepdims=True)
    yr = ((yr - mu) * lax.rsqrt(var + EPS)).reshape(b, L, RET_V) * p['ret_gn_w']
    o_ret = (yr * jax.nn.silu(g_ret)) @ p['w_ret_o']
    ys = y_ssd + p['ssd_D'].reshape(SSD_GROUPS, SSD_HPG)[:, :, None] * x_ssd
    ys = (ys.reshape(b, L, SSD_INNER) * jax.nn.silu(z)).reshape(b, L, SSD_GROUPS, SSD_NORM_GROUP)
    ys = (ys * lax.rsqrt(jnp.mean(ys * ys, axis=-1, keepdims=True) + EPS)).reshape(b, L, SSD_INNER)
    o_ssd = (ys * p['ssd_norm_w']) @ p['w_ssd_o']
    g_r, g_s = jnp.split(gates, 2, axis=-1)
    return (jax.nn.sigmoid(g_r) * o_ret + jax.nn.sigmoid(g_s) * o_ssd) @ p['w_out']


def hybrid_layer(x, xc, mod_l, mod_c, p, rope, update_ctx):
    shift_l, scale_l, gate_l = jnp.split(mod_l[:, None, :], 3, axis=-1)
    shift_c, scale_c, gate_c = jnp.split(mod_c[None, None, :], 3, axis=-1)
    h_l = rmsnorm(x, p['norm_pre']) * (1.0 + scale_l) + shift_l
    h_c = rmsnorm(xc, p['norm_pre']) * (1.0 + scale_c) + shift_c
    q_l, k_l, v_l, g_l, z_l, xbc_l, dt_l, gt_l = jnp.split(h_l @ p['w_in'], IN_OFFSETS, axis=-1)
    q_c, k_c, v_c, g_c, z_c, xbc_c, dt_c, gt_c = jnp.split(h_c @ p['w_in'], IN_OFFSETS, axis=-1)

    log_gamma = jax.nn.log_sigmoid(p['ret_decay'].astype(jnp.float32))
    rq_c, rk_c, rv_c, ra_c = retention_inputs(q_c, k_c, v_c, log_gamma, None)
    rq_l, rk_l, rv_l, ra_l = retention_inputs(q_l, k_l, v_l, log_gamma, rope)
    yr_c, yr_l = bidir_scan(rq_c, rk_c, rv_c, ra_c, rq_l, rk_l, rv_l, ra_l)

    sc_c, sb_c, sv_c, sa_c, sx_c = ssd_inputs(xbc_c, dt_c, p)
    sc_l, sb_l, sv_l, sa_l, sx_l = ssd_inputs(xbc_l, dt_l, p)
    ys_c, ys_l = bidir_scan(sc_c, sb_c, sv_c, sa_c, sc_l, sb_l, sv_l, sa_l)

    out_l = merge_branches(yr_l, g_l, ys_l, sx_l, z_l, gt_l, p)
    x_new = (x + gate_l * rmsnorm(out_l, p['norm_post'])).astype(x.dtype)
    if update_ctx:
        out_c = merge_branches(yr_c, g_c, ys_c, sx_c, z_c, gt_c, p)
        xc = (xc + gate_c * rmsnorm(out_c, p['norm_post'])).astype(xc.dtype)
    return x_new, xc


def setup_inputs(seed: int = 0) -> dict:
    key = jax.random.key(seed)
    ks = jax.random.split(key, 20)
    nrm = lambda k, shape, scale: jax.random.normal(k, shape, jnp.float32) * scale
    x = nrm(ks[0], (BATCH, SEQ, D_MODEL), 1.0)
    c = nrm(ks[1], (BATCH, D_MODEL), 1.0)
    ctx = nrm(ks[2], (BATCH, CTX_LEN, D_MODEL), 1.0)
    c_ctx = nrm(ks[3], (D_MODEL,), 1.0)
    w_mod = nrm(ks[4], (DEPTH, D_MODEL, 3 * D_MODEL), 0.5 * D_MODEL ** -0.5)
    b_mod = nrm(ks[5], (DEPTH, 3 * D_MODEL), 0.02)
    norm_pre_w = 1.0 + nrm(ks[6], (DEPTH, D_MODEL), 0.02)
    norm_post_w = 1.0 + nrm(ks[7], (DEPTH, D_MODEL), 0.02)
    w_in = nrm(ks[8], (DEPTH, D_MODEL, IN_DIM), D_MODEL ** -0.5)
    gamma0 = 1.0 - 2.0 ** (-5.0 - np.arange(RET_HEADS))
    ret_decay = (jnp.asarray(np.log(gamma0 / (1.0 - gamma0)), jnp.float32)[None, None, :]
                 + nrm(ks[9], (DEPTH, 2, RET_HEADS), 0.1))
    ret_gn_w = 1.0 + nrm(ks[10], (DEPTH, RET_V), 0.02)
    ssd_conv_w = nrm(ks[11], (DEPTH, SSD_CONV_W, SSD_CONV_DIM), SSD_CONV_W ** -0.5)
    ssd_conv_b = nrm(ks[12], (DEPTH, SSD_CONV_DIM), 0.02)
    dt0 = jnp.exp(jax.random.uniform(ks[13], (DEPTH, 2, SSD_HEADS), jnp.float32,
                                     float(np.log(1e-3)), float(np.log(1e-1))))
    ssd_dt_bias = dt0 + jnp.log(-jnp.expm1(-dt0))
    ssd_a_log = jnp.log(jax.random.uniform(ks[14], (DEPTH, 2, SSD_HEADS), jnp.float32, 1.0, 16.0))
    ssd_D = 1.0 + nrm(ks[15], (DEPTH, SSD_HEADS), 0.1)
    ssd_norm_w = 1.0 + nrm(ks[16], (DEPTH, SSD_INNER), 0.02)
    w_ret_o = nrm(ks[17], (DEPTH, RET_V, D_MODEL), RET_V ** -0.5)
    w_ssd_o = nrm(ks[18], (DEPTH, SSD_INNER, D_MODEL), SSD_INNER ** -0.5)
    w_out = nrm(ks[19], (DEPTH, D_MODEL, D_MODEL), D_MODEL ** -0.5)
    return {'x': x, 'c': c, 'ctx': ctx, 'c_ctx': c_ctx, 'w_mod': w_mod, 'b_mod': b_mod,
            'norm_pre_w': norm_pre_w, 'norm_post_w': norm_post_w, 'w_in': w_in,
            'ret_decay': ret_decay, 'ret_gn_w': ret_gn_w, 'ssd_conv_w': ssd_conv_w,
            'ssd_conv_b': ssd_conv_b, 'ssd_dt_bias': ssd_dt_bias, 'ssd_a_log': ssd_a_log,
            'ssd_D': ssd_D, 'ssd_norm_w': ssd_norm_w, 'w_ret_o': w_ret_o, 'w_ssd_o': w_ssd_o,
            'w_out': w_out}


def reference(x, c, ctx, c_ctx, w_mod, b_mod, norm_pre_w, norm_post_w, w_in, ret_decay, ret_gn_w,
              ssd_conv_w, ssd_conv_b, ssd_dt_bias, ssd_a_log, ssd_D, ssd_norm_w, w_ret_o, w_ssd_o, w_out):
    rope = grid_rope_tables(x.shape[1])
    silu_c = jax.nn.silu(c.astype(jnp.float32))
    silu_cc = jax.nn.silu(c_ctx.astype(jnp.float32))
    xl, xc = x, ctx
    for layer in range(DEPTH):
        p = {'norm_pre': norm_pre_w[layer], 'norm_post': norm_post_w[layer], 'w_in': w_in[layer],
             'ret_decay': ret_decay[layer], 'ret_gn_w': ret_gn_w[layer],
             'ssd_conv_w': ssd_conv_w[layer], 'ssd_conv_b': ssd_conv_b[layer],
             'ssd_dt_bias': ssd_dt_bias[layer], 'ssd_a_log': ssd_a_log[layer], 'ssd_D': ssd_D[layer],
             'ssd_norm_w': ssd_norm_w[layer], 'w_ret_o': w_ret_o[layer], 'w_ssd_o': w_ssd_o[layer],
             'w_out': w_out[layer]}
        mod_l = silu_c @ w_mod[layer] + b_mod[layer]
        mod_c = silu_cc @ w_mod[layer] + b_mod[layer]
        xl, xc = hybrid_layer(xl, xc, mod_l, mod_c, p, rope, layer < DEPTH - 1)
    return xl
```

```python
import numpy as np
import ml_dtypes
from contextlib import ExitStack
import concourse.bass as bass
import concourse.mybir as mybir
from concourse.bass_utils import run_bass_kernel_spmd

F32 = mybir.dt.float32
BF16 = mybir.dt.bfloat16
AF = mybir.ActivationFunctionType
ALU = mybir.AluOpType

ENGS = ("tensor", "vector", "scalar", "gpsimd", "sync")
EPS = 1e-6
EVAC_ACT = ()
NTOK = 2304
NCTX = 256
NLAT = 2048
XOFF = 2176
STRIP = 4480

V_NPW, V_BMOD, V_CONVW, V_CONVB, V_GNW, V_SNW, NV = 0, 8, 32, 192, 224, 240, 256
R_NPOST, R_BG, R_D, R_DTB, R_ALOG, R_RDEC, NR = 0, 1024, 2048, 2080, 2144, 2208, 2216
RS = 2048
C_TRI, C_L, C_M01, C_ONES, NC_SSD = 0, 256, 512, 768, 896


class Buf:
    __slots__ = ("name", "w", "r", "excl")

    def __init__(self, name="", excl=False):
        self.name = name
        self.w = None
        self.r = {}
        self.excl = excl


class Prog:
    def __init__(self, nc, st, n_dma=12, queues=("sync", "gpsimd")):
        self.nc = nc
        self.h = {e: getattr(nc, e) for e in ENGS}
        self.cnt = {e: 0 for e in ENGS}
        self.seen = {e: {} for e in ENGS}
        self.sems = {("e", e): st.enter_context(nc.semaphore("e_" + e)) for e in ENGS}
        self.n_dma = n_dma
        self.dma_tot = {}
        self.dma_rr = {q: 0 for q in queues}
        for q in queues:
            for i in range(n_dma):
                self.sems[("d", q, i)] = st.enter_context(nc.semaphore("d_%s_%d" % (q, i)))
        self.nwaits = 0

    def _deps(self, eng, reads, writes):
        deps = {}
        own = ("e", eng)
        for b in reads:
            if b.w is not None and b.w[1] > deps.get(b.w[0], 0):
                deps[b.w[0]] = b.w[1]
            if b.excl:
                for k, v in b.r.items():
                    if k != own and v > deps.get(k, 0):
                        deps[k] = v
        for b in writes:
            if b.w is not None and b.w[1] > deps.get(b.w[0], 0):
                deps[b.w[0]] = b.w[1]
            for k, v in b.r.items():
                if v > deps.get(k, 0):
                    deps[k] = v
        seen = self.seen[eng]
        for k, v in deps.items():
            if eng == "tensor" and k == ("e", "tensor"):
                continue
            if seen.get(k, 0) < v:
                seen[k] = v
                self.h[eng].wait_ge(self.sems[k], v)
                self.nwaits += 1

    @staticmethod
    def _mark(key, val, reads, writes):
        for b in reads:
            if b.r.get(key, 0) < val:
                b.r[key] = val
        for b in writes:
            b.w = (key, val)
            b.r = {}

    def op(self, eng, fn, reads=(), writes=()):
        self._deps(eng, reads, writes)
        ins = fn(self.h[eng])
        self.cnt[eng] += 1
        key = ("e", eng)
        ins.then_inc(self.sems[key], 1)
        self._mark(key, self.cnt[eng], reads, writes)

    def dma(self, q, fn, reads=(), writes=()):
        i = self.dma_rr[q]
        self.dma_rr[q] = (i + 1) % self.n_dma
        key = ("d", q, i)
        prev = self.dma_tot.get(key, 0)
        self._deps(q, reads, writes)
        if prev > 0 and self.seen[q].get(key, 0) < prev:
            self.seen[q][key] = prev
            self.h[q].wait_ge(self.sems[key], prev)
        ins = fn(self.h[q])
        ins.then_inc(self.sems[key], 16)
        self.dma_tot[key] = prev + 16
        self._mark(key, prev + 16, reads, writes)

    def barrier(self):
        tot = {("e", e): self.cnt[e] for e in ENGS}
        tot.update(self.dma_tot)
        for e in ENGS:
            for k, v in tot.items():
                if v > self.seen[e].get(k, 0):
                    self.seen[e][k] = v
                    self.h[e].wait_ge(self.sems[k], v)

    def wait_all(self, eng, bufs):
        self._deps(eng, bufs, ())


class _Stop(Exception):
    pass


class Ring:
    def __init__(self, views, name="ring", excl=False):
        self.views = views
        self.bufs = [Buf("%s%d" % (name, i), excl) for i in range(len(views))]
        self.i = 0

    def next(self):
        i = self.i
        self.i = (i + 1) % len(self.views)
        return self.views[i], self.bufs[i]


def build_program(stop_after=None, dbg_spec=None, cut=None):
    nc = bass.Bass("TRN2", target_bir_lowering=False)

    def din(name, shape, dt=F32):
        return nc.dram_tensor(name, list(shape), dt, kind="ExternalInput").ap()

    x_d = din("x", [NLAT, 1024])
    ctx_d = din("ctx", [NCTX, 1024])
    cct_d = din("cct", [128, 16])
    w_mod_d = din("w_mod", [1024, 3072])
    w_in_d = din("w_in", [1024, 14400])
    w_ret_o_d = din("w_ret_o", [2048, 1024])
    w_ssd_o_d = din("w_ssd_o", [2048, 1024])
    w_out_d = din("w_out", [1024, 1024])
    vecs_d = din("vecs", [128, NV])
    rows_d = din("rows", [1, NR])
    identb_d = din("identb", [128, 128], BF16)
    swapb_d = din("swapb", [128, 128], BF16)
    ropec_d = din("ropec", [128, 192])
    dlt_d = din("dlt", [128, STRIP])
    ssdc_d = din("ssdc", [128, NC_SSD])
    negm_d = din("negm", [128, 1024], BF16)
    self_d = din("self", [8, 1024])
    identf_d = din("identf", [128, 128])
    out_d = nc.dram_tensor("out", [NLAT, 1024], F32, kind="ExternalOutput").ap()
    yrT_d = nc.dram_tensor("yrT_scr", [2048, NLAT], BF16, kind="Internal").ap()
    ysT_d = nc.dram_tensor("ysT_scr", [2048, NLAT], BF16, kind="Internal").ap()
    dbg_d = None
    if dbg_spec is not None:
        dbg_d = nc.dram_tensor("dbg", [128, dbg_spec], F32, kind="ExternalOutput").ap()

    w_in_v = w_in_d.rearrange("(kc p) n -> p kc n", p=128)
    w_mod_v = w_mod_d.rearrange("(kc p) n -> p kc n", p=128)

    with ExitStack() as st:
        P = Prog(nc, st)

        uid = [0]

        def sb(name, shape, dt, stack=st):
            uid[0] += 1
            return stack.enter_context(nc.sbuf_tensor("s%d_%s" % (uid[0], name), list(shape), dt))

        ps_all = st.enter_context(nc.psum_tensor("ps_all", [128, 4096], F32))

        def bank(i, n=1):
            return ps_all[:, i * 512:(i + n) * 512]

        hT = sb("hT", [128, 8, NTOK], BF16)
        vecs = sb("vecs", [128, NV], F32)
        rowss = sb("rowss", [128, NR - RS], F32)
        Gt = sb("Gt", [128, 1024], F32)
        identb = sb("identb", [128, 128], BF16)
        swapb = sb("swapb", [128, 128], BF16)
        ropec = sb("ropec", [128, 192], F32)
        ssdc = sb("ssdc", [128, NC_SSD], F32)
        lg = sb("lg", [128, 8], F32)
        nlg = sb("nlg", [128, 8], F32)
        Aneg = sb("Aneg", [128, 64], F32)
        sc1 = sb("sc1", [128, 8, 2], F32)
        shf = sb("shf", [128, 8, 2], F32)
        dbg_state = {"off": 0}

        def dump(ap, n, reads=(), pstack=None):
            if dbg_d is None:
                return
            with nc.sbuf_tensor("dbgt%d" % dbg_state["off"], [128, n], F32) as t:
                tb = Buf("dbgt")
                P.op("vector", lambda e: e.tensor_copy(out=t[:], in_=ap), reads=list(reads), writes=[tb])
                o = dbg_state["off"]
                P.dma("sync", lambda e: e.dma_start(out=dbg_d[:, o:o + n], in_=t[:]), reads=[tb], writes=[])
                dbg_state["off"] = o + n
                P.barrier()

        def load_w(wst_ring, wbf_ring, src_view, c0, ncols, dst=None, dstbuf=None, scale_col=None, kc0=0, cast_eng="gpsimd", nkc=8, q="sync"):
            sv_, sbuf_ = wst_ring.next()
            if nkc == 8:
                sv = sv_[:, :, 0:ncols]
            else:
                sv = sv_[:].rearrange("p a b -> p (a b)")[:, 0:nkc * ncols].rearrange("p (a b) -> p a b", a=nkc)
            P.dma(q, lambda e: e.dma_start(out=sv, in_=src_view[:, kc0:kc0 + nkc, c0:c0 + ncols]), writes=[sbuf_])
            if dst is None:
                bv, bbuf = wbf_ring.next()
                bva = bv[:, :, 0:ncols]
            else:
                bv, bbuf = dst, dstbuf
                bva = dst[:, :, 0:ncols]
            if scale_col is None and cast_eng == "scalar":
                P.op("scalar", lambda e: e.activation(out=bva, in_=sv, func=AF.Copy), reads=[sbuf_], writes=[bbuf])
            elif scale_col is None:
                P.op("gpsimd", lambda e: e.tensor_copy(out=bva, in_=sv), reads=[sbuf_], writes=[bbuf])
            elif cast_eng == "scalar":
                for kc in range(nkc):
                    P.op("scalar", lambda e, kc=kc: e.activation(
                        out=bva[:, kc, :], in_=sv[:, kc, :], func=AF.Identity,
                        scale=vecs[:, scale_col + kc0 + kc:scale_col + kc0 + kc + 1]), reads=[sbuf_], writes=[bbuf])
            else:
                for kc in range(nkc):
                    P.op("gpsimd", lambda e, kc=kc: e.tensor_scalar(
                        out=bva[:, kc, :], in0=sv[:, kc, :],
                        scalar1=vecs[:, scale_col + kc0 + kc:scale_col + kc0 + kc + 1], scalar2=0.0,
                        op0=ALU.mult, op1=ALU.add), reads=[sbuf_], writes=[bbuf])
            return bv, bbuf

        def ck(label):
            if cut == label:
                P.barrier()
                return True
            return False

        def phase0(ph):
          if True:
              cb = Buf("consts")
              wst_ring = Ring([sb("wst0_%d" % i, [128, 8, 256], F32, ph) for i in range(4)], "wst0")
              for dst, src in ((vecs, vecs_d), (identb, identb_d), (swapb, swapb_d), (ropec, ropec_d), (ssdc, ssdc_d)):
                  P.dma("sync", lambda e, dst=dst, src=src: e.dma_start(out=dst[:], in_=src[:, :]), writes=[Buf()])
              P.dma("sync", lambda e: e.dma_start(out=rowss[:], in_=rows_d[0:1, RS:NR].partition_broadcast(128)), writes=[cb])
              rowsb = sb("rowsb", [128, 2048], F32, ph)
              P.dma("sync", lambda e: e.dma_start(out=rowsb[:], in_=rows_d[0:1, 0:2048].partition_broadcast(128)), writes=[cb])
              cct = sb("cct", [128, 8, 2], F32, ph)
              P.dma("sync", lambda e: e.dma_start(out=cct[:].rearrange("p a b -> p (a b)"), in_=cct_d[:, :]), writes=[cb])
              P.barrier()
              if ck("a"):
                  return
              scc = sb("scc", [128, 8, 2], F32, ph)
              sccrep = sb("sccrep", [128, 8, 128], F32, ph)
              b0 = Buf("p0")
              P.op("scalar", lambda e: e.activation(out=scc[:], in_=cct[:], func=AF.Silu), writes=[b0])
              P.op("vector", lambda e: e.tensor_copy(out=sccrep[:], in_=scc[:, :, 0:1].broadcast_to([128, 8, 128])),
                   reads=[b0], writes=[b0])
              P.op("scalar", lambda e: e.activation(out=nlg[:], in_=rowss[:, R_RDEC - RS:R_RDEC - RS + 8], func=AF.Exp, scale=-1.0),
                   writes=[b0])
              P.op("scalar", lambda e: e.activation(out=nlg[:], in_=nlg[:], func=AF.Ln, bias=1.0), reads=[b0], writes=[b0])
              P.op("vector", lambda e: e.tensor_scalar(out=lg[:], in0=nlg[:], scalar1=-1.0, scalar2=None, op0=ALU.mult),
                   reads=[b0], writes=[b0])
              P.op("scalar", lambda e: e.activation(out=Aneg[:], in_=rowss[:, R_ALOG - RS:R_ALOG - RS + 64], func=AF.Exp),
                   writes=[b0])
              P.op("vector", lambda e: e.tensor_scalar(out=Aneg[:], in0=Aneg[:], scalar1=-1.0, scalar2=None, op0=ALU.mult),
                   reads=[b0], writes=[b0])
              if ck("b"):
                  return
              ps_mod = bank(0)[:, 0:48]
              ps_gate = bank(1, 2)
              pm = Buf("psmod", True)
              pg = Buf("psgate", True)
              for u in range(12):
                  sv, sbuf_ = wst_ring.next()
                  P.dma("sync" if u % 2 == 0 else "gpsimd", lambda e, sv=sv, u=u: e.dma_start(out=sv[:], in_=w_mod_v[:, :, u * 256:(u + 1) * 256]), writes=[sbuf_])

                  def mm_mod(e, sv=sv, u=u):
                      ins = None
                      for j in range(2):
                          col = (u * 2 + j) * 2
                          for kc in range(8):
                              ins = e.matmul(ps_mod[:, col:col + 2], lhsT=sv[:, kc, j * 128:(j + 1) * 128], rhs=scc[:, kc, :],
                                             start=(kc == 0), stop=(kc == 7))
                      if u >= 8:
                          for kc in range(8):
                              ins = e.matmul(ps_gate[:, (u - 8) * 256:(u - 7) * 256], lhsT=sccrep[:, kc, :], rhs=sv[:, kc, :],
                                             start=(kc == 0), stop=(kc == 7))
                      return ins
                  P.op("tensor", mm_mod, reads=[sbuf_, b0], writes=[pm, pg])
              if ck("c"):
                  return
              modv = sb("modv", [128, 24, 2], F32, ph)
              P.op("vector", lambda e: e.tensor_tensor(out=modv[:], in0=ps_mod.rearrange("p (a b) -> p a b", b=2),
                                                       in1=vecs[:, V_BMOD:V_BMOD + 24].unsqueeze(2).broadcast_to([128, 24, 2]),
                                                       op=ALU.add), reads=[pm], writes=[b0])
              P.op("vector", lambda e: e.scalar_tensor_tensor(out=sc1[:], in0=modv[:, 8:16, :], scalar=1.0,
                                                              in1=vecs[:, V_NPW:V_NPW + 8].unsqueeze(2).broadcast_to([128, 8, 2]),
                                                              op0=ALU.add, op1=ALU.mult), reads=[b0], writes=[b0])
              P.op("vector", lambda e: e.tensor_copy(out=shf[:], in_=modv[:, 0:8, :]), reads=[b0], writes=[b0])
              P.op("vector", lambda e: e.tensor_tensor(out=Gt[:], in0=ps_gate, in1=rowsb[:, 1024:2048], op=ALU.add),
                   reads=[pg], writes=[b0])
              P.op("vector", lambda e: e.tensor_tensor(out=Gt[:], in0=Gt[:], in1=rowsb[:, 0:1024], op=ALU.mult),
                   reads=[b0], writes=[b0])
              P.barrier()

              if ck("d"):
                  return
              xt_ring = Ring([sb("xt%d" % i, [128, 1024], F32, ph) for i in range(3)], "xt")
              xb_ring = Ring([sb("xb%d" % i, [128, 1024], BF16, ph) for i in range(2)], "xb")
              junk = sb("junk", [128, 1024], BF16, ph)
              jb = Buf("junk")
              ss = sb("ss", [128, 18], F32, ph)
              rs = sb("rs", [128, 18], F32, ph)
              pst_ring = Ring([bank(3).bitcast(BF16), bank(4).bitcast(BF16)], "pst", True)
              for t in range(18):
                  src = ctx_d[t * 128:(t + 1) * 128, :] if t < 2 else x_d[(t - 2) * 128:(t - 1) * 128, :]
                  which = 1 if t < 2 else 0
                  xt, xtb = xt_ring.next()
                  P.dma("sync", lambda e, xt=xt, src=src: e.dma_start(out=xt[:], in_=src), writes=[xtb])
                  sb_ = Buf("ss")
                  P.op("scalar", lambda e, xt=xt, t=t: e.activation(out=junk[:], in_=xt[:], func=AF.Square, accum_out=ss[:, t:t + 1]),
                       reads=[xtb], writes=[jb, sb_])
                  if t == 0 and ck("e"):
                      return
                  P.op("scalar", lambda e, t=t: e.activation(out=rs[:, t:t + 1], in_=ss[:, t:t + 1], func=AF.Sqrt, scale=1.0 / 1024.0, bias=EPS),
                       reads=[sb_], writes=[sb_])
                  P.op("vector", lambda e, t=t: e.reciprocal(out=rs[:, t:t + 1], in_=rs[:, t:t + 1]), reads=[sb_], writes=[sb_])
                  if t == 0 and ck("f"):
                      return
                  xb, xbb = xb_ring.next()
                  P.op("vector", lambda e, xt=xt, xb=xb, t=t: e.tensor_scalar(out=xb[:], in0=xt[:], scalar1=rs[:, t:t + 1], scalar2=None, op0=ALU.mult),
                       reads=[xtb, sb_], writes=[xbb])
                  if t == 0 and ck("g"):
                      return
                  pst, pstb = pst_ring.next()

                  def tr8(e, xb=xb, pst=pst):
                      ins = None
                      for kc in range(8):
                          ins = e.transpose(pst[:, kc * 128:(kc + 1) * 128], xb[:, kc * 128:(kc + 1) * 128], identb[:])
                      return ins
                  P.op("tensor", tr8, reads=[xbb], writes=[pstb])
                  if t == 0 and ck("h"):
                      return
                  for kc in range(8):
                      o = hT[:, kc, t * 128:(t + 1) * 128]
                      i_ = pst[:, kc * 128:(kc + 1) * 128]
                      s_ = sc1[:, kc, which:which + 1]
                      b_ = shf[:, kc, which:which + 1]
                      if kc % 8 in EVAC_ACT:
                          P.op("scalar", lambda e, o=o, i_=i_, s_=s_, b_=b_: e.activation(out=o, in_=i_, func=AF.Identity, scale=s_, bias=b_),
                               reads=[pstb], writes=[])
                      else:
                          P.op("vector", lambda e, o=o, i_=i_, s_=s_, b_=b_: e.tensor_scalar(out=o, in0=i_, scalar1=s_, scalar2=b_, op0=ALU.mult, op1=ALU.add),
                               reads=[pstb], writes=[])
                  if t == 0 and ck("j"):
                      return
                  if t == 3 and ck("k"):
                      return
              P.barrier()
        with ExitStack() as ph:
            phase0(ph)
        if dbg_spec is not None and stop_after == 0:
            for kc in range(2):
                dump(hT[:, kc, 0:512], 512)
            dump(Gt[:, 0:256], 256)
            dump(lg[:], 8)

        def phase_R(ph):
            wst_ring = Ring([sb("wstR%d" % i, [128, 8, 256], F32, ph) for i in range(2)], "wstR")
            wbf_ring = Ring([sb("wbfR%d" % i, [128, 8, 256], BF16, ph) for i in range(3)], "wbfR")
            KT = sb("KT", [128, 2, NTOK], BF16, ph)
            QT = sb("QT", [128, 2, NLAT], BF16, ph)
            Vt = sb("Vt", [128, 18, 512], BF16, ph)
            Gs = sb("Gs", [128, 16, 512], BF16, ph)
            Th = sb("Th", [128, STRIP], F32, ph)
            dl_ring = Ring([sb("dl%d" % i, [128, 560], F32, ph) for i in range(2)], "dl")
            t1_ring = Ring([sb("t1_%d" % i, [128, 560], F32, ph) for i in range(2)], "t1")
            M_ring = Ring([sb("M%d" % i, [128, 512], BF16, ph) for i in range(4)], "M")
            ys4 = [sb("ys4_%d" % i, [128, 4, 512], F32, ph) for i in range(2)]
            ys4b = [[Buf("ys4") for _ in range(4)] for _ in range(2)]
            xsb_ring = Ring([sb("xsb%d" % i, [128, 512], BF16, ph) for i in range(3)], "xsb")
            rt_ring = Ring([sb("rt%d" % i, [128, 512], F32, ph) for i in range(4)], "rt")
            yn_ring = Ring([sb("yn%d" % i, [128, 512], F32, ph) for i in range(4)], "yn")
            yg_ring = Ring([sb("yg%d" % i, [128, 512], BF16, ph) for i in range(4)], "yg")
            yT_ring = Ring([sb("yT%d" % i, [128, 4, 512], BF16, ph) for i in range(2)], "yT")
            st6 = sb("st6", [128, 4, 6], F32, ph)
            mv = sb("mv", [128, 4, 2], F32, ph)
            rstd = sb("rstd", [128, 4], F32, ph)
            nmr = sb("nmr", [128, 4], F32, ph)
            pbR = [Buf("bankR%d" % i, True) for i in range(8)]
            psy = [bank(i) for i in range(4)]
            psyb = pbR[0:4]
            pss_ring = Ring([bank(4), bank(5), bank(6)], "pss")
            pss_ring.bufs = pbR[4:7]
            psp_ring = Ring([bank(7), bank(0), bank(1), bank(2), bank(3)], "psp")
            psp_ring.bufs = [pbR[7]] + pbR[0:4]
            pst_ring = Ring([bank(7).bitcast(BF16)], "pstR")
            pst_ring.bufs = [pbR[7]]
            cos0 = ropec[:, 0:32].unsqueeze(2).broadcast_to([128, 32, 64])
            cos1 = ropec[:, 32:96].unsqueeze(1).broadcast_to([128, 32, 64])
            sin0 = ropec[:, 96:128].unsqueeze(2).broadcast_to([128, 32, 64])
            sin1 = ropec[:, 128:192].unsqueeze(1).broadcast_to([128, 32, 64])

            def v3(ap):
                return ap.rearrange("p (a b) -> p a b", b=64)

            ktb = [Buf("kt") for _ in range(18)]
            qtb = [Buf("qt") for _ in range(4)]
            vtb = [Buf("vt") for _ in range(18)]
            gsb = [Buf("gs") for _ in range(16)]
            thb = Buf("Th")
            def gen_mask(h, blks=range(8)):
                for blk in blks:
                    dl, dlb = dl_ring.next()
                    c = slice(blk * 560, (blk + 1) * 560)
                    P.dma("sync", lambda e, dl=dl, c=c: e.dma_start(out=dl[:], in_=dlt_d[:, c]), writes=[dlb])
                    t1, t1b = t1_ring.next()
                    P.op("scalar", lambda e, dl=dl, t1=t1: e.activation(out=t1[:], in_=dl[:], func=AF.Identity, scale=lg[:, 4 + h:5 + h]),
                         reads=[dlb], writes=[t1b])
                    P.op("vector", lambda e, dl=dl, t1=t1: e.scalar_tensor_tensor(out=t1[:], in0=dl[:], scalar=nlg[:, h:h + 1], in1=t1[:],
                                                                                 op0=ALU.mult, op1=ALU.max), reads=[dlb], writes=[t1b])
                    P.op("scalar", lambda e, t1=t1, c=c: e.activation(out=Th[:, c], in_=t1[:], func=AF.Exp, scale=-1.0), reads=[t1b], writes=[thb])

            gen_mask(0)
            kq_pref = {}
            for h in range(4):
                def kq_stage1(kind, wv, wb, dc, t0, n, isctx):
                    psp, pspb = psp_ring.next()

                    def mmp(e, wv=wv, dc=dc, t0=t0, n=n, psp=psp):
                        ins = None
                        for kc in range(8):
                            ins = e.matmul(psp[:, 0:n], lhsT=wv[:, kc, dc * 128:(dc + 1) * 128], rhs=hT[:, kc, t0:t0 + n],
                                           start=(kc == 0), stop=(kc == 7))
                        return ins
                    P.op("tensor", mmp, reads=[wb], writes=[pspb])
                    if isctx:
                        P.op("scalar", lambda e, psp=psp, dc=dc: e.activation(out=KT[:, dc, 0:256], in_=psp[:, 0:256], func=AF.Copy),
                             reads=[pspb], writes=[ktb[0], ktb[1]])
                        return None
                    qb = (t0 - 256) // 512
                    sc = 1.0 if kind == "k" else 1.0 / 16.0
                    xsb, xsbb = xsb_ring.next()
                    P.op("scalar", lambda e, psp=psp, xsb=xsb, sc=sc: e.activation(out=xsb[:], in_=psp[:], func=AF.Copy, scale=sc),
                         reads=[pspb], writes=[xsbb])
                    rt, rtb = rt_ring.next()
                    cosb = cos0 if dc == 0 else cos1
                    crow = slice(qb * 8, qb * 8 + 8)
                    P.op("vector", lambda e, psp=psp, rt=rt, cosb=cosb, crow=crow, sc=sc: e.scalar_tensor_tensor(
                        out=v3(rt[:]), in0=v3(psp[:]), scalar=sc, in1=cosb[:, crow, :], op0=ALU.mult, op1=ALU.mult),
                        reads=[pspb], writes=[rtb])
                    return (kind, dc, t0, qb, crow, xsb, xsbb, rt, rtb)

                def kq_stage2(kind, dc, t0, qb, crow, xsb, xsbb, rt, rtb):
                    sinb = sin0 if dc == 0 else sin1
                    psw, pswb = psp_ring.next()
                    P.op("tensor", lambda e, psw=psw, xsb=xsb: e.matmul(psw[:], lhsT=swapb[:], rhs=xsb[:], start=True, stop=True),
                         reads=[xsbb], writes=[pswb])
                    rt2, rt2b = rt_ring.next()
                    P.op("vector", lambda e, psw=psw, rt2=rt2, sinb=sinb, crow=crow: e.tensor_tensor(
                        out=v3(rt2[:]), in0=v3(psw[:]), in1=sinb[:, crow, :], op=ALU.mult), reads=[pswb], writes=[rt2b])
                    if kind == "k":
                        dst = KT[:, dc, t0:t0 + 512]
                        wr = ktb[2 + qb * 4:6 + qb * 4]
                    else:
                        dst = QT[:, dc, qb * 512:(qb + 1) * 512]
                        wr = [qtb[qb]]
                    P.op("gpsimd", lambda e, dst=dst, rt=rt, rt2=rt2: e.tensor_tensor(out=dst, in0=rt[:], in1=rt2[:], op=ALU.add),
                         reads=[rtb, rt2b], writes=wr)

                kq_pending = None
                for kind in ("k", "q"):
                    c0 = (1024 if kind == "k" else 0) + h * 256
                    if (h, kind) in kq_pref:
                        wv, wb = kq_pref.pop((h, kind))
                    else:
                        wv, wb = load_w(wst_ring, wbf_ring, w_in_v, c0, 256, cast_eng="scalar")
                    for dc in range(2):
                        blocks = [(0, 256, True)] if kind == "k" else []
                        blocks += [(256 + qb * 512, 512, False) for qb in range(4)]
                        for (t0, n, isctx) in blocks:
                            st1 = kq_stage1(kind, wv, wb, dc, t0, n, isctx)
                            if kq_pending is not None:
                                kq_stage2(*kq_pending)
                            kq_pending = st1
                if kq_pending is not None:
                    kq_stage2(*kq_pending)
                if h == 0 and ck("r1"):
                    return
                vg_it = 0
                for kind in ("v", "g"):
                    for half in range(2):
                        c0 = (2048 if kind == "v" else 4096) + h * 512 + half * 256
                        wv, wb = load_w(wst_ring, wbf_ring, w_in_v, c0, 256, cast_eng="scalar")
                        for t in range(18):
                            if kind == "g" and t < 2:
                                continue
                            if h > 0 and vg_it % 8 == 0 and vg_it // 8 < 8:
                                gen_mask(h, [vg_it // 8])
                            vg_it += 1
                            psp, pspb = psp_ring.next()

                            def mmp(e, wv=wv, t=t, psp=psp):
                                ins = None
                                for kc in range(8):
                                    ins = e.matmul(psp[:, 0:256], lhsT=hT[:, kc, t * 128:(t + 1) * 128], rhs=wv[:, kc, :],
                                                   start=(kc == 0), stop=(kc == 7))
                                return ins
                            P.op("tensor", mmp, reads=[wb], writes=[pspb])
                            if kind == "v":
                                P.op("scalar", lambda e, psp=psp, t=t, half=half: e.activation(
                                    out=Vt[:, t, half * 256:(half + 1) * 256], in_=psp[:, 0:256], func=AF.Copy),
                                    reads=[pspb], writes=[vtb[t]])
                            else:
                                P.op("scalar", lambda e, psp=psp, t=t, half=half: e.activation(
                                    out=Gs[:, t - 2, half * 256:(half + 1) * 256], in_=psp[:, 0:256], func=AF.Silu),
                                    reads=[pspb], writes=[gsb[t - 2]])
                if h == 0 and ck("r2"):
                    return
                if h < 3:
                    for kind in ("k", "q"):
                        c0n = (1024 if kind == "k" else 0) + (h + 1) * 256
                        kq_pref[(h + 1, kind)] = load_w(wst_ring, wbf_ring, w_in_v, c0n, 256, cast_eng="scalar")
                if h == 0 and ck("r3"):
                    return
                ktiles = [(0, -256), (1, -128)] + [(2 + t, 128 * t) for t in range(16)] + [(0, 2048), (1, 2176)]
                LA = 2
                pending = [None]

                def gn_stages(qb, slot):
                    sbbs = [Buf("stat") for _ in range(4)]
                    srcs = [(ys4[slot][:, qt, :], ys4b[slot][qt]) for qt in range(4)]
                    yns = []
                    ygs = []

                    def s1():
                        for qt in range(4):
                            src, srcb = srcs[qt]
                            P.op("vector", lambda e, qt=qt, src=src: e.bn_stats(out=st6[:, qt, :], in_=src), reads=[srcb], writes=[sbbs[qt]])
                            P.op("vector", lambda e, qt=qt: e.bn_aggr(out=mv[:, qt, :], in_=st6[:, qt, :]), reads=[sbbs[qt]], writes=[sbbs[qt]])

                    def s2():
                        for qt in range(4):
                            P.op("scalar", lambda e, qt=qt: e.activation(out=rstd[:, qt:qt + 1], in_=mv[:, qt, 1:2], func=AF.Sqrt, bias=EPS, scale=1.0),
                                 reads=[sbbs[qt]], writes=[sbbs[qt]])

                    def s3():
                        for qt in range(4):
                            P.op("vector", lambda e, qt=qt: e.reciprocal(out=rstd[:, qt:qt + 1], in_=rstd[:, qt:qt + 1]), reads=[sbbs[qt]], writes=[sbbs[qt]])
                            P.op("vector", lambda e, qt=qt: e.scalar_tensor_tensor(out=nmr[:, qt:qt + 1], in0=mv[:, qt, 0:1], scalar=-1.0,
                                                                                  in1=rstd[:, qt:qt + 1], op0=ALU.mult, op1=ALU.mult),
                                 reads=[sbbs[qt]], writes=[sbbs[qt]])

                    def s4():
                        for qt in range(4):
                            src, srcb = srcs[qt]
                            yn, ynb = yn_ring.next()
                            P.op("scalar", lambda e, qt=qt, yn=yn, src=src: e.activation(out=yn[:], in_=src, func=AF.Identity,
                                                                                        scale=rstd[:, qt:qt + 1], bias=nmr[:, qt:qt + 1]),
                                 reads=[srcb, sbbs[qt]], writes=[ynb])
                            yns.append((yn, ynb))

                    def s5():
                        for qt in range(4):
                            gi = qb * 4 + qt
                            yn, ynb = yns[qt]
                            yg, ygb = yg_ring.next()
                            P.op("gpsimd", lambda e, yn=yn, yg=yg, gi=gi: e.tensor_tensor(out=yg[:], in0=yn[:], in1=Gs[:, gi, :], op=ALU.mult),
                                 reads=[ynb, gsb[gi]], writes=[ygb])
                            ygs.append((yg, ygb))
                    return [s1, s2, s3, s4, s5], ygs

                def gn_b(qb, ygs):
                    qs = 512 * qb
                    yT, yTb = yT_ring.next()
                    for qt in range(4):
                        yg, ygb = ygs[qt]
                        pst, pstb = pst_ring.next()

                        def tr4(e, yg=yg, pst=pst):
                            ins = None
                            for fc in range(4):
                                ins = e.transpose(pst[:, fc * 128:(fc + 1) * 128], yg[:, fc * 128:(fc + 1) * 128], identb[:])
                            return ins
                        P.op("tensor", tr4, reads=[ygb], writes=[pstb])
                        P.op("vector", lambda e, pst=pst, yT=yT, qt=qt: e.tensor_copy(
                            out=yT[:, :, qt * 128:(qt + 1) * 128], in_=pst[:, 0:512].rearrange("p (a b) -> p a b", b=128)),
                            reads=[pstb], writes=[yTb])
                    dstd = yrT_d[h * 512:(h + 1) * 512, qs:qs + 512].rearrange("(fc p) t -> p fc t", p=128)
                    P.dma("gpsimd", lambda e, yT=yT, dstd=dstd: e.dma_start(out=dstd, in_=yT[:]), reads=[yTb], writes=[yrb])

                for qb in range(4):
                    qs = 512 * qb
                    slot = qb % 2
                    nk = len(ktiles)
                    Ms = {}
                    for stp in range(nk + LA):
                        if stp < nk:
                            kt, kp = ktiles[stp]
                            pss, pssb = pss_ring.next()

                            def mms(e, pss=pss, kt=kt, qb=qb):
                                ins = None
                                for dc in range(2):
                                    ins = e.matmul(pss[:], lhsT=KT[:, dc, kt * 128:(kt + 1) * 128], rhs=QT[:, dc, qb * 512:(qb + 1) * 512],
                                                   start=(dc == 0), stop=(dc == 1))
                                return ins
                            P.op("tensor", mms, reads=[ktb[kt], qtb[qb]], writes=[pssb])
                            x0 = qs - kp + XOFF
                            Mv, Mb = M_ring.next()
                            P.op("vector", lambda e, pss=pss, Mv=Mv, x0=x0: e.tensor_tensor(out=Mv[:], in0=pss[:], in1=Th[:, x0:x0 + 512], op=ALU.mult),
                                 reads=[pssb, thb], writes=[Mb])
                            Ms[stp] = (Mv, Mb)
                        ki = stp - LA
                        if ki >= 0:
                            kt, kp = ktiles[ki]
                            Mv, Mb = Ms.pop(ki)

                            def mmy(e, Mv=Mv, kt=kt, ki=ki, nk=nk):
                                ins = None
                                for qt in range(4):
                                    ins = e.matmul(psy[qt][:], lhsT=Mv[:, qt * 128:(qt + 1) * 128], rhs=Vt[:, kt, :],
                                                   start=(ki == 0), stop=(ki == nk - 1))
                                return ins
                            P.op("tensor", mmy, reads=[Mb, vtb[kt]], writes=psyb)
                        if pending[0] is not None:
                            if stp == 1:
                                pending[0] = (pending[0][0],) + gn_stages(*pending[0])
                            if stp in (1, 3, 5, 7, 9):
                                pending[0][1][(stp - 1) // 2]()
                            if stp == 14:
                                gn_b(pending[0][0], pending[0][2])
                                pending[0] = None
                    for qt in range(4):
                        P.op("scalar", lambda e, qt=qt, slot=slot: e.activation(out=ys4[slot][:, qt, :], in_=psy[qt][:], func=AF.Copy),
                             reads=[psyb[qt]], writes=[ys4b[slot][qt]])
                    pending[0] = (qb, slot)
                stages, ygs_last = gn_stages(*pending[0])
                for st_ in stages:
                    st_()
                gn_b(pending[0][0], ygs_last)
                pending[0] = None

        def phase_S(ph):
            pb = [Buf("bank%d" % i, True) for i in range(8)]
            dtv = sb("dtv", [128, 18, 64], F32, ph)
            wdt = sb("wdt", [128, 8, 64], BF16, ph)
            a8 = sb("a8", [128, 2, 18, 4], F32, ph)
            dt8 = sb("dt8", [128, 2, 18, 4], F32, ph)
            eall = sb("eall", [128, 6, 72], F32, ph)
            wall = sb("wall", [128, 2, 72], F32, ph)
            BCT = sb("BCT", [128, 2, 2308], BF16, ph)
            xsT_ring = Ring([sb("xsT%d" % i, [128, 2308], BF16, ph) for i in range(2)], "xsT")
            raw2 = sb("raw2", [128, 2, 2312], F32, ph)
            raw_ring = Ring([raw2[:, 0, :], raw2[:, 1, :]], "raw")
            xw_all = raw2[:].rearrange("p a b -> p (a b)").bitcast(BF16)[:, 0:2 * 18 * 256].rearrange("p (d t q) -> p d t q", d=2, t=18)
            acc_ring = Ring([sb("acc%d" % i, [128, 2308], F32, ph) for i in range(1)], "acc")
            xs_tm = sb("xs_tm", [128, 18, 256], BF16, ph)
            B_tm = sb("B_tm", [128, 18, 128], BF16, ph)
            sz = sb("sz", [128, 16, 256], BF16, ph)
            S_store = sb("S_store", [128, 2, 16, 256], BF16, ph)
            S32 = sb("S32", [128, 2, 256], F32, ph)
            Dec_ring = Ring([sb("Dec%d" % i, [128, 2, 4, 128], F32, ph) for i in range(2)], "Dec")
            nacum = sb("nacum", [128, 18, 8], F32, ph)
            acum = sb("acum", [128, 18, 8], F32, ph)
            dD = sb("dD", [128, 2, 4, 128], BF16, ph)
            dDt = sb("dDt", [128, 2, 128], F32, ph)
            negm = sb("negm", [128, 2, 4, 128], BF16, ph)
            identf = sb("identf", [128, 128], F32, ph)
            cbS = Buf("constS")
            P.dma("sync", lambda e: e.dma_start(out=negm[:].rearrange("p d r i -> p (d r i)"), in_=negm_d[:, :]), writes=[cbS])
            P.dma("sync", lambda e: e.dma_start(out=identf[:], in_=identf_d[:, :]), writes=[cbS])
            M_ring = Ring([sb("Ms%d" % i, [128, 2, 4, 128], BF16, ph) for i in range(3)], "Ms")
            v_ring = Ring([sb("v%d" % i, [128, 2, 4, 64], BF16, ph) for i in range(3)], "v")
            gi_ring = Ring([sb("gi%d" % i, [128, 2, 256], F32, ph) for i in range(2)], "gi")
            t1_ring = Ring([sb("e1_%d" % i, [128, 256], F32, ph) for i in range(2)], "e1")
            t2_ring = Ring([sb("e2_%d" % i, [128, 256], F32, ph) for i in range(2)], "e2")
            yo_ring = Ring([sb("yo%d" % i, [128, 256], BF16, ph) for i in range(2)], "yo")
            junk = sb("junkS", [128, 256], BF16, ph)
            jb = Buf("junkS")
            ss_ring = Ring([sb("ss1_%d" % i, [128, 2], F32, ph) for i in range(2)], "ss1")
            ysT_ring = Ring([sb("ysT%d" % i, [128, 2, NLAT], BF16, ph) for i in range(1)], "ysT")
            tmp64 = sb("tmp64", [128, 64], F32, ph)
            epsc = sb("epsc", [128, 1], F32, ph)
            P.op("gpsimd", lambda e: e.memset(epsc[:], EPS), writes=[Buf()])
            wst_ring = Ring([sb("wstS%d" % i, [128, 8, 256], F32, ph) for i in range(2)], "wstS")
            wbf_ring = Ring([sb("wbfS%d" % i, [128, 8, 256], BF16, ph) for i in range(2)], "wbfS")
            tri = ssdc[:, C_TRI:C_TRI + 256].rearrange("p (d i) -> p d i", d=2)
            Lm = ssdc[:, C_L:C_L + 256].rearrange("p (d i) -> p d i", d=2)
            m01 = ssdc[:, C_M01:C_M01 + 256].rearrange("p (d i) -> p d i", d=2)
            ones = ssdc[:, C_ONES:C_ONES + 128]
            proj_i = [0]

            def next_proj():
                i = proj_i[0]
                proj_i[0] = (i + 1) % 4
                return bank(i), pb[i]

            wdtb = Buf("wdt")
            load_w(wst_ring, wbf_ring, w_in_v, 12288, 64, dst=wdt, dstbuf=wdtb)
            dtb = Buf("dtv")
            for t in range(18):
                psp, pspb = next_proj()

                def mmd(e, t=t, psp=psp):
                    ins = None
                    for kc in range(8):
                        ins = e.matmul(psp[:, 0:64], lhsT=hT[:, kc, t * 128:(t + 1) * 128], rhs=wdt[:, kc, :], start=(kc == 0), stop=(kc == 7))
                    return ins
                P.op("tensor", mmd, reads=[wdtb], writes=[pspb])
                P.op("vector", lambda e, psp=psp: e.tensor_tensor(out=tmp64[:], in0=psp[:, 0:64], in1=rowss[:, R_DTB - RS:R_DTB - RS + 64], op=ALU.add),
                     reads=[pspb], writes=[dtb])
                P.op("scalar", lambda e: e.activation(out=tmp64[:], in_=tmp64[:], func=AF.Exp), reads=[dtb], writes=[dtb])
                P.op("scalar", lambda e, t=t: e.activation(out=dtv[:, t, :], in_=tmp64[:], func=AF.Ln, bias=1.0), reads=[dtb], writes=[dtb])

            gb = Buf("grp")
            szb = [Buf("sz") for _ in range(16)]
            xsb_ = Buf("xs_tm")
            btb = Buf("B_tm")
            bctb = Buf("BCT")
            eb = Buf("eall")
            xwb2 = [Buf("xw_all_f"), Buf("xw_all_b")]
            acb = Buf("acum")
            ddb = Buf("dD")
            s32b = [Buf("S32f"), Buf("S32b")]
            stb = [[Buf("Sst") for _ in range(16)] for _ in range(2)]
            for g in range(8):
                P.op("vector", lambda e, g=g: e.tensor_copy(
                    out=dt8[:], in_=dtv[:].rearrange("p t (d h) -> p d t h", d=2)[:, :, :, g * 4:(g + 1) * 4]), reads=[dtb], writes=[gb])
                P.op("vector", lambda e, g=g: e.tensor_tensor(
                    out=a8[:], in0=dt8[:],
                    in1=Aneg[:].rearrange("p (d h) -> p d h", d=2)[:, :, g * 4:(g + 1) * 4].unsqueeze(2).broadcast_to([128, 2, 18, 4]),
                    op=ALU.mult), reads=[gb], writes=[gb])
                for rv, rb in zip(raw_ring.views, raw_ring.bufs):
                    for (p0, p1) in ((0, 2), (258, 262), (2310, 2312)):
                        P.op("gpsimd", lambda e, rv=rv, p0=p0, p1=p1: e.memset(rv[:, p0:p1], 0.0), writes=[rb])
                for r in range(4):
                    dcol = rowss[:, R_D - RS + g * 4 + r:R_D - RS + g * 4 + r + 1]
                    P.op("gpsimd", lambda e, dcol=dcol: e.tensor_scalar(out=dDt[:, 0, :], in0=identf[:], scalar1=dcol, scalar2=None, op0=ALU.mult),
                         reads=[cbS], writes=[ddb])
                    P.op("gpsimd", lambda e, r=r: e.tensor_copy(out=dD[:, 0, r, :], in_=dDt[:, 0, :]), reads=[ddb], writes=[ddb])
                    P.op("gpsimd", lambda e, r=r: e.tensor_tensor(out=dDt[:, 1, :], in0=dDt[:, 0, :], in1=dD[:, 0, r, :], op=ALU.subtract), reads=[ddb], writes=[ddb])
                    P.op("gpsimd", lambda e, r=r: e.tensor_copy(out=dD[:, 1, r, :], in_=dDt[:, 1, :]), reads=[ddb], writes=[ddb])
                chunks = [("x", 8192 + g * 256, 0, g * 2), ("x", 8192 + g * 256 + 128, 1, g * 2 + 1),
                          ("B", 8192 + 2048 + g * 128, 0, 16 + g), ("C", 8192 + 3072 + g * 128, 0, 24 + g)]

                def proj(k):
                    (kind, c0, j, cch) = chunks[k]
                    wv, wb = load_w(wst_ring, wbf_ring, w_in_v, c0, 128, cast_eng="scalar")
                    raw, rawb = raw_ring.next()
                    blocks = ([] if kind == "C" else [(0, 256, 2)]) + [(256 + q * 512, 512, 262 + q * 512) for q in range(4)]
                    for (t0, n, r0) in blocks:
                        psp, pspb = next_proj()

                        def mmx(e, wv=wv, t0=t0, n=n, psp=psp):
                            ins = None
                            for kc in range(8):
                                ins = e.matmul(psp[:, 0:n], lhsT=wv[:, kc, 0:128], rhs=hT[:, kc, t0:t0 + n], start=(kc == 0), stop=(kc == 7))
                            return ins
                        P.op("tensor", mmx, reads=[wb], writes=[pspb])
                        P.op("scalar", lambda e, psp=psp, raw=raw, n=n, r0=r0: e.activation(out=raw[:, r0:r0 + n], in_=psp[:, 0:n], func=AF.Copy),
                             reads=[pspb], writes=[rawb])
                    return raw, rawb

                def conv_tr(k, raw, rawb):
                    (kind, c0, j, cch) = chunks[k]
                    acc, accb = acc_ring.next()
                    for kk in range(5):
                        wk = vecs[:, V_CONVW + kk * 32 + cch:V_CONVW + kk * 32 + cch + 1]
                        if kk == 0:
                            P.op("vector", lambda e, acc=acc, raw=raw, wk=wk: e.tensor_scalar(out=acc[:], in0=raw[:, 0:2308], scalar1=wk, scalar2=None, op0=ALU.mult),
                                 reads=[rawb], writes=[accb])
                        else:
                            P.op("vector", lambda e, acc=acc, raw=raw, wk=wk, kk=kk: e.scalar_tensor_tensor(
                                out=acc[:], in0=raw[:, kk:kk + 2308], scalar=wk, in1=acc[:], op0=ALU.mult, op1=ALU.add),
                                reads=[rawb], writes=[accb])
                    bcol = vecs[:, V_CONVB + cch:V_CONVB + cch + 1]
                    if kind == "x":
                        dst, dstb = xsT_ring.next()
                        dsta = dst[:]
                    else:
                        dsta, dstb = BCT[:, 0 if kind == "B" else 1, :], bctb
                    P.op("scalar", lambda e, acc=acc, dsta=dsta, bcol=bcol: e.activation(out=dsta, in_=acc[:], func=AF.Silu, bias=bcol),
                         reads=[accb], writes=[dstb])
                    if kind == "C":
                        return
                    for t0 in range(0, 18, 6):
                        pst = bank(7).bitcast(BF16)

                        def tr6(e, dsta=dsta, t0=t0, pst=pst):
                            ins = None
                            for i in range(6):
                                t = t0 + i
                                u0 = t * 128 if t < 2 else 260 + (t - 2) * 128
                                ins = e.transpose(pst[:, i * 128:(i + 1) * 128], dsta[:, u0:u0 + 128], identb[:])
                            return ins
                        P.op("tensor", tr6, reads=[dstb], writes=[pb[7]])
                        if kind == "x":
                            o = xs_tm[:, t0:t0 + 6, j * 128:(j + 1) * 128]
                            ob_ = xsb_
                        else:
                            o = B_tm[:, t0:t0 + 6, :]
                            ob_ = btb
                        P.op("scalar", lambda e, o=o, pst=pst: e.activation(out=o, in_=pst[:, 0:768].rearrange("p (a b) -> p a b", b=128), func=AF.Copy),
                             reads=[pb[7]], writes=[ob_])

                def z_proj():
                    wv, wb = load_w(wst_ring, wbf_ring, w_in_v, 6144 + g * 256, 256, cast_eng="scalar")
                    for c in range(16):
                        psp, pspb = next_proj()

                        def mmz(e, wv=wv, c=c, psp=psp):
                            ins = None
                            for kc in range(8):
                                ins = e.matmul(psp[:, 0:256], lhsT=hT[:, kc, (2 + c) * 128:(3 + c) * 128], rhs=wv[:, kc, 0:256], start=(kc == 0), stop=(kc == 7))
                            return ins
                        P.op("tensor", mmz, reads=[wb], writes=[pspb])
                        P.op("scalar", lambda e, psp=psp, c=c: e.activation(out=sz[:, c, :], in_=psp[:, 0:256], func=AF.Silu),
                             reads=[pspb], writes=[szb[c]])

                nxt = proj(0)
                for k in range(4):
                    cur = nxt
                    nxt = proj(k + 1) if k < 3 else None
                    if k == 0:
                        z_proj()
                    conv_tr(k, *cur)
                if g == 0 and ck("s1"):
                    return


                def mmb(e):
                    ins = None
                    for d in range(2):
                        rhs = a8[:, d, :, :].rearrange("p t r -> p (t r)")
                        e.matmul(bank(0)[:, d * 72:(d + 1) * 72], lhsT=Lm[:, d, :], rhs=rhs, start=True, stop=True)
                        e.matmul(bank(0)[:, 144 + d * 72:144 + (d + 1) * 72], lhsT=ones, rhs=rhs, start=True, stop=True)
                        ins = e.matmul(bank(0)[:, 288 + d * 72:288 + (d + 1) * 72], lhsT=tri[:, d, :], rhs=rhs, start=True, stop=True)
                    return ins
                P.op("tensor", mmb, reads=[gb], writes=[pb[0]])
                P.op("scalar", lambda e: e.activation(out=eall[:].rearrange("p a b -> p (a b)"), in_=bank(0)[:, 0:432], func=AF.Exp), reads=[pb[0]], writes=[eb])
                P.op("vector", lambda e: e.tensor_tensor(out=wall[:], in0=eall[:, 0:2, :], in1=dt8[:].rearrange("p d t r -> p d (t r)"), op=ALU.mult),
                     reads=[eb, gb], writes=[eb])
                P.op("vector", lambda e: e.tensor_copy(out=acum[:].rearrange("p t (d r) -> p d t r", d=2),
                                                       in_=bank(0)[:, 288:432].rearrange("p (d t r) -> p d t r", d=2, t=18)), reads=[pb[0]], writes=[acb])
                P.op("vector", lambda e: e.tensor_scalar(out=nacum[:], in0=acum[:], scalar1=-1.0, scalar2=None, op0=ALU.mult), reads=[acb], writes=[acb])
                P._deps("gpsimd", (), raw_ring.bufs)
                for d in (0, 1):
                    P.op("vector" if d == 1 else "gpsimd", lambda e, d=d: e.tensor_tensor(
                        out=xw_all[:, d, :, :].rearrange("p t (r q) -> p t r q", q=64), in0=xs_tm[:].rearrange("p t (r q) -> p t r q", q=64),
                        in1=wall[:, d, :].rearrange("p (t r) -> p t r", r=4).unsqueeze(3).broadcast_to([128, 18, 4, 64]), op=ALU.mult),
                        reads=[eb, xsb_], writes=[xwb2[d]] + (raw_ring.bufs if d == 1 else []))
                etot = eall[:, 2:4, :].rearrange("p d (t r) -> p d t r", r=4)
                ecum = eall[:, 4:6, :].rearrange("p d (t r) -> p d t r", r=4)

                P.op("gpsimd", lambda e: e.memset(S32[:], 0.0), writes=s32b)
                fwd_tiles = list(range(0, 17))
                bwd_tiles = [1, 0] + list(range(17, 2, -1))
                step = [0]

                def chain_step(d, t, slot):
                    bi = 4 + (step[0] % 3)
                    step[0] += 1
                    P.op("tensor", lambda e, d=d, t=t, bi=bi: e.matmul(bank(bi)[:, 0:256], lhsT=B_tm[:, t, :], rhs=xw_all[:, d, t, :], start=True, stop=True),
                         reads=[xwb2[d], btb], writes=[pb[bi]])
                    P.op("vector", lambda e, d=d, t=t: e.tensor_tensor(
                        out=S32[:, d, :].rearrange("p (r q) -> p r q", q=64), in0=S32[:, d, :].rearrange("p (r q) -> p r q", q=64),
                        in1=etot[:, d, t, :].unsqueeze(2).broadcast_to([128, 4, 64]), op=ALU.mult), reads=[eb], writes=[s32b[d]])
                    P.op("vector", lambda e, d=d, bi=bi: e.tensor_tensor(out=S32[:, d, :], in0=S32[:, d, :], in1=bank(bi)[:, 0:256], op=ALU.add),
                         reads=[pb[bi]], writes=[s32b[d]])
                    if slot is not None:
                        P.op("scalar", lambda e, d=d, slot=slot: e.activation(out=S_store[:, d, slot, :], in_=S32[:, d, :], func=AF.Copy),
                             reads=[s32b[d]], writes=[stb[d][slot]])
                for s in range(17):
                    tf = fwd_tiles[s]
                    chain_step(0, tf, tf - 1 if tf >= 1 else None)
                    tb_ = bwd_tiles[s]
                    if tb_ == 1:
                        slot = None
                    elif tb_ == 0:
                        slot = 15
                    else:
                        slot = tb_ - 3
                    chain_step(1, tb_, slot)
                if g == 0 and ck("s2"):
                    return

                ysT, ysTb = ysT_ring.next()

                def sbk_of(c):
                    return (0, 1) if c % 2 == 0 else (5, 6)

                def sc_of(c):
                    return bank(2)[:, (c % 2) * 128:(c % 2) * 128 + 128]

                def prepA(c):
                    t = 2 + c
                    u0 = 260 + c * 128
                    sbk = sbk_of(c)

                    def mmseg(e, t=t, sbk=sbk):
                        ins = None
                        for d in range(2):
                            e.matmul(bank(sbk[d])[:], lhsT=identb[:], rhs=negm[:, d, :, :].rearrange("p r i -> p (r i)"), start=True, stop=False,
                                     skip_group_check=True)
                            for r in range(4):
                                ins = e.matmul(bank(sbk[d])[:, r * 128:(r + 1) * 128], lhsT=a8[:, d, t, r:r + 1].broadcast_to([128, 128]),
                                               rhs=tri[:, d, :], start=False, stop=True, skip_group_check=True)
                        return ins
                    P.op("tensor", mmseg, reads=[gb, cbS], writes=[pb[sbk[0]], pb[sbk[1]]])
                    P.op("tensor", lambda e, u0=u0, c=c: e.matmul(sc_of(c), lhsT=BCT[:, 0, u0:u0 + 128], rhs=BCT[:, 1, u0:u0 + 128], start=True, stop=True),
                         reads=[bctb], writes=[pb[2]])

                def prepB_act(c):
                    t = 2 + c
                    sbk = sbk_of(c)
                    Dec, Decb = Dec_ring.next()
                    for d in range(2):
                        for r in range(4):
                            q = d * 4 + r
                            P.op("scalar", lambda e, Dec=Dec, d=d, r=r, q=q, t=t, sbk=sbk: e.activation(
                                out=Dec[:, d, r, :], in_=bank(sbk[d])[:, r * 128:(r + 1) * 128], func=AF.Exp, bias=nacum[:, t, q:q + 1], scale=1.0),
                                reads=[pb[sbk[d]], acb], writes=[Decb])
                    return Dec, Decb

                def prepB_mv(c, Dec, Decb):
                    t = 2 + c
                    Mv, Mb = M_ring.next()
                    P.op("vector", lambda e, Mv=Mv, Dec=Dec, c=c: e.tensor_tensor(
                        out=Mv[:].rearrange("p d r i -> p (d r) i"), in0=sc_of(c).unsqueeze(1).broadcast_to([128, 8, 128]),
                        in1=Dec[:].rearrange("p d r i -> p (d r) i"), op=ALU.mult), reads=[pb[2], Decb], writes=[Mb])
                    vv, vb = v_ring.next()
                    P.op("gpsimd", lambda e, vv=vv, t=t: e.tensor_tensor(
                        out=vv[:], in0=xs_tm[:, t, :].rearrange("p (r q) -> p r q", q=64).unsqueeze(1).broadcast_to([128, 2, 4, 64]),
                        in1=dt8[:, :, t, :].unsqueeze(3).broadcast_to([128, 2, 4, 64]), op=ALU.mult), reads=[gb, xsb_], writes=[vb])
                    return (Mv, Mb, vv, vb)

                def fin_1(c, Mv, Mb, vv, vb):
                    t = 2 + c
                    u0 = 260 + c * 128

                    def mmy(e, Mv=Mv, vv=vv):
                        ins = None
                        for r in range(4):
                            o = bank(3)[:, r * 64:(r + 1) * 64]
                            e.matmul(o, lhsT=Mv[:, 0, r, :], rhs=vv[:, 0, r, :], start=True, stop=False)
                            e.matmul(o, lhsT=Mv[:, 1, r, :], rhs=vv[:, 1, r, :], start=False, stop=False)
                            e.matmul(o, lhsT=dD[:, 0, r, :], rhs=xs_tm[:, t, r * 64:(r + 1) * 64], start=False, stop=False)
                            ins = e.matmul(o, lhsT=dD[:, 1, r, :], rhs=xs_tm[:, t, r * 64:(r + 1) * 64], start=False, stop=True)
                        return ins
                    P.op("tensor", mmy, reads=[Mb, vb, ddb, xsb_], writes=[pb[3]])
                    ib = 4

                    def mmi(e, c=c, u0=u0, ib=ib):
                        ins = None
                        for d in range(2):
                            ins = e.matmul(bank(ib)[:, d * 256:(d + 1) * 256], lhsT=BCT[:, 1, u0:u0 + 128], rhs=S_store[:, d, c, :], start=True, stop=True)
                        return ins
                    P.op("tensor", mmi, reads=[bctb, stb[0][c], stb[1][c]], writes=[pb[ib]])
                    gi, gib = gi_ring.next()
                    P.op("vector", lambda e, gi=gi, ib=ib, t=t: e.tensor_tensor(
                        out=gi[:].rearrange("p d (r q) -> p d r q", q=64), in0=bank(ib)[:].rearrange("p (d r q) -> p d r q", d=2, q=64),
                        in1=ecum[:, :, t, :].unsqueeze(3).broadcast_to([128, 2, 4, 64]), op=ALU.mult), reads=[pb[ib], eb], writes=[gib])
                    t1, t1b = t1_ring.next()
                    P.op("vector", lambda e, t1=t1, gi=gi: e.tensor_tensor(out=t1[:], in0=gi[:, 0, :], in1=gi[:, 1, :], op=ALU.add), reads=[gib], writes=[t1b])
                    P.op("vector", lambda e, t1=t1: e.tensor_tensor(out=t1[:], in0=bank(3)[:, 0:256], in1=t1[:], op=ALU.add), reads=[pb[3]], writes=[t1b])
                    t2, t2b = t2_ring.next()
                    P.op("vector", lambda e, t1=t1, t2=t2, c=c: e.tensor_tensor(out=t2[:], in0=t1[:], in1=sz[:, c, :], op=ALU.mult),
                         reads=[t1b, szb[c]], writes=[t2b])
                    ss1, sb1 = ss_ring.next()
                    P.op("scalar", lambda e, t2=t2, ss1=ss1: e.activation(out=junk[:], in_=t2[:], func=AF.Square, accum_out=ss1[:, 0:1]), reads=[t2b], writes=[jb, sb1])
                    P.op("scalar", lambda e, ss1=ss1: e.activation(out=ss1[:, 1:2], in_=ss1[:, 0:1], func=AF.Ln, scale=1.0 / 256.0, bias=epsc[:, 0:1]), reads=[sb1], writes=[sb1])
                    P.op("scalar", lambda e, ss1=ss1: e.activation(out=ss1[:, 1:2], in_=ss1[:, 1:2], func=AF.Exp, scale=-0.5), reads=[sb1], writes=[sb1])
                    return (t2, t2b, ss1, sb1)

                def fin_2(c, t2, t2b, ss1, sb1):
                    yo, yob = yo_ring.next()
                    P.op("vector", lambda e, yo=yo, t2=t2, ss1=ss1: e.tensor_scalar(out=yo[:], in0=t2[:], scalar1=ss1[:, 1:2], scalar2=None, op0=ALU.mult),
                         reads=[t2b, sb1], writes=[yob])
                    return (yo, yob)

                def fin_b(c, yo, yob):
                    pst = bank(7).bitcast(BF16)

                    def tr2(e, yo=yo, pst=pst):
                        e.transpose(pst[:, 0:128], yo[:, 0:128], identb[:])
                        return e.transpose(pst[:, 128:256], yo[:, 128:256], identb[:])
                    P.op("tensor", tr2, reads=[yob], writes=[pb[7]])
                    P.op("vector", lambda e, ysT=ysT, c=c, pst=pst: e.tensor_copy(
                        out=ysT[:, :, c * 128:(c + 1) * 128], in_=pst[:, 0:256].rearrange("p (a b) -> p a b", b=128)), reads=[pb[7]], writes=[ysTb])

                hnds = {}
                prepA(0)
                prepA(1)
                hnds[0] = prepB_mv(0, *prepB_act(0))
                prepA(2)
                hnds[1] = prepB_mv(1, *prepB_act(1))
                prev = None
                for c in range(16):
                    if c + 3 < 16:
                        prepA(c + 3)
                    dec = prepB_act(c + 2) if c + 2 < 16 else None
                    f1 = fin_1(c, *hnds.pop(c))
                    if dec is not None:
                        hnds[c + 2] = prepB_mv(c + 2, *dec)
                    cur = fin_2(c, *f1)
                    if prev is not None:
                        fin_b(c - 1, *prev)
                    prev = cur
                fin_b(15, *prev)
                dstd = ysT_d[g * 256:(g + 1) * 256, :].rearrange("(fc p) t -> p fc t", p=128)
                P.dma("gpsimd", lambda e, ysT=ysT, dstd=dstd: e.dma_start(out=dstd, in_=ysT[:]), reads=[ysTb], writes=[ysb])
                if g == 0 and ck("s4"):
                    return

        def phase_F(ph):
            pb = [Buf("bank%d" % i, True) for i in range(8)]
            mT = sb("mT", [128, 8, NLAT], BF16, ph)
            wA = sb("wA", [128, 16, 1024], BF16, ph)
            wG = sb("wG", [128, 8, 1024], BF16, ph)
            yb_ring = Ring([sb("yblk%d" % i, [128, 16, 512], BF16, ph) for i in range(2)], "yblk")
            sg_ring = Ring([sb("sg%d" % i, [128, 512], F32, ph) for i in range(2)], "sg")
            tm_ring = Ring([sb("tm%d" % i, [128, 512], F32, ph) for i in range(2)], "tm")
            xt_ring = Ring([sb("xF%d" % i, [128, 1024], F32, ph) for i in range(3)], "xF")
            ot_ring = Ring([sb("oF%d" % i, [128, 1024], F32, ph) for i in range(2)], "oF")
            junk = sb("junkF", [128, 512], BF16, ph)
            jb = Buf("junkF")
            ssf = sb("ssf", [128, 4], F32, ph)
            wst_ring = Ring([sb("wstF%d" % i, [128, 8, 256], F32, ph) for i in range(2)], "wstF")
            mTb = [Buf("mT%d" % i) for i in range(4)]
            wAb = [Buf("wA%d" % i) for i in range(2)]
            wGb = [Buf("wG%d" % i) for i in range(2)]
            for br in range(2):
                src_o = (w_ret_o_d if br == 0 else w_ssd_o_d).rearrange("(kc p) n -> p kc n", p=128)
                scol = V_GNW if br == 0 else V_SNW
                for cu in range(2):
                    for kq in range(4):
                        alt = (br == 0 and cu == 0)
                        load_w(wst_ring, None, src_o, cu * 512, 512, dst=wA[:, kq * 4:(kq + 1) * 4, cu * 512:(cu + 1) * 512], dstbuf=wAb[cu],
                               scale_col=scol, kc0=kq * 4, nkc=4, cast_eng="scalar" if alt else "gpsimd")
                    for kq in range(2):
                        load_w(wst_ring, None, w_in_v, 12352 + br * 1024 + cu * 512, 512, dst=wG[:, kq * 4:(kq + 1) * 4, cu * 512:(cu + 1) * 512],
                               dstbuf=wGb[cu], kc0=kq * 4, nkc=4, cast_eng="scalar", q="gpsimd")
                scr = yrT_d if br == 0 else ysT_d
                scrb = yrb if br == 0 else ysb
                for tb in range(4):
                    yblk, yblkb = yb_ring.next()
                    P.dma("sync", lambda e, yblk=yblk, tb=tb, scr=scr: e.dma_start(
                        out=yblk[:], in_=scr[:, tb * 512:(tb + 1) * 512].rearrange("(kc p) t -> p kc t", p=128)), reads=[scrb], writes=[yblkb])
                    for fo in range(8):
                        pa, pab = bank(fo % 2), pb[fo % 2]
                        pg, pgb = bank(2 + fo % 2), pb[2 + fo % 2]

                        def mma(e, yblk=yblk, fo=fo, pa=pa):
                            ins = None
                            for kc in range(16):
                                ins = e.matmul(pa[:], lhsT=wA[:, kc, fo * 128:(fo + 1) * 128], rhs=yblk[:, kc, :], start=(kc == 0), stop=(kc == 15))
                            return ins
                        P.op("tensor", mma, reads=[yblkb, wAb[fo // 4]], writes=[pab])

                        def mmg(e, fo=fo, tb=tb, pg=pg):
                            ins = None
                            for kc in range(8):
                                ins = e.matmul(pg[:], lhsT=wG[:, kc, fo * 128:(fo + 1) * 128], rhs=hT[:, kc, 256 + tb * 512:256 + (tb + 1) * 512],
                                               start=(kc == 0), stop=(kc == 7))
                            return ins
                        P.op("tensor", mmg, reads=[wGb[fo // 4]], writes=[pgb])
                        sg, sgb = sg_ring.next()
                        P.op("scalar", lambda e, sg=sg, pg=pg: e.activation(out=sg[:], in_=pg[:], func=AF.Sigmoid), reads=[pgb], writes=[sgb])
                        mdst = mT[:, fo, tb * 512:(tb + 1) * 512]
                        if br == 0:
                            P.op("vector", lambda e, mdst=mdst, pa=pa, sg=sg: e.tensor_tensor(out=mdst, in0=pa[:], in1=sg[:], op=ALU.mult),
                                 reads=[pab, sgb], writes=[mTb[tb]])
                        else:
                            tm, tmb = tm_ring.next()
                            P.op("vector", lambda e, tm=tm, pa=pa, sg=sg: e.tensor_tensor(out=tm[:], in0=pa[:], in1=sg[:], op=ALU.mult),
                                 reads=[pab, sgb], writes=[tmb])
                            P.op("gpsimd", lambda e, mdst=mdst, tm=tm: e.tensor_tensor(out=mdst, in0=mdst, in1=tm[:], op=ALU.add),
                                 reads=[tmb], writes=[mTb[tb]])
                if br == 0 and ck("f1"):
                    return
            src_o = w_out_d.rearrange("(kc p) n -> p kc n", p=128)
            for cu in range(2):
                for kq in range(2):
                    load_w(wst_ring, None, src_o, cu * 512, 512, dst=wA[:, kq * 4:(kq + 1) * 4, cu * 512:(cu + 1) * 512], dstbuf=wAb[cu],
                           kc0=kq * 4, nkc=4, cast_eng="scalar")
            outb = Buf("out")
            ssf3 = [sb("ssf3_%d" % i, [128, 4], F32, ph) for i in range(3)]

            def wo_a(t):
                po = bank(4 + 2 * (t % 2), 2)
                pob = [pb[4 + 2 * (t % 2)], pb[5 + 2 * (t % 2)]]
                ssf_ = ssf3[t % 3]

                def mmo(e, t=t, po=po):
                    ins = None
                    for half in range(2):
                        for kc in range(8):
                            ins = e.matmul(po[:, half * 512:(half + 1) * 512], lhsT=mT[:, kc, t * 128:(t + 1) * 128], rhs=wA[:, kc, half * 512:(half + 1) * 512],
                                           start=(kc == 0), stop=(kc == 7))
                    return ins
                P.op("tensor", mmo, reads=wAb + [mTb[t // 4]], writes=pob)
                return (po, pob, ssf_)

            def wo_sq(t, po, pob, ssf_):
                sfb = Buf("ssf")
                for half in range(2):
                    P.op("scalar", lambda e, po=po, half=half, ssf_=ssf_: e.activation(out=junk[:], in_=po[:, half * 512:(half + 1) * 512], func=AF.Square,
                                                                                       accum_out=ssf_[:, half:half + 1]), reads=[pob[half]], writes=[jb, sfb])
                xt, xtb = xt_ring.next()
                P.dma("sync", lambda e, xt=xt, t=t: e.dma_start(out=xt[:], in_=x_d[t * 128:(t + 1) * 128, :]), writes=[xtb])
                return (po, pob, ssf_, sfb, xt, xtb)

            def wo_b(t, po, pob, ssf_, sfb, xt, xtb):
                P.op("vector", lambda e: e.tensor_tensor(out=ssf_[:, 2:3], in0=ssf_[:, 0:1], in1=ssf_[:, 1:2], op=ALU.add), reads=[sfb], writes=[sfb])
                P.op("scalar", lambda e: e.activation(out=ssf_[:, 3:4], in_=ssf_[:, 2:3], func=AF.Sqrt, scale=1.0 / 1024.0, bias=EPS), reads=[sfb], writes=[sfb])
                P.op("vector", lambda e: e.reciprocal(out=ssf_[:, 3:4], in_=ssf_[:, 3:4]), reads=[sfb], writes=[sfb])
                ot, otb = ot_ring.next()
                P.op("vector", lambda e, ot=ot: e.scalar_tensor_tensor(out=ot[:], in0=po[:], scalar=ssf_[:, 3:4], in1=Gt[:], op0=ALU.mult, op1=ALU.mult),
                     reads=pob + [sfb], writes=[otb])
                P.op("gpsimd", lambda e, ot=ot: e.tensor_tensor(out=ot[:], in0=ot[:], in1=xt[:], op=ALU.add), reads=[xtb], writes=[otb])
                P.dma("gpsimd", lambda e, ot=ot: e.dma_start(out=out_d[t * 128:(t + 1) * 128, :], in_=ot[:]), reads=[otb], writes=[outb])

            pend = wo_sq(0, *wo_a(0))
            for t in range(16):
                mm_next = wo_a(t + 1) if t < 15 else None
                wo_b(t, *pend)
                pend = wo_sq(t + 1, *mm_next) if mm_next is not None else None
            P.barrier()
            if dbg_spec is not None and stop_after == 3:
                dump(mT[:, 0, 0:512], 512)
                dump(mT[:, 5, 1024:1536], 512)

        yrb = Buf("yrT_d")
        ysb = Buf("ysT_d")
        if stop_after is None or stop_after in (1, 3, 5):
            with ExitStack() as ph:
                phase_R(ph)
            P.barrier()
        if stop_after is None or stop_after in (2, 3, 5, 6):
            with ExitStack() as ph:
                phase_S(ph)
            P.barrier()
        if stop_after is None or stop_after in (3, 4, 6):
            with ExitStack() as ph:
                phase_F(ph)
            P.barrier()

        if dbg_spec is not None and stop_after == 1:
            with nc.sbuf_tensor("dbl", [128, 2048], BF16) as dbl:
                lb = Buf()
                for r in range(2):
                    P.dma("sync", lambda e, r=r: e.dma_start(out=dbl[:], in_=yrT_d[r * 1024:r * 1024 + 128, :]), writes=[lb])
                    dump(dbl[:], 2048, reads=[lb])

        if dbg_spec is not None and stop_after == 2:
            with nc.sbuf_tensor("dbl2", [128, 2048], BF16) as dbl:
                lb = Buf()
                for r in range(2):
                    P.dma("sync", lambda e, r=r: e.dma_start(out=dbl[:], in_=ysT_d[r * 128:r * 128 + 128, :]), writes=[lb])
                    dump(dbl[:], 2048, reads=[lb])

        if dbg_spec is not None and stop_after == 3:
            with nc.sbuf_tensor("dbl3", [128, 512], BF16) as dbl:
                lb = Buf()
                for r in range(4):
                    P.dma("sync", lambda e, r=r: e.dma_start(out=dbl[:], in_=yrT_d[r * 512:r * 512 + 128, 512:1024]), writes=[lb])
                    dump(dbl[:], 512, reads=[lb])
                for r in range(8):
                    P.dma("sync", lambda e, r=r: e.dma_start(out=dbl[:], in_=ysT_d[r * 256 + 128:r * 256 + 256, 512:1024]), writes=[lb])
                    dump(dbl[:], 512, reads=[lb])

        if stop_after is not None and stop_after not in (3, 4, 6):
            with nc.sbuf_tensor("zt", [128, 1024], F32) as zt:
                zb = Buf()
                P.op("vector", lambda e: e.memset(zt[:], 0.0), writes=[zb])
                ob = Buf()
                for t in range(16):
                    P.dma("sync", lambda e, t=t: e.dma_start(out=out_d[t * 128:(t + 1) * 128, :], in_=zt[:]), reads=[zb], writes=[ob])
                P.barrier()
            return nc

        P.barrier()
    return nc


def host_constants():
    bf = ml_dtypes.bfloat16
    identb = np.eye(128, dtype=np.float32).astype(bf)
    swap = np.zeros((128, 128), np.float32)
    for m in range(64):
        swap[m + 64, m] = 1.0
        swap[m, m + 64] = 1.0
    swapb = swap.astype(bf)
    inv_freq = (10000.0 ** (-np.arange(64, dtype=np.float32) / 64.0)).astype(np.float32)
    fr = np.concatenate([inv_freq, inv_freq])
    sign = np.concatenate([-np.ones(64), np.ones(64)]).astype(np.float32)
    rows = np.arange(32, dtype=np.float32)
    cols = np.arange(64, dtype=np.float32)
    a0 = (rows[None, :] * fr[:, None]).astype(np.float32)
    a1 = (cols[None, :] * fr[:, None]).astype(np.float32)
    ropec = np.concatenate([np.cos(a0), np.cos(a1), np.sin(a0) * sign[:, None], np.sin(a1) * sign[:, None]], axis=1).astype(np.float32)
    j = np.arange(128, dtype=np.float32)[:, None]
    xx = np.arange(STRIP, dtype=np.float32)[None, :]
    dlt = (xx - XOFF - j).astype(np.float32)
    k = np.arange(128)[:, None]
    i = np.arange(128)[None, :]
    tri = np.stack([(k <= i), (k >= i)], axis=1).astype(np.float32).reshape(128, 256)
    L = np.stack([(k > i), (k < i)], axis=1).astype(np.float32).reshape(128, 256)
    m01 = np.stack([(i >= k), (i < k)], axis=1).astype(np.float32).reshape(128, 256)
    ones = np.ones((128, 128), np.float32)
    ssdc = np.concatenate([tri, L, m01, ones], axis=1).astype(np.float32)
    NEG = -30000.0
    negm = np.stack([np.where(i >= k, 0.0, NEG), np.where(i < k, 0.0, NEG)], axis=1)
    negm = np.repeat(negm[:, :, None, :], 4, axis=2).reshape(128, 1024).astype(np.float32).astype(bf)
    sel = np.zeros((8, 8, 128), np.float32)
    for q in range(8):
        sel[q, q, :] = 1.0
    return dict(identb=identb, swapb=swapb, ropec=ropec, dlt=dlt, ssdc=ssdc, negm=negm, self=sel.reshape(8, 1024),
                identf=np.eye(128, dtype=np.float32))


def host_inputs(inputs):
    f = np.float32
    col = lambda v: np.ascontiguousarray(np.asarray(v, f).reshape(-1, 128).T)
    conv_w = np.asarray(inputs["ssd_conv_w"][0], f)
    vecs = np.concatenate([
        col(inputs["norm_pre_w"][0]), col(inputs["b_mod"][0]),
        np.concatenate([col(conv_w[kk]) for kk in range(5)], axis=1),
        col(inputs["ssd_conv_b"][0]), col(inputs["ret_gn_w"][0]), col(inputs["ssd_norm_w"][0])], axis=1).astype(f)
    assert vecs.shape == (128, NV)
    rows = np.concatenate([
        np.asarray(inputs["norm_post_w"][0], f), np.asarray(inputs["b_mod"][0], f)[2048:3072],
        np.asarray(inputs["ssd_D"][0], f), np.asarray(inputs["ssd_dt_bias"][0], f).reshape(-1),
        np.asarray(inputs["ssd_a_log"][0], f).reshape(-1), np.asarray(inputs["ret_decay"][0], f).reshape(-1)])[None, :].astype(f)
    assert rows.shape == (1, NR)
    shared = dict(
        w_mod=np.ascontiguousarray(inputs["w_mod"][0], f), w_in=np.ascontiguousarray(inputs["w_in"][0], f),
        w_ret_o=np.ascontiguousarray(inputs["w_ret_o"][0], f), w_ssd_o=np.ascontiguousarray(inputs["w_ssd_o"][0], f),
        w_out=np.ascontiguousarray(inputs["w_out"][0], f), vecs=vecs, rows=rows)
    shared.update(host_constants())
    maps = []
    for b in range(8):
        cc = np.stack([np.asarray(inputs["c"][b], f), np.asarray(inputs["c_ctx"], f)], axis=1)
        cct = np.ascontiguousarray(cc.reshape(8, 128, 2).transpose(1, 0, 2).reshape(128, 16))
        m = dict(shared)
        m.update(x=np.ascontiguousarray(inputs["x"][b], f), ctx=np.ascontiguousarray(inputs["ctx"][b], f), cct=cct)
        maps.append(m)
    return maps


def kernel(**inputs):
    maps = host_inputs(inputs)
    nc = build_program()
    res = run_bass_kernel_spmd(nc, maps, core_ids=list(range(8)))
    return np.stack([np.asarray(r["out"], np.float32) for r in res.results], axis=0)
```

```python
import numpy as np
import ml_dtypes
from contextlib import ExitStack
import concourse.bass as bass
import concourse.mybir as mybir
from concourse.bass_utils import run_bass_kernel_spmd

F32 = mybir.dt.float32
BF16 = mybir.dt.bfloat16
AF = mybir.ActivationFunctionType
ALU = mybir.AluOpType

ENGS = ("tensor", "vector", "scalar", "gpsimd", "sync")
EPS = 1e-6
EVAC_ACT = ()
NTOK = 2304
NCTX = 256
NLAT = 2048
XOFF = 2176
STRIP = 4480

V_NPW, V_BMOD, V_CONVW, V_CONVB, V_GNW, V_SNW, NV = 0, 8, 32, 192, 224, 240, 256
R_NPOST, R_BG, R_D, R_DTB, R_ALOG, R_RDEC, NR = 0, 1024, 2048, 2080, 2144, 2208, 2216
RS = 2048
C_TRI, C_L, C_M01, C_ONES, NC_SSD = 0, 256, 512, 768, 896


class Buf:
    __slots__ = ("name", "w", "r", "excl")

    def __init__(self, name="", excl=False):
        self.name = name
        self.w = None
        self.r = {}
        self.excl = excl


class Prog:
    def __init__(self, nc, st, n_dma=12, queues=("sync", "gpsimd")):
        self.nc = nc
        self.h = {e: getattr(nc, e) for e in ENGS}
        self.cnt = {e: 0 for e in ENGS}
        self.seen = {e: {} for e in ENGS}
        self.sems = {("e", e): st.enter_context(nc.semaphore("e_" + e)) for e in ENGS}
        self.n_dma = n_dma
        self.dma_tot = {}
        self.dma_rr = {q: 0 for q in queues}
        for q in queues:
            for i in range(n_dma):
                self.sems[("d", q, i)] = st.enter_context(nc.semaphore("d_%s_%d" % (q, i)))
        self.nwaits = 0

    def _deps(self, eng, reads, writes):
        deps = {}
        own = ("e", eng)
        for b in reads:
            if b.w is not None and b.w[1] > deps.get(b.w[0], 0):
                deps[b.w[0]] = b.w[1]
            if b.excl:
                for k, v in b.r.items():
                    if k != own and v > deps.get(k, 0):
                        deps[k] = v
        for b in writes:
            if b.w is not None and b.w[1] > deps.get(b.w[0], 0):
                deps[b.w[0]] = b.w[1]
            for k, v in b.r.items():
                if v > deps.get(k, 0):
                    deps[k] = v
        seen = self.seen[eng]
        for k, v in deps.items():
            if eng == "tensor" and k == ("e", "tensor"):
                continue
            if seen.get(k, 0) < v:
                seen[k] = v
                self.h[eng].wait_ge(self.sems[k], v)
                self.nwaits += 1

    @staticmethod
    def _mark(key, val, reads, writes):
        for b in reads:
            if b.r.get(key, 0) < val:
                b.r[key] = val
        for b in writes:
            b.w = (key, val)
            b.r = {}

    def op(self, eng, fn, reads=(), writes=()):
        self._deps(eng, reads, writes)
        ins = fn(self.h[eng])
        self.cnt[eng] += 1
        key = ("e", eng)
        ins.then_inc(self.sems[key], 1)
        self._mark(key, self.cnt[eng], reads, writes)

    def dma(self, q, fn, reads=(), writes=()):
        i = self.dma_rr[q]
        self.dma_rr[q] = (i + 1) % self.n_dma
        key = ("d", q, i)
        prev = self.dma_tot.get(key, 0)
        self._deps(q, reads, writes)
        if prev > 0 and self.seen[q].get(key, 0) < prev:
            self.seen[q][key] = prev
            self.h[q].wait_ge(self.sems[key], prev)
        ins = fn(self.h[q])
        ins.then_inc(self.sems[key], 16)
        self.dma_tot[key] = prev + 16
        self._mark(key, prev + 16, reads, writes)

    def barrier(self):
        tot = {("e", e): self.cnt[e] for e in ENGS}
        tot.update(self.dma_tot)
        for e in ENGS:
            for k, v in tot.items():
                if v > self.seen[e].get(k, 0):
                    self.seen[e][k] = v
                    self.h[e].wait_ge(self.sems[k], v)

    def wait_all(self, eng, bufs):
        self._deps(eng, bufs, ())


class _Stop(Exception):
    pass


class Ring:
    def __init__(self, views, name="ring", excl=False):
        self.views = views
        self.bufs = [Buf("%s%d" % (name, i), excl) for i in range(len(views))]
        self.i = 0

    def next(self):
        i = self.i
        self.i = (i + 1) % len(self.views)
        return self.views[i], self.bufs[i]


def build_program(stop_after=None, dbg_spec=None, cut=None):
    nc = bass.Bass("TRN2", target_bir_lowering=False)

    def din(name, shape, dt=F32):
        return nc.dram_tensor(name, list(shape), dt, kind="ExternalInput").ap()

    x_d = din("x", [NLAT, 1024])
    ctx_d = din("ctx", [NCTX, 1024])
    cct_d = din("cct", [128, 16])
    w_mod_d = din("w_mod", [1024, 3072])
    w_in_d = din("w_in", [1024, 14400])
    w_ret_o_d = din("w_ret_o", [2048, 1024])
    w_ssd_o_d = din("w_ssd_o", [2048, 1024])
    w_out_d = din("w_out", [1024, 1024])
    vecs_d = din("vecs", [128, NV])
    rows_d = din("rows", [1, NR])
    identb_d = din("identb", [128, 128], BF16)
    swapb_d = din("swapb", [128, 128], BF16)
    ropec_d = din("ropec", [128, 192])
    dlt_d = din("dlt", [128, STRIP])
    ssdc_d = din("ssdc", [128, NC_SSD])
    negm_d = din("negm", [128, 1024], BF16)
    self_d = din("self", [8, 1024])
    identf_d = din("identf", [128, 128])
    out_d = nc.dram_tensor("out", [NLAT, 1024], F32, kind="ExternalOutput").ap()
    yrT_d = nc.dram_tensor("yrT_scr", [2048, NLAT], BF16, kind="Internal").ap()
    ysT_d = nc.dram_tensor("ysT_scr", [2048, NLAT], BF16, kind="Internal").ap()
    dbg_d = None
    if dbg_spec is not None:
        dbg_d = nc.dram_tensor("dbg", [128, dbg_spec], F32, kind="ExternalOutput").ap()

    w_in_v = w_in_d.rearrange("(kc p) n -> p kc n", p=128)
    w_mod_v = w_mod_d.rearrange("(kc p) n -> p kc n", p=128)

    with ExitStack() as st:
        P = Prog(nc, st)

        uid = [0]

        def sb(name, shape, dt, stack=st):
            uid[0] += 1
            return stack.enter_context(nc.sbuf_tensor("s%d_%s" % (uid[0], name), list(shape), dt))

        ps_all = st.enter_context(nc.psum_tensor("ps_all", [128, 4096], F32))

        def bank(i, n=1):
            return ps_all[:, i * 512:(i + n) * 512]

        hT = sb("hT", [128, 8, NTOK], BF16)
        vecs = sb("vecs", [128, NV], F32)
        rowss = sb("rowss", [128, NR - RS], F32)
        Gt = sb("Gt", [128, 1024], F32)
        identb = sb("identb", [128, 128], BF16)
        swapb = sb("swapb", [128, 128], BF16)
        ropec = sb("ropec", [128, 192], F32)
        ssdc = sb("ssdc", [128, NC_SSD], F32)
        lg = sb("lg", [128, 8], F32)
        nlg = sb("nlg", [128, 8], F32)
        Aneg = sb("Aneg", [128, 64], F32)
        sc1 = sb("sc1", [128, 8, 2], F32)
        shf = sb("shf", [128, 8, 2], F32)
        dbg_state = {"off": 0}

        def dump(ap, n, reads=(), pstack=None):
            if dbg_d is None:
                return
            with nc.sbuf_tensor("dbgt%d" % dbg_state["off"], [128, n], F32) as t:
                tb = Buf("dbgt")
                P.op("vector", lambda e: e.tensor_copy(out=t[:], in_=ap), reads=list(reads), writes=[tb])
                o = dbg_state["off"]
                P.dma("sync", lambda e: e.dma_start(out=dbg_d[:, o:o + n], in_=t[:]), reads=[tb], writes=[])
                dbg_state["off"] = o + n
                P.barrier()

        def load_w(wst_ring, wbf_ring, src_view, c0, ncols, dst=None, dstbuf=None, scale_col=None, kc0=0, cast_eng="gpsimd", nkc=8, q="sync"):
            sv_, sbuf_ = wst_ring.next()
            if nkc == 8:
                sv = sv_[:, :, 0:ncols]
            else:
                sv = sv_[:].rearrange("p a b -> p (a b)")[:, 0:nkc * ncols].rearrange("p (a b) -> p a b", a=nkc)
            P.dma(q, lambda e: e.dma_start(out=sv, in_=src_view[:, kc0:kc0 + nkc, c0:c0 + ncols]), writes=[sbuf_])
            if dst is None:
                bv, bbuf = wbf_ring.next()
                bva = bv[:, :, 0:ncols]
            else:
                bv, bbuf = dst, dstbuf
                bva = dst[:, :, 0:ncols]
            if scale_col is None and cast_eng == "scalar":
                P.op("scalar", lambda e: e.activation(out=bva, in_=sv, func=AF.Copy), reads=[sbuf_], writes=[bbuf])
            elif scale_col is None:
                P.op("gpsimd", lambda e: e.tensor_copy(out=bva, in_=sv), reads=[sbuf_], writes=[bbuf])
            elif cast_eng == "scalar":
                for kc in range(nkc):
                    P.op("scalar", lambda e, kc=kc: e.activation(
                        out=bva[:, kc, :], in_=sv[:, kc, :], func=AF.Identity,
                        scale=vecs[:, scale_col + kc0 + kc:scale_col + kc0 + kc + 1]), reads=[sbuf_], writes=[bbuf])
            else:
                for kc in range(nkc):
                    P.op("gpsimd", lambda e, kc=kc: e.tensor_scalar(
                        out=bva[:, kc, :], in0=sv[:, kc, :],
                        scalar1=vecs[:, scale_col + kc0 + kc:scale_col + kc0 + kc + 1], scalar2=0.0,
                        op0=ALU.mult, op1=ALU.add), reads=[sbuf_], writes=[bbuf])
            return bv, bbuf

        def ck(label):
            if cut == label:
                P.barrier()
                return True
            return False

        def phase0(ph):
          if True:
              cb = Buf("consts")
              wst_ring = Ring([sb("wst0_%d" % i, [128, 8, 256], F32, ph) for i in range(4)], "wst0")
              for dst, src in ((vecs, vecs_d), (identb, identb_d), (swapb, swapb_d), (ropec, ropec_d), (ssdc, ssdc_d)):
                  P.dma("sync", lambda e, dst=dst, src=src: e.dma_start(out=dst[:], in_=src[:, :]), writes=[Buf()])
              P.dma("sync", lambda e: e.dma_start(out=rowss[:], in_=rows_d[0:1, RS:NR].partition_broadcast(128)), writes=[cb])
              rowsb = sb("rowsb", [128, 2048], F32, ph)
              P.dma("sync", lambda e: e.dma_start(out=rowsb[:], in_=rows_d[0:1, 0:2048].partition_broadcast(128)), writes=[cb])
              cct = sb("cct", [128, 8, 2], F32, ph)
              P.dma("sync", lambda e: e.dma_start(out=cct[:].rearrange("p a b -> p (a b)"), in_=cct_d[:, :]), writes=[cb])
              P.barrier()
              if ck("a"):
                  return
              scc = sb("scc", [128, 8, 2], F32, ph)
              sccrep = sb("sccrep", [128, 8, 128], F32, ph)
              b0 = Buf("p0")
              P.op("scalar", lambda e: e.activation(out=scc[:], in_=cct[:], func=AF.Silu), writes=[b0])
              P.op("vector", lambda e: e.tensor_copy(out=sccrep[:], in_=scc[:, :, 0:1].broadcast_to([128, 8, 128])),
                   reads=[b0], writes=[b0])
              P.op("scalar", lambda e: e.activation(out=nlg[:], in_=rowss[:, R_RDEC - RS:R_RDEC - RS + 8], func=AF.Exp, scale=-1.0),
                   writes=[b0])
              P.op("scalar", lambda e: e.activation(out=nlg[:], in_=nlg[:], func=AF.Ln, bias=1.0), reads=[b0], writes=[b0])
              P.op("vector", lambda e: e.tensor_scalar(out=lg[:], in0=nlg[:], scalar1=-1.0, scalar2=None, op0=ALU.mult),
                   reads=[b0], writes=[b0])
              P.op("scalar", lambda e: e.activation(out=Aneg[:], in_=rowss[:, R_ALOG - RS:R_ALOG - RS + 64], func=AF.Exp),
                   writes=[b0])
              P.op("vector", lambda e: e.tensor_scalar(out=Aneg[:], in0=Aneg[:], scalar1=-1.0, scalar2=None, op0=ALU.mult),
                   reads=[b0], writes=[b0])
              if ck("b"):
                  return
              ps_mod = bank(0)[:, 0:48]
              ps_gate = bank(1, 2)
              pm = Buf("psmod", True)
              pg = Buf("psgate", True)
              for u in range(12):
                  sv, sbuf_ = wst_ring.next()
                  P.dma("sync" if u % 2 == 0 else "gpsimd", lambda e, sv=sv, u=u: e.dma_start(out=sv[:], in_=w_mod_v[:, :, u * 256:(u + 1) * 256]), writes=[sbuf_])

                  def mm_mod(e, sv=sv, u=u):
                      ins = None
                      for j in range(2):
                          col = (u * 2 + j) * 2
                          for kc in range(8):
                              ins = e.matmul(ps_mod[:, col:col + 2], lhsT=sv[:, kc, j * 128:(j + 1) * 128], rhs=scc[:, kc, :],
                                             start=(kc == 0), stop=(kc == 7))
                      if u >= 8:
                          for kc in range(8):
                              ins = e.matmul(ps_gate[:, (u - 8) * 256:(u - 7) * 256], lhsT=sccrep[:, kc, :], rhs=sv[:, kc, :],
                                             start=(kc == 0), stop=(kc == 7))
                      return ins
                  P.op("tensor", mm_mod, reads=[sbuf_, b0], writes=[pm, pg])
              if ck("c"):
                  return
              modv = sb("modv", [128, 24, 2], F32, ph)
              P.op("vector", lambda e: e.tensor_tensor(out=modv[:], in0=ps_mod.rearrange("p (a b) -> p a b", b=2),
                                                       in1=vecs[:, V_BMOD:V_BMOD + 24].unsqueeze(2).broadcast_to([128, 24, 2]),
                                                       op=ALU.add), reads=[pm], writes=[b0])
              P.op("vector", lambda e: e.scalar_tensor_tensor(out=sc1[:], in0=modv[:, 8:16, :], scalar=1.0,
                                                              in1=vecs[:, V_NPW:V_NPW + 8].unsqueeze(2).broadcast_to([128, 8, 2]),
                                                              op0=ALU.add, op1=ALU.mult), reads=[b0], writes=[b0])
              P.op("vector", lambda e: e.tensor_copy(out=shf[:], in_=modv[:, 0:8, :]), reads=[b0], writes=[b0])
              P.op("vector", lambda e: e.tensor_tensor(out=Gt[:], in0=ps_gate, in1=rowsb[:, 1024:2048], op=ALU.add),
                   reads=[pg], writes=[b0])
              P.op("vector", lambda e: e.tensor_tensor(out=Gt[:], in0=Gt[:], in1=rowsb[:, 0:1024], op=ALU.mult),
                   reads=[b0], writes=[b0])
              P.barrier()

              if ck("d"):
                  return
              xt_ring = Ring([sb("xt%d" % i, [128, 1024], F32, ph) for i in range(3)], "xt")
              xb_ring = Ring([sb("xb%d" % i, [128, 1024], BF16, ph) for i in range(2)], "xb")
              junk = sb("junk", [128, 1024], BF16, ph)
              jb = Buf("junk")
              ss = sb("ss", [128, 18], F32, ph)
              rs = sb("rs", [128, 18], F32, ph)
              pst_ring = Ring([bank(3).bitcast(BF16), bank(4).bitcast(BF16)], "pst", True)
              for t in range(18):
                  src = ctx_d[t * 128:(t + 1) * 128, :] if t < 2 else x_d[(t - 2) * 128:(t - 1) * 128, :]
                  which = 1 if t < 2 else 0
                  xt, xtb = xt_ring.next()
                  P.dma("sync", lambda e, xt=xt, src=src: e.dma_start(out=xt[:], in_=src), writes=[xtb])
                  sb_ = Buf("ss")
                  P.op("scalar", lambda e, xt=xt, t=t: e.activation(out=junk[:], in_=xt[:], func=AF.Square, accum_out=ss[:, t:t + 1]),
                       reads=[xtb], writes=[jb, sb_])
                  if t == 0 and ck("e"):
                      return
                  P.op("scalar", lambda e, t=t: e.activation(out=rs[:, t:t + 1], in_=ss[:, t:t + 1], func=AF.Sqrt, scale=1.0 / 1024.0, bias=EPS),
                       reads=[sb_], writes=[sb_])
                  P.op("vector", lambda e, t=t: e.reciprocal(out=rs[:, t:t + 1], in_=rs[:, t:t + 1]), reads=[sb_], writes=[sb_])
                  if t == 0 and ck("f"):
                      return
                  xb, xbb = xb_ring.next()
                  P.op("vector", lambda e, xt=xt, xb=xb, t=t: e.tensor_scalar(out=xb[:], in0=xt[:], scalar1=rs[:, t:t + 1], scalar2=None, op0=ALU.mult),
                       reads=[xtb, sb_], writes=[xbb])
                  if t == 0 and ck("g"):
                      return
                  pst, pstb = pst_ring.next()

                  def tr8(e, xb=xb, pst=pst):
                      ins = None
                      for kc in range(8):
                          ins = e.transpose(pst[:, kc * 128:(kc + 1) * 128], xb[:, kc * 128:(kc + 1) * 128], identb[:])
                      return ins
                  P.op("tensor", tr8, reads=[xbb], writes=[pstb])
                  if t == 0 and ck("h"):
                      return
                  for kc in range(8):
                      o = hT[:, kc, t * 128:(t + 1) * 128]
                      i_ = pst[:, kc * 128:(kc + 1) * 128]
                      s_ = sc1[:, kc, which:which + 1]
                      b_ = shf[:, kc, which:which + 1]
                      if kc % 8 in EVAC_ACT:
                          P.op("scalar", lambda e, o=o, i_=i_, s_=s_, b_=b_: e.activation(out=o, in_=i_, func=AF.Identity, scale=s_, bias=b_),
                               reads=[pstb], writes=[])
                      else:
                          P.op("vector", lambda e, o=o, i_=i_, s_=s_, b_=b_: e.tensor_scalar(out=o, in0=i_, scalar1=s_, scalar2=b_, op0=ALU.mult, op1=ALU.add),
                               reads=[pstb], writes=[])
                  if t == 0 and ck("j"):
                      return
                  if t == 3 and ck("k"):
                      return
              P.barrier()
        with ExitStack() as ph:
            phase0(ph)
        if dbg_spec is not None and stop_after == 0:
            for kc in range(2):
                dump(hT[:, kc, 0:512], 512)
            dump(Gt[:, 0:256], 256)
            dump(lg[:], 8)

        def phase_R(ph):
            wst_ring = Ring([sb("wstR%d" % i, [128, 8, 256], F32, ph) for i in range(2)], "wstR")
            wbf_ring = Ring([sb("wbfR%d" % i, [128, 8, 256], BF16, ph) for i in range(3)], "wbfR")
            KT = sb("KT", [128, 2, NTOK], BF16, ph)
            QT = sb("QT", [128, 2, NLAT], BF16, ph)
            Vt = sb("Vt", [128, 18, 512], BF16, ph)
            Gs = sb("Gs", [128, 16, 512], BF16, ph)
            Th = sb("Th", [128, STRIP], F32, ph)
            dl_ring = Ring([sb("dl%d" % i, [128, 560], F32, ph) for i in range(2)], "dl")
            t1_ring = Ring([sb("t1_%d" % i, [128, 560], F32, ph) for i in range(2)], "t1")
            M_ring = Ring([sb("M%d" % i, [128, 512], BF16, ph) for i in range(4)], "M")
            ys4 = [sb("ys4_%d" % i, [128, 4, 512], F32, ph) for i in range(2)]
            ys4b = [[Buf("ys4") for _ in range(4)] for _ in range(2)]
            xsb_ring = Ring([sb("xsb%d" % i, [128, 512], BF16, ph) for i in range(3)], "xsb")
            rt_ring = Ring([sb("rt%d" % i, [128, 512], F32, ph) for i in range(4)], "rt")
            yn_ring = Ring([sb("yn%d" % i, [128, 512], F32, ph) for i in range(4)], "yn")
            yg_ring = Ring([sb("yg%d" % i, [128, 512], BF16, ph) for i in range(4)], "yg")
            yT_ring = Ring([sb("yT%d" % i, [128, 4, 512], BF16, ph) for i in range(2)], "yT")
            st6 = sb("st6", [128, 4, 6], F32, ph)
            mv = sb("mv", [128, 4, 2], F32, ph)
            rstd = sb("rstd", [128, 4], F32, ph)
            nmr = sb("nmr", [128, 4], F32, ph)
            pbR = [Buf("bankR%d" % i, True) for i in range(8)]
            psy = [bank(i) for i in range(4)]
            psyb = pbR[0:4]
            pss_ring = Ring([bank(4), bank(5), bank(6)], "pss")
            pss_ring.bufs = pbR[4:7]
            psp_ring = Ring([bank(7), bank(0), bank(1), bank(2), bank(3)], "psp")
            psp_ring.bufs = [pbR[7]] + pbR[0:4]
            pst_ring = Ring([bank(7).bitcast(BF16)], "pstR")
            pst_ring.bufs = [pbR[7]]
            cos0 = ropec[:, 0:32].unsqueeze(2).broadcast_to([128, 32, 64])
            cos1 = ropec[:, 32:96].unsqueeze(1).broadcast_to([128, 32, 64])
            sin0 = ropec[:, 96:128].unsqueeze(2).broadcast_to([128, 32, 64])
            sin1 = ropec[:, 128:192].unsqueeze(1).broadcast_to([128, 32, 64])

            def v3(ap):
                return ap.rearrange("p (a b) -> p a b", b=64)

            ktb = [Buf("kt") for _ in range(18)]
            qtb = [Buf("qt") for _ in range(4)]
            vtb = [Buf("vt") for _ in range(18)]
            gsb = [Buf("gs") for _ in range(16)]
            thb = Buf("Th")
            def gen_mask(h, blks=range(8)):
                for blk in blks:
                    dl, dlb = dl_ring.next()
                    c = slice(blk * 560, (blk + 1) * 560)
                    P.dma("sync", lambda e, dl=dl, c=c: e.dma_start(out=dl[:], in_=dlt_d[:, c]), writes=[dlb])
                    t1, t1b = t1_ring.next()
                    P.op("scalar", lambda e, dl=dl, t1=t1: e.activation(out=t1[:], in_=dl[:], func=AF.Identity, scale=lg[:, 4 + h:5 + h]),
                         reads=[dlb], writes=[t1b])
                    P.op("vector", lambda e, dl=dl, t1=t1: e.scalar_tensor_tensor(out=t1[:], in0=dl[:], scalar=nlg[:, h:h + 1], in1=t1[:],
                                                                                 op0=ALU.mult, op1=ALU.max), reads=[dlb], writes=[t1b])
                    P.op("scalar", lambda e, t1=t1, c=c: e.activation(out=Th[:, c], in_=t1[:], func=AF.Exp, scale=-1.0), reads=[t1b], writes=[thb])

            gen_mask(0)
            kq_pref = {}
            for h in range(4):
                def kq_stage1(kind, wv, wb, dc, t0, n, isctx):
                    psp, pspb = psp_ring.next()

                    def mmp(e, wv=wv, dc=dc, t0=t0, n=n, psp=psp):
                        ins = None
                        for kc in range(8):
                            ins = e.matmul(psp[:, 0:n], lhsT=wv[:, kc, dc * 128:(dc + 1) * 128], rhs=hT[:, kc, t0:t0 + n],
                                           start=(kc == 0), stop=(kc == 7))
                        return ins
                    P.op("tensor", mmp, reads=[wb], writes=[pspb])
                    if isctx:
                        P.op("scalar", lambda e, psp=psp, dc=dc: e.activation(out=KT[:, dc, 0:256], in_=psp[:, 0:256], func=AF.Copy),
                             reads=[pspb], writes=[ktb[0], ktb[1]])
                        return None
                    qb = (t0 - 256) // 512
                    sc = 1.0 if kind == "k" else 1.0 / 16.0
                    xsb, xsbb = xsb_ring.next()
                    P.op("scalar", lambda e, psp=psp, xsb=xsb, sc=sc: e.activation(out=xsb[:], in_=psp[:], func=AF.Copy, scale=sc),
                         reads=[pspb], writes=[xsbb])
                    rt, rtb = rt_ring.next()
                    cosb = cos0 if dc == 0 else cos1
                    crow = slice(qb * 8, qb * 8 + 8)
                    P.op("vector", lambda e, psp=psp, rt=rt, cosb=cosb, crow=crow, sc=sc: e.scalar_tensor_tensor(
                        out=v3(rt[:]), in0=v3(psp[:]), scalar=sc, in1=cosb[:, crow, :], op0=ALU.mult, op1=ALU.mult),
                        reads=[pspb], writes=[rtb])
                    return (kind, dc, t0, qb, crow, xsb, xsbb, rt, rtb)

                def kq_stage2(kind, dc, t0, qb, crow, xsb, xsbb, rt, rtb):
                    sinb = sin0 if dc == 0 else sin1
                    psw, pswb = psp_ring.next()
                    P.op("tensor", lambda e, psw=psw, xsb=xsb: e.matmul(psw[:], lhsT=swapb[:], rhs=xsb[:], start=True, stop=True),
                         reads=[xsbb], writes=[pswb])
                    rt2, rt2b = rt_ring.next()
                    P.op("vector", lambda e, psw=psw, rt2=rt2, sinb=sinb, crow=crow: e.tensor_tensor(
                        out=v3(rt2[:]), in0=v3(psw[:]), in1=sinb[:, crow, :], op=ALU.mult), reads=[pswb], writes=[rt2b])
                    if kind == "k":
                        dst = KT[:, dc, t0:t0 + 512]
                        wr = ktb[2 + qb * 4:6 + qb * 4]
                    else:
                        dst = QT[:, dc, qb * 512:(qb + 1) * 512]
                        wr = [qtb[qb]]
                    P.op("gpsimd", lambda e, dst=dst, rt=rt, rt2=rt2: e.tensor_tensor(out=dst, in0=rt[:], in1=rt2[:], op=ALU.add),
                         reads=[rtb, rt2b], writes=wr)

                kq_pending = None
                for kind in ("k", "q"):
                    c0 = (1024 if kind == "k" else 0) + h * 256
                    if (h, kind) in kq_pref:
                        wv, wb = kq_pref.pop((h, kind))
                    else:
                        wv, wb = load_w(wst_ring, wbf_ring, w_in_v, c0, 256, cast_eng="scalar")
                    for dc in range(2):
                        blocks = [(0, 256, True)] if kind == "k" else []
                        blocks += [(256 + qb * 512, 512, False) for qb in range(4)]
                        for (t0, n, isctx) in blocks:
                            st1 = kq_stage1(kind, wv, wb, dc, t0, n, isctx)
                            if kq_pending is not None:
                                kq_stage2(*kq_pending)
                            kq_pending = st1
                if kq_pending is not None:
                    kq_stage2(*kq_pending)
                if h == 0 and ck("r1"):
                    return
                vg_it = 0
                for kind in ("v", "g"):
                    for half in range(2):
                        c0 = (2048 if kind == "v" else 4096) + h * 512 + half * 256
                        wv, wb = load_w(wst_ring, wbf_ring, w_in_v, c0, 256, cast_eng="scalar")
                        for t in range(18):
                            if kind == "g" and t < 2:
                                continue
                            if h > 0 and vg_it % 8 == 0 and vg_it // 8 < 8:
                                gen_mask(h, [vg_it // 8])
                            vg_it += 1
                            psp, pspb = psp_ring.next()

                            def mmp(e, wv=wv, t=t, psp=psp):
                                ins = None
                                for kc in range(8):
                                    ins = e.matmul(psp[:, 0:256], lhsT=hT[:, kc, t * 128:(t + 1) * 128], rhs=wv[:, kc, :],
                                                   start=(kc == 0), stop=(kc == 7))
                                return ins
                            P.op("tensor", mmp, reads=[wb], writes=[pspb])
                            if kind == "v":
                                P.op("scalar", lambda e, psp=psp, t=t, half=half: e.activation(
                                    out=Vt[:, t, half * 256:(half + 1) * 256], in_=psp[:, 0:256], func=AF.Copy),
                                    reads=[pspb], writes=[vtb[t]])
                            else:
                                P.op("scalar", lambda e, psp=psp, t=t, half=half: e.activation(
                                    out=Gs[:, t - 2, half * 256:(half + 1) * 256], in_=psp[:, 0:256], func=AF.Silu),
                                    reads=[pspb], writes=[gsb[t - 2]])
                if h == 0 and ck("r2"):
                    return
                if h < 3:
                    for kind in ("k", "q"):
                        c0n = (1024 if kind == "k" else 0) + (h + 1) * 256
                        kq_pref[(h + 1, kind)] = load_w(wst_ring, wbf_ring, w_in_v, c0n, 256, cast_eng="scalar")
                if h == 0 and ck("r3"):
                    return
                ktiles = [(0, -256), (1, -128)] + [(2 + t, 128 * t) for t in range(16)] + [(0, 2048), (1, 2176)]
                LA = 2
                pending = [None]

                def gn_stages(qb, slot):
                    sbbs = [Buf("stat") for _ in range(4)]
                    srcs = [(ys4[slot][:, qt, :], ys4b[slot][qt]) for qt in range(4)]
                    yns = []
                    ygs = []

                    def s1():
                        for qt in range(4):
                            src, srcb = srcs[qt]
                            P.op("vector", lambda e, qt=qt, src=src: e.bn_stats(out=st6[:, qt, :], in_=src), reads=[srcb], writes=[sbbs[qt]])
                            P.op("vector", lambda e, qt=qt: e.bn_aggr(out=mv[:, qt, :], in_=st6[:, qt, :]), reads=[sbbs[qt]], writes=[sbbs[qt]])

                    def s2():
                        for qt in range(4):
                            P.op("scalar", lambda e, qt=qt: e.activation(out=rstd[:, qt:qt + 1], in_=mv[:, qt, 1:2], func=AF.Sqrt, bias=EPS, scale=1.0),
                                 reads=[sbbs[qt]], writes=[sbbs[qt]])

                    def s3():
                        for qt in range(4):
                            P.op("vector", lambda e, qt=qt: e.reciprocal(out=rstd[:, qt:qt + 1], in_=rstd[:, qt:qt + 1]), reads=[sbbs[qt]], writes=[sbbs[qt]])
                            P.op("vector", lambda e, qt=qt: e.scalar_tensor_tensor(out=nmr[:, qt:qt + 1], in0=mv[:, qt, 0:1], scalar=-1.0,
                                                                                  in1=rstd[:, qt:qt + 1], op0=ALU.mult, op1=ALU.mult),
                                 reads=[sbbs[qt]], writes=[sbbs[qt]])

                    def s4():
                        for qt in range(4):
                            src, srcb = srcs[qt]
                            yn, ynb = yn_ring.next()
                            P.op("scalar", lambda e, qt=qt, yn=yn, src=src: e.activation(out=yn[:], in_=src, func=AF.Identity,
                                                                                        scale=rstd[:, qt:qt + 1], bias=nmr[:, qt:qt + 1]),
                                 reads=[srcb, sbbs[qt]], writes=[ynb])
                            yns.append((yn, ynb))

                    def s5():
                        for qt in range(4):
                            gi = qb * 4 + qt
                            yn, ynb = yns[qt]
                            yg, ygb = yg_ring.next()
                            P.op("gpsimd", lambda e, yn=yn, yg=yg, gi=gi: e.tensor_tensor(out=yg[:], in0=yn[:], in1=Gs[:, gi, :], op=ALU.mult),
                                 reads=[ynb, gsb[gi]], writes=[ygb])
                            ygs.append((yg, ygb))
                    return [s1, s2, s3, s4, s5], ygs

                def gn_b(qb, ygs):
                    qs = 512 * qb
                    yT, yTb = yT_ring.next()
                    for qt in range(4):
                        yg, ygb = ygs[qt]
                        pst, pstb = pst_ring.next()

                        def tr4(e, yg=yg, pst=pst):
                            ins = None
                            for fc in range(4):
                                ins = e.transpose(pst[:, fc * 128:(fc + 1) * 128], yg[:, fc * 128:(fc + 1) * 128], identb[:])
                            return ins
                        P.op("tensor", tr4, reads=[ygb], writes=[pstb])
                        P.op("vector", lambda e, pst=pst, yT=yT, qt=qt: e.tensor_copy(
                            out=yT[:, :, qt * 128:(qt + 1) * 128], in_=pst[:, 0:512].rearrange("p (a b) -> p a b", b=128)),
                            reads=[pstb], writes=[yTb])
                    dstd = yrT_d[h * 512:(h + 1) * 512, qs:qs + 512].rearrange("(fc p) t -> p fc t", p=128)
                    P.dma("gpsimd", lambda e, yT=yT, dstd=dstd: e.dma_start(out=dstd, in_=yT[:]), reads=[yTb], writes=[yrb])

                for qb in range(4):
                    qs = 512 * qb
                    slot = qb % 2
                    nk = len(ktiles)
                    Ms = {}
                    for stp in range(nk + LA):
                        if stp < nk:
                            kt, kp = ktiles[stp]
                            pss, pssb = pss_ring.next()

                            def mms(e, pss=pss, kt=kt, qb=qb):
                                ins = None
                                for dc in range(2):
                                    ins = e.matmul(pss[:], lhsT=KT[:, dc, kt * 128:(kt + 1) * 128], rhs=QT[:, dc, qb * 512:(qb + 1) * 512],
                                                   start=(dc == 0), stop=(dc == 1))
                                return ins
                            P.op("tensor", mms, reads=[ktb[kt], qtb[qb]], writes=[pssb])
                            x0 = qs - kp + XOFF
                            Mv, Mb = M_ring.next()
                            P.op("vector", lambda e, pss=pss, Mv=Mv, x0=x0: e.tensor_tensor(out=Mv[:], in0=pss[:], in1=Th[:, x0:x0 + 512], op=ALU.mult),
                                 reads=[pssb, thb], writes=[Mb])
                            Ms[stp] = (Mv, Mb)
                        ki = stp - LA
                        if ki >= 0:
                            kt, kp = ktiles[ki]
                            Mv, Mb = Ms.pop(ki)

                            def mmy(e, Mv=Mv, kt=kt, ki=ki, nk=nk):
                                ins = None
                                for qt in range(4):
                                    ins = e.matmul(psy[qt][:], lhsT=Mv[:, qt * 128:(qt + 1) * 128], rhs=Vt[:, kt, :],
                                                   start=(ki == 0), stop=(ki == nk - 1))
                                return ins
                            P.op("tensor", mmy, reads=[Mb, vtb[kt]], writes=psyb)
                        if pending[0] is not None:
                            if stp == 1:
                                pending[0] = (pending[0][0],) + gn_stages(*pending[0])
                            if stp in (1, 3, 5, 7, 9):
                                pending[0][1][(stp - 1) // 2]()
                            if stp == 14:
                                gn_b(pending[0][0], pending[0][2])
                                pending[0] = None
                    for qt in range(4):
                        P.op("scalar", lambda e, qt=qt, slot=slot: e.activation(out=ys4[slot][:, qt, :], in_=psy[qt][:], func=AF.Copy),
                             reads=[psyb[qt]], writes=[ys4b[slot][qt]])
                    pending[0] = (qb, slot)
                stages, ygs_last = gn_stages(*pending[0])
                for st_ in stages:
                    st_()
                gn_b(pending[0][0], ygs_last)
                pending[0] = None

        def phase_S(ph):
            pb = [Buf("bank%d" % i, True) for i in range(8)]
            dtv = sb("dtv", [128, 18, 64], F32, ph)
            wdt = sb("wdt", [128, 8, 64], BF16, ph)
            a8 = sb("a8", [128, 2, 18, 4], F32, ph)
            dt8 = sb("dt8", [128, 2, 18, 4], F32, ph)
            eall = sb("eall", [128, 6, 72], F32, ph)
            wall = sb("wall", [128, 2, 72], F32, ph)
            BCT = sb("BCT", [128, 2, 2308], BF16, ph)
            xsT_ring = Ring([sb("xsT%d" % i, [128, 2308], BF16, ph) for i in range(2)], "xsT")
            raw2 = sb("raw2", [128, 2, 2312], F32, ph)
            raw_ring = Ring([raw2[:, 0, :], raw2[:, 1, :]], "raw")
            xw_all = raw2[:].rearrange("p a b -> p (a b)").bitcast(BF16)[:, 0:2 * 18 * 256].rearrange("p (d t q) -> p d t q", d=2, t=18)
            acc_ring = Ring([sb("acc%d" % i, [128, 2308], F32, ph) for i in range(1)], "acc")
            xs_tm = sb("xs_tm", [128, 18, 256], BF16, ph)
            B_tm = sb("B_tm", [128, 18, 128], BF16, ph)
            sz = sb("sz", [128, 16, 256], BF16, ph)
            S_store = sb("S_store", [128, 2, 16, 256], BF16, ph)
            S32 = sb("S32", [128, 2, 256], F32, ph)
            Dec_ring = Ring([sb("Dec%d" % i, [128, 2, 4, 128], F32, ph) for i in range(2)], "Dec")
            nacum = sb("nacum", [128, 18, 8], F32, ph)
            acum = sb("acum", [128, 18, 8], F32, ph)
            dD = sb("dD", [128, 2, 4, 128], BF16, ph)
            dDt = sb("dDt", [128, 2, 128], F32, ph)
            negm = sb("negm", [128, 2, 4, 128], BF16, ph)
            identf = sb("identf", [128, 128], F32, ph)
            cbS = Buf("constS")
            P.dma("sync", lambda e: e.dma_start(out=negm[:].rearrange("p d r i -> p (d r i)"), in_=negm_d[:, :]), writes=[cbS])
            P.dma("sync", lambda e: e.dma_start(out=identf[:], in_=identf_d[:, :]), writes=[cbS])
            M_ring = Ring([sb("Ms%d" % i, [128, 2, 4, 128], BF16, ph) for i in range(3)], "Ms")
            v_ring = Ring([sb("v%d" % i, [128, 2, 4, 64], BF16, ph) for i in range(3)], "v")
            gi_ring = Ring([sb("gi%d" % i, [128, 2, 256], F32, ph) for i in range(2)], "gi")
            t1_ring = Ring([sb("e1_%d" % i, [128, 256], F32, ph) for i in range(2)], "e1")
            t2_ring = Ring([sb("e2_%d" % i, [128, 256], F32, ph) for i in range(2)], "e2")
            yo_ring = Ring([sb("yo%d" % i, [128, 256], BF16, ph) for i in range(2)], "yo")
            junk = sb("junkS", [128, 256], BF16, ph)
            jb = Buf("junkS")
            ss_ring = Ring([sb("ss1_%d" % i, [128, 2], F32, ph) for i in range(2)], "ss1")
            ysT_ring = Ring([sb("ysT%d" % i, [128, 2, NLAT], BF16, ph) for i in range(1)], "ysT")
            tmp64 = sb("tmp64", [128, 64], F32, ph)
            epsc = sb("epsc", [128, 1], F32, ph)
            P.op("gpsimd", lambda e: e.memset(epsc[:], EPS), writes=[Buf()])
            wst_ring = Ring([sb("wstS%d" % i, [128, 8, 256], F32, ph) for i in range(2)], "wstS")
            wbf_ring = Ring([sb("wbfS%d" % i, [128, 8, 256], BF16, ph) for i in range(2)], "wbfS")
            tri = ssdc[:, C_TRI:C_TRI + 256].rearrange("p (d i) -> p d i", d=2)
            Lm = ssdc[:, C_L:C_L + 256].rearrange("p (d i) -> p d i", d=2)
            m01 = ssdc[:, C_M01:C_M01 + 256].rearrange("p (d i) -> p d i", d=2)
            ones = ssdc[:, C_ONES:C_ONES + 128]
            proj_i = [0]

            def next_proj():
                i = proj_i[0]
                proj_i[0] = (i + 1) % 4
                return bank(i), pb[i]

            wdtb = Buf("wdt")
            load_w(wst_ring, wbf_ring, w_in_v, 12288, 64, dst=wdt, dstbuf=wdtb)
            dtb = Buf("dtv")
            for t0_ in range(0, 18, 8):
                n_ = min(8, 18 - t0_)
                psp, pspb = next_proj()

                def mmd(e, t0_=t0_, n_=n_, psp=psp):
                    ins = None
                    for i in range(n_):
                        t = t0_ + i
                        for kc in range(8):
                            ins = e.matmul(psp[:, i * 64:(i + 1) * 64], lhsT=hT[:, kc, t * 128:(t + 1) * 128], rhs=wdt[:, kc, :], start=(kc == 0), stop=(kc == 7))
                    return ins
                P.op("tensor", mmd, reads=[wdtb], writes=[pspb])
                dsl = dtv[:, t0_:t0_ + n_, :]
                P.op("vector", lambda e, psp=psp, dsl=dsl, n_=n_: e.tensor_tensor(
                    out=dsl, in0=psp[:, 0:n_ * 64].rearrange("p (a b) -> p a b", b=64),
                    in1=rowss[:, R_DTB - RS:R_DTB - RS + 64].unsqueeze(1).broadcast_to([128, n_, 64]), op=ALU.add), reads=[pspb], writes=[dtb])
                P.op("scalar", lambda e, dsl=dsl: e.activation(out=dsl, in_=dsl, func=AF.Exp), reads=[dtb], writes=[dtb])
                P.op("scalar", lambda e, dsl=dsl: e.activation(out=dsl, in_=dsl, func=AF.Ln, bias=1.0), reads=[dtb], writes=[dtb])

            gb = Buf("grp")
            szb = [Buf("sz") for _ in range(16)]
            xsb_ = Buf("xs_tm")
            btb = Buf("B_tm")
            bctb = Buf("BCT")
            eb = Buf("eall")
            xwb2 = [Buf("xw_all_f"), Buf("xw_all_b")]
            acb = Buf("acum")
            ddb = Buf("dD")
            s32b = [Buf("S32f"), Buf("S32b")]
            stb = [[Buf("Sst") for _ in range(16)] for _ in range(2)]
            for g in range(8):
                P.op("vector", lambda e, g=g: e.tensor_copy(
                    out=dt8[:], in_=dtv[:].rearrange("p t (d h) -> p d t h", d=2)[:, :, :, g * 4:(g + 1) * 4]), reads=[dtb], writes=[gb])
                P.op("vector", lambda e, g=g: e.tensor_tensor(
                    out=a8[:], in0=dt8[:],
                    in1=Aneg[:].rearrange("p (d h) -> p d h", d=2)[:, :, g * 4:(g + 1) * 4].unsqueeze(2).broadcast_to([128, 2, 18, 4]),
                    op=ALU.mult), reads=[gb], writes=[gb])
                for rv, rb in zip(raw_ring.views, raw_ring.bufs):
                    for (p0, p1) in ((0, 2), (258, 262), (2310, 2312)):
                        P.op("gpsimd", lambda e, rv=rv, p0=p0, p1=p1: e.memset(rv[:, p0:p1], 0.0), writes=[rb])
                for r in range(4):
                    dcol = rowss[:, R_D - RS + g * 4 + r:R_D - RS + g * 4 + r + 1]
                    P.op("gpsimd", lambda e, dcol=dcol: e.tensor_scalar(out=dDt[:, 0, :], in0=identf[:], scalar1=dcol, scalar2=None, op0=ALU.mult),
                         reads=[cbS], writes=[ddb])
                    P.op("gpsimd", lambda e, r=r: e.tensor_copy(out=dD[:, 0, r, :], in_=dDt[:, 0, :]), reads=[ddb], writes=[ddb])
                    P.op("gpsimd", lambda e, r=r: e.tensor_tensor(out=dDt[:, 1, :], in0=dDt[:, 0, :], in1=dD[:, 0, r, :], op=ALU.subtract), reads=[ddb], writes=[ddb])
                    P.op("gpsimd", lambda e, r=r: e.tensor_copy(out=dD[:, 1, r, :], in_=dDt[:, 1, :]), reads=[ddb], writes=[ddb])
                chunks = [("x", 8192 + g * 256, 0, g * 2), ("x", 8192 + g * 256 + 128, 1, g * 2 + 1),
                          ("B", 8192 + 2048 + g * 128, 0, 16 + g), ("C", 8192 + 3072 + g * 128, 0, 24 + g)]

                def proj(k):
                    (kind, c0, j, cch) = chunks[k]
                    wv, wb = load_w(wst_ring, wbf_ring, w_in_v, c0, 128, cast_eng="scalar")
                    raw, rawb = raw_ring.next()
                    blocks = ([] if kind == "C" else [(0, 256, 2)]) + [(256 + q * 512, 512, 262 + q * 512) for q in range(4)]
                    for (t0, n, r0) in blocks:
                        psp, pspb = next_proj()

                        def mmx(e, wv=wv, t0=t0, n=n, psp=psp):
                            ins = None
                            for kc in range(8):
                                ins = e.matmul(psp[:, 0:n], lhsT=wv[:, kc, 0:128], rhs=hT[:, kc, t0:t0 + n], start=(kc == 0), stop=(kc == 7))
                            return ins
                        P.op("tensor", mmx, reads=[wb], writes=[pspb])
                        P.op("scalar", lambda e, psp=psp, raw=raw, n=n, r0=r0: e.activation(out=raw[:, r0:r0 + n], in_=psp[:, 0:n], func=AF.Copy),
                             reads=[pspb], writes=[rawb])
                    return raw, rawb

                def conv_tr(k, raw, rawb):
                    (kind, c0, j, cch) = chunks[k]
                    acc, accb = acc_ring.next()
                    for kk in range(5):
                        wk = vecs[:, V_CONVW + kk * 32 + cch:V_CONVW + kk * 32 + cch + 1]
                        if kk == 0:
                            P.op("vector", lambda e, acc=acc, raw=raw, wk=wk: e.tensor_scalar(out=acc[:], in0=raw[:, 0:2308], scalar1=wk, scalar2=None, op0=ALU.mult),
                                 reads=[rawb], writes=[accb])
                        else:
                            P.op("vector", lambda e, acc=acc, raw=raw, wk=wk, kk=kk: e.scalar_tensor_tensor(
                                out=acc[:], in0=raw[:, kk:kk + 2308], scalar=wk, in1=acc[:], op0=ALU.mult, op1=ALU.add),
                                reads=[rawb], writes=[accb])
                    bcol = vecs[:, V_CONVB + cch:V_CONVB + cch + 1]
                    if kind == "x":
                        dst, dstb = xsT_ring.next()
                        dsta = dst[:]
                    else:
                        dsta, dstb = BCT[:, 0 if kind == "B" else 1, :], bctb
                    P.op("scalar", lambda e, acc=acc, dsta=dsta, bcol=bcol: e.activation(out=dsta, in_=acc[:], func=AF.Silu, bias=bcol),
                         reads=[accb], writes=[dstb])
                    if kind == "C":
                        return
                    for t0 in range(0, 18, 6):
                        pst = bank(7).bitcast(BF16)

                        def tr6(e, dsta=dsta, t0=t0, pst=pst):
                            ins = None
                            for i in range(6):
                                t = t0 + i
                                u0 = t * 128 if t < 2 else 260 + (t - 2) * 128
                                ins = e.transpose(pst[:, i * 128:(i + 1) * 128], dsta[:, u0:u0 + 128], identb[:])
                            return ins
                        P.op("tensor", tr6, reads=[dstb], writes=[pb[7]])
                        if kind == "x":
                            o = xs_tm[:, t0:t0 + 6, j * 128:(j + 1) * 128]
                            ob_ = xsb_
                        else:
                            o = B_tm[:, t0:t0 + 6, :]
                            ob_ = btb
                        P.op("scalar", lambda e, o=o, pst=pst: e.activation(out=o, in_=pst[:, 0:768].rearrange("p (a b) -> p a b", b=128), func=AF.Copy),
                             reads=[pb[7]], writes=[ob_])

                def z_proj():
                    wv, wb = load_w(wst_ring, wbf_ring, w_in_v, 6144 + g * 256, 256, cast_eng="scalar")
                    for c in range(16):
                        psp, pspb = next_proj()

                        def mmz(e, wv=wv, c=c, psp=psp):
                            ins = None
                            for kc in range(8):
                                ins = e.matmul(psp[:, 0:256], lhsT=hT[:, kc, (2 + c) * 128:(3 + c) * 128], rhs=wv[:, kc, 0:256], start=(kc == 0), stop=(kc == 7))
                            return ins
                        P.op("tensor", mmz, reads=[wb], writes=[pspb])
                        P.op("scalar", lambda e, psp=psp, c=c: e.activation(out=sz[:, c, :], in_=psp[:, 0:256], func=AF.Silu),
                             reads=[pspb], writes=[szb[c]])

                nxt = proj(0)
                for k in range(4):
                    cur = nxt
                    nxt = proj(k + 1) if k < 3 else None
                    if k == 0:
                        z_proj()
                    conv_tr(k, *cur)
                if g == 0 and ck("s1"):
                    return


                def mmb(e):
                    ins = None
                    for d in range(2):
                        rhs = a8[:, d, :, :].rearrange("p t r -> p (t r)")
                        e.matmul(bank(0)[:, d * 72:(d + 1) * 72], lhsT=Lm[:, d, :], rhs=rhs, start=True, stop=True)
                        e.matmul(bank(0)[:, 144 + d * 72:144 + (d + 1) * 72], lhsT=ones, rhs=rhs, start=True, stop=True)
                        ins = e.matmul(bank(0)[:, 288 + d * 72:288 + (d + 1) * 72], lhsT=tri[:, d, :], rhs=rhs, start=True, stop=True)
                    return ins
                P.op("tensor", mmb, reads=[gb], writes=[pb[0]])
                P.op("scalar", lambda e: e.activation(out=eall[:].rearrange("p a b -> p (a b)"), in_=bank(0)[:, 0:432], func=AF.Exp), reads=[pb[0]], writes=[eb])
                P.op("vector", lambda e: e.tensor_tensor(out=wall[:], in0=eall[:, 0:2, :], in1=dt8[:].rearrange("p d t r -> p d (t r)"), op=ALU.mult),
                     reads=[eb, gb], writes=[eb])
                P.op("vector", lambda e: e.tensor_copy(out=acum[:].rearrange("p t (d r) -> p d t r", d=2),
                                                       in_=bank(0)[:, 288:432].rearrange("p (d t r) -> p d t r", d=2, t=18)), reads=[pb[0]], writes=[acb])
                P.op("vector", lambda e: e.tensor_scalar(out=nacum[:], in0=acum[:], scalar1=-1.0, scalar2=None, op0=ALU.mult), reads=[acb], writes=[acb])
                P._deps("gpsimd", (), raw_ring.bufs)
                for d in (0, 1):
                    P.op("vector" if d == 1 else "gpsimd", lambda e, d=d: e.tensor_tensor(
                        out=xw_all[:, d, :, :].rearrange("p t (r q) -> p t r q", q=64), in0=xs_tm[:].rearrange("p t (r q) -> p t r q", q=64),
                        in1=wall[:, d, :].rearrange("p (t r) -> p t r", r=4).unsqueeze(3).broadcast_to([128, 18, 4, 64]), op=ALU.mult),
                        reads=[eb, xsb_], writes=[xwb2[d]] + (raw_ring.bufs if d == 1 else []))
                etot = eall[:, 2:4, :].rearrange("p d (t r) -> p d t r", r=4)
                ecum = eall[:, 4:6, :].rearrange("p d (t r) -> p d t r", r=4)

                P.op("gpsimd", lambda e: e.memset(S32[:], 0.0), writes=s32b)
                fwd_tiles = list(range(0, 17))
                bwd_tiles = [1, 0] + list(range(17, 2, -1))
                step = [0]

                def chain_step(d, t, slot):
                    bi = 4 + (step[0] % 3)
                    step[0] += 1
                    P.op("tensor", lambda e, d=d, t=t, bi=bi: e.matmul(bank(bi)[:, 0:256], lhsT=B_tm[:, t, :], rhs=xw_all[:, d, t, :], start=True, stop=True),
                         reads=[xwb2[d], btb], writes=[pb[bi]])
                    P.op("vector", lambda e, d=d, t=t: e.tensor_tensor(
                        out=S32[:, d, :].rearrange("p (r q) -> p r q", q=64), in0=S32[:, d, :].rearrange("p (r q) -> p r q", q=64),
                        in1=etot[:, d, t, :].unsqueeze(2).broadcast_to([128, 4, 64]), op=ALU.mult), reads=[eb], writes=[s32b[d]])
                    P.op("vector", lambda e, d=d, bi=bi: e.tensor_tensor(out=S32[:, d, :], in0=S32[:, d, :], in1=bank(bi)[:, 0:256], op=ALU.add),
                         reads=[pb[bi]], writes=[s32b[d]])
                    if slot is not None:
                        P.op("scalar", lambda e, d=d, slot=slot: e.activation(out=S_store[:, d, slot, :], in_=S32[:, d, :], func=AF.Copy),
                             reads=[s32b[d]], writes=[stb[d][slot]])
                for s in range(17):
                    tf = fwd_tiles[s]
                    chain_step(0, tf, tf - 1 if tf >= 1 else None)
                    tb_ = bwd_tiles[s]
                    if tb_ == 1:
                        slot = None
                    elif tb_ == 0:
                        slot = 15
                    else:
                        slot = tb_ - 3
                    chain_step(1, tb_, slot)
                if g == 0 and ck("s2"):
                    return

                ysT, ysTb = ysT_ring.next()

                def sbk_of(c):
                    return (0, 1) if c % 2 == 0 else (5, 6)

                def sc_of(c):
                    return bank(2)[:, (c % 2) * 128:(c % 2) * 128 + 128]

                def prepA(c):
                    t = 2 + c
                    u0 = 260 + c * 128
                    sbk = sbk_of(c)

                    def mmseg(e, t=t, sbk=sbk):
                        ins = None
                        for d in range(2):
                            e.matmul(bank(sbk[d])[:], lhsT=identb[:], rhs=negm[:, d, :, :].rearrange("p r i -> p (r i)"), start=True, stop=False,
                                     skip_group_check=True)
                            for r in range(4):
                                ins = e.matmul(bank(sbk[d])[:, r * 128:(r + 1) * 128], lhsT=a8[:, d, t, r:r + 1].broadcast_to([128, 128]),
                                               rhs=tri[:, d, :], start=False, stop=True, skip_group_check=True)
                        return ins
                    P.op("tensor", mmseg, reads=[gb, cbS], writes=[pb[sbk[0]], pb[sbk[1]]])
                    P.op("tensor", lambda e, u0=u0, c=c: e.matmul(sc_of(c), lhsT=BCT[:, 0, u0:u0 + 128], rhs=BCT[:, 1, u0:u0 + 128], start=True, stop=True),
                         reads=[bctb], writes=[pb[2]])

                def prepB_act(c):
                    t = 2 + c
                    sbk = sbk_of(c)
                    Dec, Decb = Dec_ring.next()
                    for d in range(2):
                        for r in range(4):
                            q = d * 4 + r
                            P.op("scalar", lambda e, Dec=Dec, d=d, r=r, q=q, t=t, sbk=sbk: e.activation(
                                out=Dec[:, d, r, :], in_=bank(sbk[d])[:, r * 128:(r + 1) * 128], func=AF.Exp, bias=nacum[:, t, q:q + 1], scale=1.0),
                                reads=[pb[sbk[d]], acb], writes=[Decb])
                    return Dec, Decb

                def prepB_mv(c, Dec, Decb):
                    t = 2 + c
                    Mv, Mb = M_ring.next()
                    P.op("vector", lambda e, Mv=Mv, Dec=Dec, c=c: e.tensor_tensor(
                        out=Mv[:].rearrange("p d r i -> p (d r) i"), in0=sc_of(c).unsqueeze(1).broadcast_to([128, 8, 128]),
                        in1=Dec[:].rearrange("p d r i -> p (d r) i"), op=ALU.mult), reads=[pb[2], Decb], writes=[Mb])
                    vv, vb = v_ring.next()
                    P.op("gpsimd", lambda e, vv=vv, t=t: e.tensor_tensor(
                        out=vv[:], in0=xs_tm[:, t, :].rearrange("p (r q) -> p r q", q=64).unsqueeze(1).broadcast_to([128, 2, 4, 64]),
                        in1=dt8[:, :, t, :].unsqueeze(3).broadcast_to([128, 2, 4, 64]), op=ALU.mult), reads=[gb, xsb_], writes=[vb])
                    return (Mv, Mb, vv, vb)

                def fin_1(c, Mv, Mb, vv, vb):
                    t = 2 + c
                    u0 = 260 + c * 128

                    def mmy(e, Mv=Mv, vv=vv):
                        ins = None
                        for r in range(4):
                            o = bank(3)[:, r * 64:(r + 1) * 64]
                            e.matmul(o, lhsT=Mv[:, 0, r, :], rhs=vv[:, 0, r, :], start=True, stop=False)
                            e.matmul(o, lhsT=Mv[:, 1, r, :], rhs=vv[:, 1, r, :], start=False, stop=False)
                            e.matmul(o, lhsT=dD[:, 0, r, :], rhs=xs_tm[:, t, r * 64:(r + 1) * 64], start=False, stop=False)
                            ins = e.matmul(o, lhsT=dD[:, 1, r, :], rhs=xs_tm[:, t, r * 64:(r + 1) * 64], start=False, stop=True)
                        return ins
                    P.op("tensor", mmy, reads=[Mb, vb, ddb, xsb_], writes=[pb[3]])
                    ib = 4

                    def mmi(e, c=c, u0=u0, ib=ib):
                        ins = None
                        for d in range(2):
                            ins = e.matmul(bank(ib)[:, d * 256:(d + 1) * 256], lhsT=BCT[:, 1, u0:u0 + 128], rhs=S_store[:, d, c, :], start=True, stop=True)
                        return ins
                    P.op("tensor", mmi, reads=[bctb, stb[0][c], stb[1][c]], writes=[pb[ib]])
                    gi, gib = gi_ring.next()
                    P.op("vector", lambda e, gi=gi, ib=ib, t=t: e.tensor_tensor(
                        out=gi[:].rearrange("p d (r q) -> p d r q", q=64), in0=bank(ib)[:].rearrange("p (d r q) -> p d r q", d=2, q=64),
                        in1=ecum[:, :, t, :].unsqueeze(3).broadcast_to([128, 2, 4, 64]), op=ALU.mult), reads=[pb[ib], eb], writes=[gib])
                    t1, t1b = t1_ring.next()
                    P.op("vector", lambda e, t1=t1, gi=gi: e.tensor_tensor(out=t1[:], in0=gi[:, 0, :], in1=gi[:, 1, :], op=ALU.add), reads=[gib], writes=[t1b])
                    P.op("vector", lambda e, t1=t1: e.tensor_tensor(out=t1[:], in0=bank(3)[:, 0:256], in1=t1[:], op=ALU.add), reads=[pb[3]], writes=[t1b])
                    t2, t2b = t2_ring.next()
                    P.op("vector", lambda e, t1=t1, t2=t2, c=c: e.tensor_tensor(out=t2[:], in0=t1[:], in1=sz[:, c, :], op=ALU.mult),
                         reads=[t1b, szb[c]], writes=[t2b])
                    ss1, sb1 = ss_ring.next()
                    P.op("scalar", lambda e, t2=t2, ss1=ss1: e.activation(out=junk[:], in_=t2[:], func=AF.Square, accum_out=ss1[:, 0:1]), reads=[t2b], writes=[jb, sb1])
                    P.op("scalar", lambda e, ss1=ss1: e.activation(out=ss1[:, 1:2], in_=ss1[:, 0:1], func=AF.Ln, scale=1.0 / 256.0, bias=epsc[:, 0:1]), reads=[sb1], writes=[sb1])
                    P.op("scalar", lambda e, ss1=ss1: e.activation(out=ss1[:, 1:2], in_=ss1[:, 1:2], func=AF.Exp, scale=-0.5), reads=[sb1], writes=[sb1])
                    return (t2, t2b, ss1, sb1)

                def fin_2(c, t2, t2b, ss1, sb1):
                    yo, yob = yo_ring.next()
                    P.op("vector", lambda e, yo=yo, t2=t2, ss1=ss1: e.tensor_scalar(out=yo[:], in0=t2[:], scalar1=ss1[:, 1:2], scalar2=None, op0=ALU.mult),
                         reads=[t2b, sb1], writes=[yob])
                    return (yo, yob)

                def fin_b(c, yo, yob):
                    pst = bank(7).bitcast(BF16)

                    def tr2(e, yo=yo, pst=pst):
                        e.transpose(pst[:, 0:128], yo[:, 0:128], identb[:])
                        return e.transpose(pst[:, 128:256], yo[:, 128:256], identb[:])
                    P.op("tensor", tr2, reads=[yob], writes=[pb[7]])
                    P.op("vector", lambda e, ysT=ysT, c=c, pst=pst: e.tensor_copy(
                        out=ysT[:, :, c * 128:(c + 1) * 128], in_=pst[:, 0:256].rearrange("p (a b) -> p a b", b=128)), reads=[pb[7]], writes=[ysTb])

                hnds = {}
                prepA(0)
                prepA(1)
                hnds[0] = prepB_mv(0, *prepB_act(0))
                prepA(2)
                hnds[1] = prepB_mv(1, *prepB_act(1))
                prev = None
                for c in range(16):
                    if c + 3 < 16:
                        prepA(c + 3)
                    dec = prepB_act(c + 2) if c + 2 < 16 else None
                    f1 = fin_1(c, *hnds.pop(c))
                    if dec is not None:
                        hnds[c + 2] = prepB_mv(c + 2, *dec)
                    cur = fin_2(c, *f1)
                    if prev is not None:
                        fin_b(c - 1, *prev)
                    prev = cur
                fin_b(15, *prev)
                dstd = ysT_d[g * 256:(g + 1) * 256, :].rearrange("(fc p) t -> p fc t", p=128)
                P.dma("gpsimd", lambda e, ysT=ysT, dstd=dstd: e.dma_start(out=dstd, in_=ysT[:]), reads=[ysTb], writes=[ysb])
                if g == 0 and ck("s4"):
                    return

        def phase_F(ph):
            pb = [Buf("bank%d" % i, True) for i in range(8)]
            mT = sb("mT", [128, 8, NLAT], BF16, ph)
            wA = sb("wA", [128, 16, 1024], BF16, ph)
            wG = sb("wG", [128, 8, 1024], BF16, ph)
            yb_ring = Ring([sb("yblk%d" % i, [128, 16, 512], BF16, ph) for i in range(2)], "yblk")
            sg_ring = Ring([sb("sg%d" % i, [128, 512], F32, ph) for i in range(2)], "sg")
            tm_ring = Ring([sb("tm%d" % i, [128, 512], F32, ph) for i in range(2)], "tm")
            xt_ring = Ring([sb("xF%d" % i, [128, 1024], F32, ph) for i in range(3)], "xF")
            ot_ring = Ring([sb("oF%d" % i, [128, 1024], F32, ph) for i in range(2)], "oF")
            junk = sb("junkF", [128, 512], BF16, ph)
            jb = Buf("junkF")
            ssf = sb("ssf", [128, 4], F32, ph)
            wst_ring = Ring([sb("wstF%d" % i, [128, 8, 256], F32, ph) for i in range(2)], "wstF")
            mTb = [Buf("mT%d" % i) for i in range(4)]
            wAb = [Buf("wA%d" % i) for i in range(2)]
            wGb = [Buf("wG%d" % i) for i in range(2)]
            for br in range(2):
                src_o = (w_ret_o_d if br == 0 else w_ssd_o_d).rearrange("(kc p) n -> p kc n", p=128)
                scol = V_GNW if br == 0 else V_SNW
                for cu in range(2):
                    for kq in range(4):
                        alt = (br == 0 and cu == 0)
                        load_w(wst_ring, None, src_o, cu * 512, 512, dst=wA[:, kq * 4:(kq + 1) * 4, cu * 512:(cu + 1) * 512], dstbuf=wAb[cu],
                               scale_col=scol, kc0=kq * 4, nkc=4, cast_eng="scalar" if alt else "gpsimd")
                    for kq in range(2):
                        load_w(wst_ring, None, w_in_v, 12352 + br * 1024 + cu * 512, 512, dst=wG[:, kq * 4:(kq + 1) * 4, cu * 512:(cu + 1) * 512],
                               dstbuf=wGb[cu], kc0=kq * 4, nkc=4, cast_eng="scalar", q="gpsimd")
                scr = yrT_d if br == 0 else ysT_d
                scrb = yrb if br == 0 else ysb
                for tb in range(4):
                    yblk, yblkb = yb_ring.next()
                    P.dma("sync", lambda e, yblk=yblk, tb=tb, scr=scr: e.dma_start(
                        out=yblk[:], in_=scr[:, tb * 512:(tb + 1) * 512].rearrange("(kc p) t -> p kc t", p=128)), reads=[scrb], writes=[yblkb])
                    for fo in range(8):
                        pa, pab = bank(fo % 2), pb[fo % 2]
                        pg, pgb = bank(2 + fo % 2), pb[2 + fo % 2]

                        def mma(e, yblk=yblk, fo=fo, pa=pa):
                            ins = None
                            for kc in range(16):
                                ins = e.matmul(pa[:], lhsT=wA[:, kc, fo * 128:(fo + 1) * 128], rhs=yblk[:, kc, :], start=(kc == 0), stop=(kc == 15))
                            return ins
                        P.op("tensor", mma, reads=[yblkb, wAb[fo // 4]], writes=[pab])

                        def mmg(e, fo=fo, tb=tb, pg=pg):
                            ins = None
                            for kc in range(8):
                                ins = e.matmul(pg[:], lhsT=wG[:, kc, fo * 128:(fo + 1) * 128], rhs=hT[:, kc, 256 + tb * 512:256 + (tb + 1) * 512],
                                               start=(kc == 0), stop=(kc == 7))
                            return ins
                        P.op("tensor", mmg, reads=[wGb[fo // 4]], writes=[pgb])
                        sg, sgb = sg_ring.next()
                        P.op("scalar", lambda e, sg=sg, pg=pg: e.activation(out=sg[:], in_=pg[:], func=AF.Sigmoid), reads=[pgb], writes=[sgb])
                        mdst = mT[:, fo, tb * 512:(tb + 1) * 512]
                        if br == 0:
                            P.op("vector", lambda e, mdst=mdst, pa=pa, sg=sg: e.tensor_tensor(out=mdst, in0=pa[:], in1=sg[:], op=ALU.mult),
                                 reads=[pab, sgb], writes=[mTb[tb]])
                        else:
                            tm, tmb = tm_ring.next()
                            P.op("vector", lambda e, tm=tm, pa=pa, sg=sg: e.tensor_tensor(out=tm[:], in0=pa[:], in1=sg[:], op=ALU.mult),
                                 reads=[pab, sgb], writes=[tmb])
                            P.op("gpsimd", lambda e, mdst=mdst, tm=tm: e.tensor_tensor(out=mdst, in0=mdst, in1=tm[:], op=ALU.add),
                                 reads=[tmb], writes=[mTb[tb]])
                if br == 0 and ck("f1"):
                    return
            src_o = w_out_d.rearrange("(kc p) n -> p kc n", p=128)
            for cu in range(2):
                for kq in range(2):
                    load_w(wst_ring, None, src_o, cu * 512, 512, dst=wA[:, kq * 4:(kq + 1) * 4, cu * 512:(cu + 1) * 512], dstbuf=wAb[cu],
                           kc0=kq * 4, nkc=4, cast_eng="scalar")
            outb = Buf("out")
            ssf3 = [sb("ssf3_%d" % i, [128, 4], F32, ph) for i in range(3)]

            def wo_a(t):
                po = bank(4 + 2 * (t % 2), 2)
                pob = [pb[4 + 2 * (t % 2)], pb[5 + 2 * (t % 2)]]
                ssf_ = ssf3[t % 3]

                def mmo(e, t=t, po=po):
                    ins = None
                    for half in range(2):
                        for kc in range(8):
                            ins = e.matmul(po[:, half * 512:(half + 1) * 512], lhsT=mT[:, kc, t * 128:(t + 1) * 128], rhs=wA[:, kc, half * 512:(half + 1) * 512],
                                           start=(kc == 0), stop=(kc == 7))
                    return ins
                P.op("tensor", mmo, reads=wAb + [mTb[t // 4]], writes=pob)
                return (po, pob, ssf_)

            def wo_sq(t, po, pob, ssf_):
                sfb = Buf("ssf")
                for half in range(2):
                    P.op("scalar", lambda e, po=po, half=half, ssf_=ssf_: e.activation(out=junk[:], in_=po[:, half * 512:(half + 1) * 512], func=AF.Square,
                                                                                       accum_out=ssf_[:, half:half + 1]), reads=[pob[half]], writes=[jb, sfb])
                xt, xtb = xt_ring.next()
                P.dma("sync", lambda e, xt=xt, t=t: e.dma_start(out=xt[:], in_=x_d[t * 128:(t + 1) * 128, :]), writes=[xtb])
                return (po, pob, ssf_, sfb, xt, xtb)

            def wo_b(t, po, pob, ssf_, sfb, xt, xtb):
                P.op("vector", lambda e: e.tensor_tensor(out=ssf_[:, 2:3], in0=ssf_[:, 0:1], in1=ssf_[:, 1:2], op=ALU.add), reads=[sfb], writes=[sfb])
                P.op("scalar", lambda e: e.activation(out=ssf_[:, 3:4], in_=ssf_[:, 2:3], func=AF.Sqrt, scale=1.0 / 1024.0, bias=EPS), reads=[sfb], writes=[sfb])
                P.op("vector", lambda e: e.reciprocal(out=ssf_[:, 3:4], in_=ssf_[:, 3:4]), reads=[sfb], writes=[sfb])
                ot, otb = ot_ring.next()
                P.op("vector", lambda e, ot=ot: e.scalar_tensor_tensor(out=ot[:], in0=po[:], scalar=ssf_[:, 3:4], in1=Gt[:], op0=ALU.mult, op1=ALU.mult),
                     reads=pob + [sfb], writes=[otb])
                P.op("gpsimd", lambda e, ot=ot: e.tensor_tensor(out=ot[:], in0=ot[:], in1=xt[:], op=ALU.add), reads=[xtb], writes=[otb])
                P.dma("gpsimd", lambda e, ot=ot: e.dma_start(out=out_d[t * 128:(t + 1) * 128, :], in_=ot[:]), reads=[otb], writes=[outb])

            pend = wo_sq(0, *wo_a(0))
            for t in range(16):
                mm_next = wo_a(t + 1) if t < 15 else None
                wo_b(t, *pend)
                pend = wo_sq(t + 1, *mm_next) if mm_next is not None else None
            P.barrier()
            if dbg_spec is not None and stop_after == 3:
                dump(mT[:, 0, 0:512], 512)
                dump(mT[:, 5, 1024:1536], 512)

        yrb = Buf("yrT_d")
        ysb = Buf("ysT_d")
        if stop_after is None or stop_after in (1, 3, 5):
            with ExitStack() as ph:
                phase_R(ph)
            P.barrier()
        if stop_after is None or stop_after in (2, 3, 5, 6):
            with ExitStack() as ph:
                phase_S(ph)
            P.barrier()
        if stop_after is None or stop_after in (3, 4, 6):
            with ExitStack() as ph:
                phase_F(ph)
            P.barrier()

        if dbg_spec is not None and stop_after == 1:
            with nc.sbuf_tensor("dbl", [128, 2048], BF16) as dbl:
                lb = Buf()
                for r in range(2):
                    P.dma("sync", lambda e, r=r: e.dma_start(out=dbl[:], in_=yrT_d[r * 1024:r * 1024 + 128, :]), writes=[lb])
                    dump(dbl[:], 2048, reads=[lb])

        if dbg_spec is not None and stop_after == 2:
            with nc.sbuf_tensor("dbl2", [128, 2048], BF16) as dbl:
                lb = Buf()
                for r in range(2):
                    P.dma("sync", lambda e, r=r: e.dma_start(out=dbl[:], in_=ysT_d[r * 128:r * 128 + 128, :]), writes=[lb])
                    dump(dbl[:], 2048, reads=[lb])

        if dbg_spec is not None and stop_after == 3:
            with nc.sbuf_tensor("dbl3", [128, 512], BF16) as dbl:
                lb = Buf()
                for r in range(4):
                    P.dma("sync", lambda e, r=r: e.dma_start(out=dbl[:], in_=yrT_d[r * 512:r * 512 + 128, 512:1024]), writes=[lb])
                    dump(dbl[:], 512, reads=[lb])
                for r in range(8):
                    P.dma("sync", lambda e, r=r: e.dma_start(out=dbl[:], in_=ysT_d[r * 256 + 128:r * 256 + 256, 512:1024]), writes=[lb])
                    dump(dbl[:], 512, reads=[lb])

        if stop_after is not None and stop_after not in (3, 4, 6):
            with nc.sbuf_tensor("zt", [128, 1024], F32) as zt:
                zb = Buf()
                P.op("vector", lambda e: e.memset(zt[:], 0.0), writes=[zb])
                ob = Buf()
                for t in range(16):
                    P.dma("sync", lambda e, t=t: e.dma_start(out=out_d[t * 128:(t + 1) * 128, :], in_=zt[:]), reads=[zb], writes=[ob])
                P.barrier()
            return nc

        P.barrier()
    return nc


def host_constants():
    bf = ml_dtypes.bfloat16
    identb = np.eye(128, dtype=np.float32).astype(bf)
    swap = np.zeros((128, 128), np.float32)
    for m in range(64):
        swap[m + 64, m] = 1.0
        swap[m, m + 64] = 1.0
    swapb = swap.astype(bf)
    inv_freq = (10000.0 ** (-np.arange(64, dtype=np.float32) / 64.0)).astype(np.float32)
    fr = np.concatenate([inv_freq, inv_freq])
    sign = np.concatenate([-np.ones(64), np.ones(64)]).astype(np.float32)
    rows = np.arange(32, dtype=np.float32)
    cols = np.arange(64, dtype=np.float32)
    a0 = (rows[None, :] * fr[:, None]).astype(np.float32)
    a1 = (cols[None, :] * fr[:, None]).astype(np.float32)
    ropec = np.concatenate([np.cos(a0), np.cos(a1), np.sin(a0) * sign[:, None], np.sin(a1) * sign[:, None]], axis=1).astype(np.float32)
    j = np.arange(128, dtype=np.float32)[:, None]
    xx = np.arange(STRIP, dtype=np.float32)[None, :]
    dlt = (xx - XOFF - j).astype(np.float32)
    k = np.arange(128)[:, None]
    i = np.arange(128)[None, :]
    tri = np.stack([(k <= i), (k >= i)], axis=1).astype(np.float32).reshape(128, 256)
    L = np.stack([(k > i), (k < i)], axis=1).astype(np.float32).reshape(128, 256)
    m01 = np.stack([(i >= k), (i < k)], axis=1).astype(np.float32).reshape(128, 256)
    ones = np.ones((128, 128), np.float32)
    ssdc = np.concatenate([tri, L, m01, ones], axis=1).astype(np.float32)
    NEG = -30000.0
    negm = np.stack([np.where(i >= k, 0.0, NEG), np.where(i < k, 0.0, NEG)], axis=1)
    negm = np.repeat(negm[:, :, None, :], 4, axis=2).reshape(128, 1024).astype(np.float32).astype(bf)
    sel = np.zeros((8, 8, 128), np.float32)
    for q in range(8):
        sel[q, q, :] = 1.0
    return dict(identb=identb, swapb=swapb, ropec=ropec, dlt=dlt, ssdc=ssdc, negm=negm, self=sel.reshape(8, 1024),
                identf=np.eye(128, dtype=np.float32))


def host_inputs(inputs):
    f = np.float32
    col = lambda v: np.ascontiguousarray(np.asarray(v, f).reshape(-1, 128).T)
    conv_w = np.asarray(inputs["ssd_conv_w"][0], f)
    vecs = np.concatenate([
        col(inputs["norm_pre_w"][0]), col(inputs["b_mod"][0]),
        np.concatenate([col(conv_w[kk]) for kk in range(5)], axis=1),
        col(inputs["ssd_conv_b"][0]), col(inputs["ret_gn_w"][0]), col(inputs["ssd_norm_w"][0])], axis=1).astype(f)
    assert vecs.shape == (128, NV)
    rows = np.concatenate([
        np.asarray(inputs["norm_post_w"][0], f), np.asarray(inputs["b_mod"][0], f)[2048:3072],
        np.asarray(inputs["ssd_D"][0], f), np.asarray(inputs["ssd_dt_bias"][0], f).reshape(-1),
        np.asarray(inputs["ssd_a_log"][0], f).reshape(-1), np.asarray(inputs["ret_decay"][0], f).reshape(-1)])[None, :].astype(f)
    assert rows.shape == (1, NR)
    shared = dict(
        w_mod=np.ascontiguousarray(inputs["w_mod"][0], f), w_in=np.ascontiguousarray(inputs["w_in"][0], f),
        w_ret_o=np.ascontiguousarray(inputs["w_ret_o"][0], f), w_ssd_o=np.ascontiguousarray(inputs["w_ssd_o"][0], f),
        w_out=np.ascontiguousarray(inputs["w_out"][0], f), vecs=vecs, rows=rows)
    shared.update(host_constants())
    maps = []
    for b in range(8):
        cc = np.stack([np.asarray(inputs["c"][b], f), np.asarray(inputs["c_ctx"], f)], axis=1)
        cct = np.ascontiguousarray(cc.reshape(8, 128, 2).transpose(1, 0, 2).reshape(128, 16))
        m = dict(shared)
        m.update(x=np.ascontiguousarray(inputs["x"][b], f), ctx=np.ascontiguousarray(inputs["ctx"][b], f), cct=cct)
        maps.append(m)
    return maps


def kernel(**inputs):
    maps = host_inputs(inputs)
    nc = build_program()
    res = run_bass_kernel_spmd(nc, maps, core_ids=list(range(8)))
    return np.stack([np.asarray(r["out"], np.float32) for r in res.results], axis=0)
```

```python
import numpy as np
import ml_dtypes
from contextlib import ExitStack
import concourse.bass as bass
import concourse.mybir as mybir
from concourse.bass_utils import run_bass_kernel_spmd

F32 = mybir.dt.float32
BF16 = mybir.dt.bfloat16
AF = mybir.ActivationFunctionType
ALU = mybir.AluOpType

ENGS = ("tensor", "vector", "scalar", "gpsimd", "sync")
EPS = 1e-6
EVAC_ACT = ()
NTOK = 2304
NCTX = 256
NLAT = 2048
XOFF = 2176
STRIP = 4480

V_NPW, V_BMOD, V_CONVW, V_CONVB, V_GNW, V_SNW, NV = 0, 8, 32, 192, 224, 240, 256
R_NPOST, R_BG, R_D, R_DTB, R_ALOG, R_RDEC, NR = 0, 1024, 2048, 2080, 2144, 2208, 2216
RS = 2048
C_TRI, C_L, C_M01, C_ONES, NC_SSD = 0, 256, 512, 768, 896


class Buf:
    __slots__ = ("name", "w", "r", "excl")

    def __init__(self, name="", excl=False):
        self.name = name
        self.w = None
        self.r = {}
        self.excl = excl


class Prog:
    def __init__(self, nc, st, n_dma=12, queues=("sync", "gpsimd")):
        self.nc = nc
        self.h = {e: getattr(nc, e) for e in ENGS}
        self.cnt = {e: 0 for e in ENGS}
        self.seen = {e: {} for e in ENGS}
        self.sems = {("e", e): st.enter_context(nc.semaphore("e_" + e)) for e in ENGS}
        self.n_dma = n_dma
        self.dma_tot = {}
        self.dma_rr = {q: 0 for q in queues}
        for q in queues:
            for i in range(n_dma):
                self.sems[("d", q, i)] = st.enter_context(nc.semaphore("d_%s_%d" % (q, i)))
        self.nwaits = 0

    def _deps(self, eng, reads, writes):
        deps = {}
        own = ("e", eng)
        for b in reads:
            if b.w is not None and b.w[1] > deps.get(b.w[0], 0):
                deps[b.w[0]] = b.w[1]
            if b.excl:
                for k, v in b.r.items():
                    if k != own and v > deps.get(k, 0):
                        deps[k] = v
        for b in writes:
            if b.w is not None and b.w[1] > deps.get(b.w[0], 0):
                deps[b.w[0]] = b.w[1]
            for k, v in b.r.items():
                if v > deps.get(k, 0):
                    deps[k] = v
        seen = self.seen[eng]
        for k, v in deps.items():
            if eng == "tensor" and k == ("e", "tensor"):
                continue
            if seen.get(k, 0) < v:
                seen[k] = v
                self.h[eng].wait_ge(self.sems[k], v)
                self.nwaits += 1

    @staticmethod
    def _mark(key, val, reads, writes):
        for b in reads:
            if b.r.get(key, 0) < val:
                b.r[key] = val
        for b in writes:
            b.w = (key, val)
            b.r = {}

    def op(self, eng, fn, reads=(), writes=()):
        self._deps(eng, reads, writes)
        ins = fn(self.h[eng])
        self.cnt[eng] += 1
        key = ("e", eng)
        ins.then_inc(self.sems[key], 1)
        self._mark(key, self.cnt[eng], reads, writes)

    def dma(self, q, fn, reads=(), writes=()):
        i = self.dma_rr[q]
        self.dma_rr[q] = (i + 1) % self.n_dma
        key = ("d", q, i)
        prev = self.dma_tot.get(key, 0)
        self._deps(q, reads, writes)
        if prev > 0 and self.seen[q].get(key, 0) < prev:
            self.seen[q][key] = prev
            self.h[q].wait_ge(self.sems[key], prev)
        ins = fn(self.h[q])
        ins.then_inc(self.sems[key], 16)
        self.dma_tot[key] = prev + 16
        self._mark(key, prev + 16, reads, writes)

    def barrier(self):
        tot = {("e", e): self.cnt[e] for e in ENGS}
        tot.update(self.dma_tot)
        for e in ENGS:
            for k, v in tot.items():
                if v > self.seen[e].get(k, 0):
                    self.seen[e][k] = v
                    self.h[e].wait_ge(self.sems[k], v)

    def wait_all(self, eng, bufs):
        self._deps(eng, bufs, ())


class _Stop(Exception):
    pass


class Ring:
    def __init__(self, views, name="ring", excl=False):
        self.views = views
        self.bufs = [Buf("%s%d" % (name, i), excl) for i in range(len(views))]
        self.i = 0

    def next(self):
        i = self.i
        self.i = (i + 1) % len(self.views)
        return self.views[i], self.bufs[i]


def build_program(stop_after=None, dbg_spec=None, cut=None):
    nc = bass.Bass("TRN2", target_bir_lowering=False)

    def din(name, shape, dt=F32):
        return nc.dram_tensor(name, list(shape), dt, kind="ExternalInput").ap()

    x_d = din("x", [NLAT, 1024])
    ctx_d = din("ctx", [NCTX, 1024])
    cct_d = din("cct", [128, 16])
    w_mod_d = din("w_mod", [1024, 3072])
    w_in_d = din("w_in", [1024, 14400])
    w_ret_o_d = din("w_ret_o", [2048, 1024])
    w_ssd_o_d = din("w_ssd_o", [2048, 1024])
    w_out_d = din("w_out", [1024, 1024])
    vecs_d = din("vecs", [128, NV])
    rows_d = din("rows", [1, NR])
    identb_d = din("identb", [128, 128], BF16)
    swapb_d = din("swapb", [128, 128], BF16)
    ropec_d = din("ropec", [128, 192])
    dlt_d = din("dlt", [128, STRIP])
    ssdc_d = din("ssdc", [128, NC_SSD])
    negm_d = din("negm", [128, 1024], BF16)
    self_d = din("self", [8, 1024])
    identf_d = din("identf", [128, 128])
    out_d = nc.dram_tensor("out", [NLAT, 1024], F32, kind="ExternalOutput").ap()
    yrT_d = nc.dram_tensor("yrT_scr", [2048, NLAT], BF16, kind="Internal").ap()
    ysT_d = nc.dram_tensor("ysT_scr", [2048, NLAT], BF16, kind="Internal").ap()
    dbg_d = None
    if dbg_spec is not None:
        dbg_d = nc.dram_tensor("dbg", [128, dbg_spec], F32, kind="ExternalOutput").ap()

    w_in_v = w_in_d.rearrange("(kc p) n -> p kc n", p=128)
    w_mod_v = w_mod_d.rearrange("(kc p) n -> p kc n", p=128)

    with ExitStack() as st:
        P = Prog(nc, st)

        uid = [0]

        def sb(name, shape, dt, stack=st):
            uid[0] += 1
            return stack.enter_context(nc.sbuf_tensor("s%d_%s" % (uid[0], name), list(shape), dt))

        ps_all = st.enter_context(nc.psum_tensor("ps_all", [128, 4096], F32))

        def bank(i, n=1):
            return ps_all[:, i * 512:(i + n) * 512]

        hT = sb("hT", [128, 8, NTOK], BF16)
        vecs = sb("vecs", [128, NV], F32)
        rowss = sb("rowss", [128, NR - RS], F32)
        Gt = sb("Gt", [128, 1024], F32)
        identb = sb("identb", [128, 128], BF16)
        swapb = sb("swapb", [128, 128], BF16)
        ropec = sb("ropec", [128, 192], F32)
        ssdc = sb("ssdc", [128, NC_SSD], F32)
        lg = sb("lg", [128, 8], F32)
        nlg = sb("nlg", [128, 8], F32)
        Aneg = sb("Aneg", [128, 64], F32)
        sc1 = sb("sc1", [128, 8, 2], F32)
        shf = sb("shf", [128, 8, 2], F32)
        dbg_state = {"off": 0}

        def dump(ap, n, reads=(), pstack=None):
            if dbg_d is None:
                return
            with nc.sbuf_tensor("dbgt%d" % dbg_state["off"], [128, n], F32) as t:
                tb = Buf("dbgt")
                P.op("vector", lambda e: e.tensor_copy(out=t[:], in_=ap), reads=list(reads), writes=[tb])
                o = dbg_state["off"]
                P.dma("sync", lambda e: e.dma_start(out=dbg_d[:, o:o + n], in_=t[:]), reads=[tb], writes=[])
                dbg_state["off"] = o + n
                P.barrier()

        def load_w(wst_ring, wbf_ring, src_view, c0, ncols, dst=None, dstbuf=None, scale_col=None, kc0=0, cast_eng="gpsimd", nkc=8, q="sync"):
            sv_, sbuf_ = wst_ring.next()
            if nkc == 8:
                sv = sv_[:, :, 0:ncols]
            else:
                sv = sv_[:].rearrange("p a b -> p (a b)")[:, 0:nkc * ncols].rearrange("p (a b) -> p a b", a=nkc)
            P.dma(q, lambda e: e.dma_start(out=sv, in_=src_view[:, kc0:kc0 + nkc, c0:c0 + ncols]), writes=[sbuf_])
            if dst is None:
                bv, bbuf = wbf_ring.next()
                bva = bv[:, :, 0:ncols]
            else:
                bv, bbuf = dst, dstbuf
                bva = dst[:, :, 0:ncols]
            if scale_col is None and cast_eng == "scalar":
                P.op("scalar", lambda e: e.activation(out=bva, in_=sv, func=AF.Copy), reads=[sbuf_], writes=[bbuf])
            elif scale_col is None:
                P.op("gpsimd", lambda e: e.tensor_copy(out=bva, in_=sv), reads=[sbuf_], writes=[bbuf])
            elif cast_eng == "scalar":
                for kc in range(nkc):
                    P.op("scalar", lambda e, kc=kc: e.activation(
                        out=bva[:, kc, :], in_=sv[:, kc, :], func=AF.Identity,
                        scale=vecs[:, scale_col + kc0 + kc:scale_col + kc0 + kc + 1]), reads=[sbuf_], writes=[bbuf])
            else:
                for kc in range(nkc):
                    P.op("gpsimd", lambda e, kc=kc: e.tensor_scalar(
                        out=bva[:, kc, :], in0=sv[:, kc, :],
                        scalar1=vecs[:, scale_col + kc0 + kc:scale_col + kc0 + kc + 1], scalar2=0.0,
                        op0=ALU.mult, op1=ALU.add), reads=[sbuf_], writes=[bbuf])
            return bv, bbuf

        def ck(label):
            if cut == label:
                P.barrier()
                return True
            return False

        def phase0(ph):
          if True:
              cb = Buf("consts")
              wst_ring = Ring([sb("wst0_%d" % i, [128, 8, 256], F32, ph) for i in range(4)], "wst0")
              for dst, src in ((vecs, vecs_d), (identb, identb_d), (swapb, swapb_d), (ropec, ropec_d), (ssdc, ssdc_d)):
                  P.dma("sync", lambda e, dst=dst, src=src: e.dma_start(out=dst[:], in_=src[:, :]), writes=[Buf()])
              P.dma("sync", lambda e: e.dma_start(out=rowss[:], in_=rows_d[0:1, RS:NR].partition_broadcast(128)), writes=[cb])
              rowsb = sb("rowsb", [128, 2048], F32, ph)
              P.dma("sync", lambda e: e.dma_start(out=rowsb[:], in_=rows_d[0:1, 0:2048].partition_broadcast(128)), writes=[cb])
              cct = sb("cct", [128, 8, 2], F32, ph)
              P.dma("sync", lambda e: e.dma_start(out=cct[:].rearrange("p a b -> p (a b)"), in_=cct_d[:, :]), writes=[cb])
              P.barrier()
              if ck("a"):
                  return
              scc = sb("scc", [128, 8, 2], F32, ph)
              sccrep = sb("sccrep", [128, 8, 128], F32, ph)
              b0 = Buf("p0")
              P.op("scalar", lambda e: e.activation(out=scc[:], in_=cct[:], func=AF.Silu), writes=[b0])
              P.op("vector", lambda e: e.tensor_copy(out=sccrep[:], in_=scc[:, :, 0:1].broadcast_to([128, 8, 128])),
                   reads=[b0], writes=[b0])
              P.op("scalar", lambda e: e.activation(out=nlg[:], in_=rowss[:, R_RDEC - RS:R_RDEC - RS + 8], func=AF.Exp, scale=-1.0),
                   writes=[b0])
              P.op("scalar", lambda e: e.activation(out=nlg[:], in_=nlg[:], func=AF.Ln, bias=1.0), reads=[b0], writes=[b0])
              P.op("vector", lambda e: e.tensor_scalar(out=lg[:], in0=nlg[:], scalar1=-1.0, scalar2=None, op0=ALU.mult),
                   reads=[b0], writes=[b0])
              P.op("scalar", lambda e: e.activation(out=Aneg[:], in_=rowss[:, R_ALOG - RS:R_ALOG - RS + 64], func=AF.Exp),
                   writes=[b0])
              P.op("vector", lambda e: e.tensor_scalar(out=Aneg[:], in0=Aneg[:], scalar1=-1.0, scalar2=None, op0=ALU.mult),
                   reads=[b0], writes=[b0])
              if ck("b"):
                  return
              ps_mod = bank(0)[:, 0:48]
              ps_gate = bank(1, 2)
              pm = Buf("psmod", True)
              pg = Buf("psgate", True)
              for u in range(12):
                  sv, sbuf_ = wst_ring.next()
                  P.dma("sync" if u % 2 == 0 else "gpsimd", lambda e, sv=sv, u=u: e.dma_start(out=sv[:], in_=w_mod_v[:, :, u * 256:(u + 1) * 256]), writes=[sbuf_])

                  def mm_mod(e, sv=sv, u=u):
                      ins = None
                      for j in range(2):
                          col = (u * 2 + j) * 2
                          for kc in range(8):
                              ins = e.matmul(ps_mod[:, col:col + 2], lhsT=sv[:, kc, j * 128:(j + 1) * 128], rhs=scc[:, kc, :],
                                             start=(kc == 0), stop=(kc == 7))
                      if u >= 8:
                          for kc in range(8):
                              ins = e.matmul(ps_gate[:, (u - 8) * 256:(u - 7) * 256], lhsT=sccrep[:, kc, :], rhs=sv[:, kc, :],
                                             start=(kc == 0), stop=(kc == 7))
                      return ins
                  P.op("tensor", mm_mod, reads=[sbuf_, b0], writes=[pm, pg])
              if ck("c"):
                  return
              modv = sb("modv", [128, 24, 2], F32, ph)
              P.op("vector", lambda e: e.tensor_tensor(out=modv[:], in0=ps_mod.rearrange("p (a b) -> p a b", b=2),
                                                       in1=vecs[:, V_BMOD:V_BMOD + 24].unsqueeze(2).broadcast_to([128, 24, 2]),
                                                       op=ALU.add), reads=[pm], writes=[b0])
              P.op("vector", lambda e: e.scalar_tensor_tensor(out=sc1[:], in0=modv[:, 8:16, :], scalar=1.0,
                                                              in1=vecs[:, V_NPW:V_NPW + 8].unsqueeze(2).broadcast_to([128, 8, 2]),
                                                              op0=ALU.add, op1=ALU.mult), reads=[b0], writes=[b0])
              P.op("vector", lambda e: e.tensor_copy(out=shf[:], in_=modv[:, 0:8, :]), reads=[b0], writes=[b0])
              P.op("vector", lambda e: e.tensor_tensor(out=Gt[:], in0=ps_gate, in1=rowsb[:, 1024:2048], op=ALU.add),
                   reads=[pg], writes=[b0])
              P.op("vector", lambda e: e.tensor_tensor(out=Gt[:], in0=Gt[:], in1=rowsb[:, 0:1024], op=ALU.mult),
                   reads=[b0], writes=[b0])
              P.barrier()

              if ck("d"):
                  return
              xt_ring = Ring([sb("xt%d" % i, [128, 1024], F32, ph) for i in range(3)], "xt")
              xb_ring = Ring([sb("xb%d" % i, [128, 1024], BF16, ph) for i in range(2)], "xb")
              junk = sb("junk", [128, 1024], BF16, ph)
              jb = Buf("junk")
              ss = sb("ss", [128, 18], F32, ph)
              rs = sb("rs", [128, 18], F32, ph)
              pst_ring = Ring([bank(3).bitcast(BF16), bank(4).bitcast(BF16)], "pst", True)
              for t in range(18):
                  src = ctx_d[t * 128:(t + 1) * 128, :] if t < 2 else x_d[(t - 2) * 128:(t - 1) * 128, :]
                  which = 1 if t < 2 else 0
                  xt, xtb = xt_ring.next()
                  P.dma("sync", lambda e, xt=xt, src=src: e.dma_start(out=xt[:], in_=src), writes=[xtb])
                  sb_ = Buf("ss")
                  P.op("scalar", lambda e, xt=xt, t=t: e.activation(out=junk[:], in_=xt[:], func=AF.Square, accum_out=ss[:, t:t + 1]),
                       reads=[xtb], writes=[jb, sb_])
                  if t == 0 and ck("e"):
                      return
                  P.op("scalar", lambda e, t=t: e.activation(out=rs[:, t:t + 1], in_=ss[:, t:t + 1], func=AF.Sqrt, scale=1.0 / 1024.0, bias=EPS),
                       reads=[sb_], writes=[sb_])
                  P.op("vector", lambda e, t=t: e.reciprocal(out=rs[:, t:t + 1], in_=rs[:, t:t + 1]), reads=[sb_], writes=[sb_])
                  if t == 0 and ck("f"):
                      return
                  xb, xbb = xb_ring.next()
                  P.op("vector", lambda e, xt=xt, xb=xb, t=t: e.tensor_scalar(out=xb[:], in0=xt[:], scalar1=rs[:, t:t + 1], scalar2=None, op0=ALU.mult),
                       reads=[xtb, sb_], writes=[xbb])
                  if t == 0 and ck("g"):
                      return
                  pst, pstb = pst_ring.next()

                  def tr8(e, xb=xb, pst=pst):
                      ins = None
                      for kc in range(8):
                          ins = e.transpose(pst[:, kc * 128:(kc + 1) * 128], xb[:, kc * 128:(kc + 1) * 128], identb[:])
                      return ins
                  P.op("tensor", tr8, reads=[xbb], writes=[pstb])
                  if t == 0 and ck("h"):
                      return
                  for kc in range(8):
                      o = hT[:, kc, t * 128:(t + 1) * 128]
                      i_ = pst[:, kc * 128:(kc + 1) * 128]
                      s_ = sc1[:, kc, which:which + 1]
                      b_ = shf[:, kc, which:which + 1]
                      if kc % 8 in EVAC_ACT:
                          P.op("scalar", lambda e, o=o, i_=i_, s_=s_, b_=b_: e.activation(out=o, in_=i_, func=AF.Identity, scale=s_, bias=b_),
                               reads=[pstb], writes=[])
                      else:
                          P.op("vector", lambda e, o=o, i_=i_, s_=s_, b_=b_: e.tensor_scalar(out=o, in0=i_, scalar1=s_, scalar2=b_, op0=ALU.mult, op1=ALU.add),
                               reads=[pstb], writes=[])
                  if t == 0 and ck("j"):
                      return
                  if t == 3 and ck("k"):
                      return
              P.barrier()
        with ExitStack() as ph:
            phase0(ph)
        if dbg_spec is not None and stop_after == 0:
            for kc in range(2):
                dump(hT[:, kc, 0:512], 512)
            dump(Gt[:, 0:256], 256)
            dump(lg[:], 8)

        def phase_R(ph):
            wst_ring = Ring([sb("wstR%d" % i, [128, 8, 256], F32, ph) for i in range(2)], "wstR")
            wbf_ring = Ring([sb("wbfR%d" % i, [128, 8, 256], BF16, ph) for i in range(3)], "wbfR")
            KT = sb("KT", [128, 2, NTOK], BF16, ph)
            QT = sb("QT", [128, 2, NLAT], BF16, ph)
            Vt = sb("Vt", [128, 18, 512], BF16, ph)
            Gs = sb("Gs", [128, 16, 512], BF16, ph)
            Th = sb("Th", [128, STRIP], F32, ph)
            dl_ring = Ring([sb("dl%d" % i, [128, 560], F32, ph) for i in range(2)], "dl")
            t1_ring = Ring([sb("t1_%d" % i, [128, 560], F32, ph) for i in range(2)], "t1")
            M_ring = Ring([sb("M%d" % i, [128, 512], BF16, ph) for i in range(4)], "M")
            ys4 = [sb("ys4_%d" % i, [128, 4, 512], F32, ph) for i in range(2)]
            ys4b = [[Buf("ys4") for _ in range(4)] for _ in range(2)]
            xsb_ring = Ring([sb("xsb%d" % i, [128, 512], BF16, ph) for i in range(3)], "xsb")
            rt_ring = Ring([sb("rt%d" % i, [128, 512], F32, ph) for i in range(4)], "rt")
            yn_ring = Ring([sb("yn%d" % i, [128, 512], F32, ph) for i in range(4)], "yn")
            yg_ring = Ring([sb("yg%d" % i, [128, 512], BF16, ph) for i in range(4)], "yg")
            yT_ring = Ring([sb("yT%d" % i, [128, 4, 512], BF16, ph) for i in range(2)], "yT")
            st6 = sb("st6", [128, 4, 6], F32, ph)
            mv = sb("mv", [128, 4, 2], F32, ph)
            rstd = sb("rstd", [128, 4], F32, ph)
            nmr = sb("nmr", [128, 4], F32, ph)
            pbR = [Buf("bankR%d" % i, True) for i in range(8)]
            psy = [bank(i) for i in range(4)]
            psyb = pbR[0:4]
            pss_ring = Ring([bank(4), bank(5), bank(6)], "pss")
            pss_ring.bufs = pbR[4:7]
            psp_ring = Ring([bank(7), bank(0), bank(1), bank(2), bank(3)], "psp")
            psp_ring.bufs = [pbR[7]] + pbR[0:4]
            pst_ring = Ring([bank(7).bitcast(BF16)], "pstR")
            pst_ring.bufs = [pbR[7]]
            cos0 = ropec[:, 0:32].unsqueeze(2).broadcast_to([128, 32, 64])
            cos1 = ropec[:, 32:96].unsqueeze(1).broadcast_to([128, 32, 64])
            sin0 = ropec[:, 96:128].unsqueeze(2).broadcast_to([128, 32, 64])
            sin1 = ropec[:, 128:192].unsqueeze(1).broadcast_to([128, 32, 64])

            def v3(ap):
                return ap.rearrange("p (a b) -> p a b", b=64)

            ktb = [Buf("kt") for _ in range(18)]
            qtb = [Buf("qt") for _ in range(4)]
            vtb = [Buf("vt") for _ in range(18)]
            gsb = [Buf("gs") for _ in range(16)]
            thb = Buf("Th")
            def gen_mask(h, blks=range(8)):
                for blk in blks:
                    dl, dlb = dl_ring.next()
                    c = slice(blk * 560, (blk + 1) * 560)
                    P.dma("sync", lambda e, dl=dl, c=c: e.dma_start(out=dl[:], in_=dlt_d[:, c]), writes=[dlb])
                    t1, t1b = t1_ring.next()
                    P.op("scalar", lambda e, dl=dl, t1=t1: e.activation(out=t1[:], in_=dl[:], func=AF.Identity, scale=lg[:, 4 + h:5 + h]),
                         reads=[dlb], writes=[t1b])
                    P.op("vector", lambda e, dl=dl, t1=t1: e.scalar_tensor_tensor(out=t1[:], in0=dl[:], scalar=nlg[:, h:h + 1], in1=t1[:],
                                                                                 op0=ALU.mult, op1=ALU.max), reads=[dlb], writes=[t1b])
                    P.op("scalar", lambda e, t1=t1, c=c: e.activation(out=Th[:, c], in_=t1[:], func=AF.Exp, scale=-1.0), reads=[t1b], writes=[thb])

            gen_mask(0)
            kq_pref = {}
            for h in range(4):
                def kq_stage1(kind, wv, wb, dc, t0, n, isctx):
                    psp, pspb = psp_ring.next()

                    def mmp(e, wv=wv, dc=dc, t0=t0, n=n, psp=psp):
                        ins = None
                        for kc in range(8):
                            ins = e.matmul(psp[:, 0:n], lhsT=wv[:, kc, dc * 128:(dc + 1) * 128], rhs=hT[:, kc, t0:t0 + n],
                                           start=(kc == 0), stop=(kc == 7))
                        return ins
                    P.op("tensor", mmp, reads=[wb], writes=[pspb])
                    if isctx:
                        P.op("scalar", lambda e, psp=psp, dc=dc: e.activation(out=KT[:, dc, 0:256], in_=psp[:, 0:256], func=AF.Copy),
                             reads=[pspb], writes=[ktb[0], ktb[1]])
                        return None
                    qb = (t0 - 256) // 512
                    sc = 1.0 if kind == "k" else 1.0 / 16.0
                    xsb, xsbb = xsb_ring.next()
                    P.op("scalar", lambda e, psp=psp, xsb=xsb, sc=sc: e.activation(out=xsb[:], in_=psp[:], func=AF.Copy, scale=sc),
                         reads=[pspb], writes=[xsbb])
                    rt, rtb = rt_ring.next()
                    cosb = cos0 if dc == 0 else cos1
                    crow = slice(qb * 8, qb * 8 + 8)
                    P.op("vector", lambda e, psp=psp, rt=rt, cosb=cosb, crow=crow, sc=sc: e.scalar_tensor_tensor(
                        out=v3(rt[:]), in0=v3(psp[:]), scalar=sc, in1=cosb[:, crow, :], op0=ALU.mult, op1=ALU.mult),
                        reads=[pspb], writes=[rtb])
                    return (kind, dc, t0, qb, crow, xsb, xsbb, rt, rtb)

                def kq_stage2(kind, dc, t0, qb, crow, xsb, xsbb, rt, rtb):
                    sinb = sin0 if dc == 0 else sin1
                    psw, pswb = psp_ring.next()
                    P.op("tensor", lambda e, psw=psw, xsb=xsb: e.matmul(psw[:], lhsT=swapb[:], rhs=xsb[:], start=True, stop=True),
                         reads=[xsbb], writes=[pswb])
                    rt2, rt2b = rt_ring.next()
                    P.op("vector", lambda e, psw=psw, rt2=rt2, sinb=sinb, crow=crow: e.tensor_tensor(
                        out=v3(rt2[:]), in0=v3(psw[:]), in1=sinb[:, crow, :], op=ALU.mult), reads=[pswb], writes=[rt2b])
                    if kind == "k":
                        dst = KT[:, dc, t0:t0 + 512]
                        wr = ktb[2 + qb * 4:6 + qb * 4]
                    else:
                        dst = QT[:, dc, qb * 512:(qb + 1) * 512]
                        wr = [qtb[qb]]
                    P.op("gpsimd", lambda e, dst=dst, rt=rt, rt2=rt2: e.tensor_tensor(out=dst, in0=rt[:], in1=rt2[:], op=ALU.add),
                         reads=[rtb, rt2b], writes=wr)

                kq_pending = None
                for kind in ("k", "q"):
                    c0 = (1024 if kind == "k" else 0) + h * 256
                    if (h, kind) in kq_pref:
                        wv, wb = kq_pref.pop((h, kind))
                    else:
                        wv, wb = load_w(wst_ring, wbf_ring, w_in_v, c0, 256, cast_eng="scalar")
                    for dc in range(2):
                        blocks = [(0, 256, True)] if kind == "k" else []
                        blocks += [(256 + qb * 512, 512, False) for qb in range(4)]
                        for (t0, n, isctx) in blocks:
                            st1 = kq_stage1(kind, wv, wb, dc, t0, n, isctx)
                            if kq_pending is not None:
                                kq_stage2(*kq_pending)
                            kq_pending = st1
                if kq_pending is not None:
                    kq_stage2(*kq_pending)
                if h == 0 and ck("r1"):
                    return
                vg_it = 0
                for kind in ("v", "g"):
                    for half in range(2):
                        c0 = (2048 if kind == "v" else 4096) + h * 512 + half * 256
                        wv, wb = load_w(wst_ring, wbf_ring, w_in_v, c0, 256, cast_eng="scalar")
                        for t in range(18):
                            if kind == "g" and t < 2:
                                continue
                            if h > 0 and vg_it % 8 == 0 and vg_it // 8 < 8:
                                gen_mask(h, [vg_it // 8])
                            vg_it += 1
                            psp, pspb = psp_ring.next()

                            def mmp(e, wv=wv, t=t, psp=psp):
                                ins = None
                                for kc in range(8):
                                    ins = e.matmul(psp[:, 0:256], lhsT=hT[:, kc, t * 128:(t + 1) * 128], rhs=wv[:, kc, :],
                                                   start=(kc == 0), stop=(kc == 7))
                                return ins
                            P.op("tensor", mmp, reads=[wb], writes=[pspb])
                            if kind == "v":
                                P.op("scalar", lambda e, psp=psp, t=t, half=half: e.activation(
                                    out=Vt[:, t, half * 256:(half + 1) * 256], in_=psp[:, 0:256], func=AF.Copy),
                                    reads=[pspb], writes=[vtb[t]])
                            else:
                                P.op("scalar", lambda e, psp=psp, t=t, half=half: e.activation(
                                    out=Gs[:, t - 2, half * 256:(half + 1) * 256], in_=psp[:, 0:256], func=AF.Silu),
                                    reads=[pspb], writes=[gsb[t - 2]])
                if h == 0 and ck("r2"):
                    return
                if h < 3:
                    for kind in ("k", "q"):
                        c0n = (1024 if kind == "k" else 0) + (h + 1) * 256
                        kq_pref[(h + 1, kind)] = load_w(wst_ring, wbf_ring, w_in_v, c0n, 256, cast_eng="scalar")
                if h == 0 and ck("r3"):
                    return
                ktiles = [(0, -256), (1, -128)] + [(2 + t, 128 * t) for t in range(16)] + [(0, 2048), (1, 2176)]
                LA = 2
                pending = [None]

                def gn_stages(qb, slot):
                    sbbs = [Buf("stat") for _ in range(4)]
                    srcs = [(ys4[slot][:, qt, :], ys4b[slot][qt]) for qt in range(4)]
                    yns = []
                    ygs = []

                    def s1():
                        for qt in range(4):
                            src, srcb = srcs[qt]
                            P.op("vector", lambda e, qt=qt, src=src: e.bn_stats(out=st6[:, qt, :], in_=src), reads=[srcb], writes=[sbbs[qt]])
                            P.op("vector", lambda e, qt=qt: e.bn_aggr(out=mv[:, qt, :], in_=st6[:, qt, :]), reads=[sbbs[qt]], writes=[sbbs[qt]])

                    def s2():
                        for qt in range(4):
                            P.op("scalar", lambda e, qt=qt: e.activation(out=rstd[:, qt:qt + 1], in_=mv[:, qt, 1:2], func=AF.Sqrt, bias=EPS, scale=1.0),
                                 reads=[sbbs[qt]], writes=[sbbs[qt]])

                    def s3():
                        for qt in range(4):
                            P.op("vector", lambda e, qt=qt: e.reciprocal(out=rstd[:, qt:qt + 1], in_=rstd[:, qt:qt + 1]), reads=[sbbs[qt]], writes=[sbbs[qt]])
                            P.op("vector", lambda e, qt=qt: e.scalar_tensor_tensor(out=nmr[:, qt:qt + 1], in0=mv[:, qt, 0:1], scalar=-1.0,
                                                                                  in1=rstd[:, qt:qt + 1], op0=ALU.mult, op1=ALU.mult),
                                 reads=[sbbs[qt]], writes=[sbbs[qt]])

                    def s4():
                        for qt in range(4):
                            src, srcb = srcs[qt]
                            yn, ynb = yn_ring.next()
                            P.op("scalar", lambda e, qt=qt, yn=yn, src=src: e.activation(out=yn[:], in_=src, func=AF.Identity,
                                                                                        scale=rstd[:, qt:qt + 1], bias=nmr[:, qt:qt + 1]),
                                 reads=[srcb, sbbs[qt]], writes=[ynb])
                            yns.append((yn, ynb))

                    def s5():
                        for qt in range(4):
                            gi = qb * 4 + qt
                            yn, ynb = yns[qt]
                            yg, ygb = yg_ring.next()
                            P.op("gpsimd", lambda e, yn=yn, yg=yg, gi=gi: e.tensor_tensor(out=yg[:], in0=yn[:], in1=Gs[:, gi, :], op=ALU.mult),
                                 reads=[ynb, gsb[gi]], writes=[ygb])
                            ygs.append((yg, ygb))
                    return [s1, s2, s3, s4, s5], ygs

                def gn_b(qb, ygs):
                    qs = 512 * qb
                    yT, yTb = yT_ring.next()
                    for qt in range(4):
                        yg, ygb = ygs[qt]
                        pst, pstb = pst_ring.next()

                        def tr4(e, yg=yg, pst=pst):
                            ins = None
                            for fc in range(4):
                                ins = e.transpose(pst[:, fc * 128:(fc + 1) * 128], yg[:, fc * 128:(fc + 1) * 128], identb[:])
                            return ins
                        P.op("tensor", tr4, reads=[ygb], writes=[pstb])
                        P.op("vector", lambda e, pst=pst, yT=yT, qt=qt: e.tensor_copy(
                            out=yT[:, :, qt * 128:(qt + 1) * 128], in_=pst[:, 0:512].rearrange("p (a b) -> p a b", b=128)),
                            reads=[pstb], writes=[yTb])
                    dstd = yrT_d[h * 512:(h + 1) * 512, qs:qs + 512].rearrange("(fc p) t -> p fc t", p=128)
                    P.dma("gpsimd", lambda e, yT=yT, dstd=dstd: e.dma_start(out=dstd, in_=yT[:]), reads=[yTb], writes=[yrb])

                for qb in range(4):
                    qs = 512 * qb
                    slot = qb % 2
                    nk = len(ktiles)
                    Ms = {}
                    for stp in range(nk + LA):
                        if stp < nk:
                            kt, kp = ktiles[stp]
                            pss, pssb = pss_ring.next()

                            def mms(e, pss=pss, kt=kt, qb=qb):
                                ins = None
                                for dc in range(2):
                                    ins = e.matmul(pss[:], lhsT=KT[:, dc, kt * 128:(kt + 1) * 128], rhs=QT[:, dc, qb * 512:(qb + 1) * 512],
                                                   start=(dc == 0), stop=(dc == 1))
                                return ins
                            P.op("tensor", mms, reads=[ktb[kt], qtb[qb]], writes=[pssb])
                            x0 = qs - kp + XOFF
                            Mv, Mb = M_ring.next()
                            P.op("vector", lambda e, pss=pss, Mv=Mv, x0=x0: e.tensor_tensor(out=Mv[:], in0=pss[:], in1=Th[:, x0:x0 + 512], op=ALU.mult),
                                 reads=[pssb, thb], writes=[Mb])
                            Ms[stp] = (Mv, Mb)
                        ki = stp - LA
                        if ki >= 0:
                            kt, kp = ktiles[ki]
                            Mv, Mb = Ms.pop(ki)

                            def mmy(e, Mv=Mv, kt=kt, ki=ki, nk=nk):
                                ins = None
                                for qt in range(4):
                                    ins = e.matmul(psy[qt][:], lhsT=Mv[:, qt * 128:(qt + 1) * 128], rhs=Vt[:, kt, :],
                                                   start=(ki == 0), stop=(ki == nk - 1))
                                return ins
                            P.op("tensor", mmy, reads=[Mb, vtb[kt]], writes=psyb)
                        if pending[0] is not None:
                            if stp == 1:
                                pending[0] = (pending[0][0],) + gn_stages(*pending[0])
                            if stp in (1, 3, 5, 7, 9):
                                pending[0][1][(stp - 1) // 2]()
                            if stp == 14:
                                gn_b(pending[0][0], pending[0][2])
                                pending[0] = None
                    for qt in range(4):
                        P.op("scalar", lambda e, qt=qt, slot=slot: e.activation(out=ys4[slot][:, qt, :], in_=psy[qt][:], func=AF.Copy),
                             reads=[psyb[qt]], writes=[ys4b[slot][qt]])
                    pending[0] = (qb, slot)
                stages, ygs_last = gn_stages(*pending[0])
                for st_ in stages:
                    st_()
                gn_b(pending[0][0], ygs_last)
                pending[0] = None

        def phase_S(ph):
            pb = [Buf("bank%d" % i, True) for i in range(8)]
            dtv = sb("dtv", [128, 18, 64], F32, ph)
            wdt = sb("wdt", [128, 8, 64], BF16, ph)
            a8 = sb("a8", [128, 2, 18, 4], F32, ph)
            dt8 = sb("dt8", [128, 2, 18, 4], F32, ph)
            eall = sb("eall", [128, 6, 72], F32, ph)
            wall = sb("wall", [128, 2, 72], F32, ph)
            BCT = sb("BCT", [128, 2, 2308], BF16, ph)
            xsT_ring = Ring([sb("xsT%d" % i, [128, 2308], BF16, ph) for i in range(2)], "xsT")
            raw2 = sb("raw2", [128, 2, 2312], F32, ph)
            raw_ring = Ring([raw2[:, 0, :], raw2[:, 1, :]], "raw")
            xw_all = raw2[:].rearrange("p a b -> p (a b)").bitcast(BF16)[:, 0:2 * 18 * 256].rearrange("p (d t q) -> p d t q", d=2, t=18)
            acc_ring = Ring([sb("acc%d" % i, [128, 2308], F32, ph) for i in range(1)], "acc")
            xs_tm = sb("xs_tm", [128, 18, 256], BF16, ph)
            B_tm = sb("B_tm", [128, 18, 128], BF16, ph)
            sz = sb("sz", [128, 16, 256], BF16, ph)
            S_store = sb("S_store", [128, 2, 16, 256], BF16, ph)
            S32 = sb("S32", [128, 2, 256], F32, ph)
            Dec_ring = Ring([sb("Dec%d" % i, [128, 2, 4, 128], F32, ph) for i in range(2)], "Dec")
            nacum = sb("nacum", [128, 18, 8], F32, ph)
            acum = sb("acum", [128, 18, 8], F32, ph)
            dD = sb("dD", [128, 2, 4, 128], BF16, ph)
            dDt = sb("dDt", [128, 2, 128], F32, ph)
            negm = sb("negm", [128, 2, 4, 128], BF16, ph)
            identf = sb("identf", [128, 128], F32, ph)
            cbS = Buf("constS")
            P.dma("sync", lambda e: e.dma_start(out=negm[:].rearrange("p d r i -> p (d r i)"), in_=negm_d[:, :]), writes=[cbS])
            P.dma("sync", lambda e: e.dma_start(out=identf[:], in_=identf_d[:, :]), writes=[cbS])
            M_ring = Ring([sb("Ms%d" % i, [128, 2, 4, 128], BF16, ph) for i in range(3)], "Ms")
            v_ring = Ring([sb("v%d" % i, [128, 2, 4, 64], BF16, ph) for i in range(3)], "v")
            gi_ring = Ring([sb("gi%d" % i, [128, 2, 256], F32, ph) for i in range(2)], "gi")
            t1_ring = Ring([sb("e1_%d" % i, [128, 256], F32, ph) for i in range(2)], "e1")
            t2_ring = Ring([sb("e2_%d" % i, [128, 256], F32, ph) for i in range(2)], "e2")
            yo_ring = Ring([sb("yo%d" % i, [128, 256], BF16, ph) for i in range(2)], "yo")
            junk = sb("junkS", [128, 256], BF16, ph)
            jb = Buf("junkS")
            ss_ring = Ring([sb("ss1_%d" % i, [128, 2], F32, ph) for i in range(2)], "ss1")
            ysT_ring = Ring([sb("ysT%d" % i, [128, 2, NLAT], BF16, ph) for i in range(1)], "ysT")
            tmp64 = sb("tmp64", [128, 64], F32, ph)
            epsc = sb("epsc", [128, 1], F32, ph)
            P.op("gpsimd", lambda e: e.memset(epsc[:], EPS), writes=[Buf()])
            wst_ring = Ring([sb("wstS%d" % i, [128, 8, 256], F32, ph) for i in range(2)], "wstS")
            wbf_ring = Ring([sb("wbfS%d" % i, [128, 8, 256], BF16, ph) for i in range(2)], "wbfS")
            tri = ssdc[:, C_TRI:C_TRI + 256].rearrange("p (d i) -> p d i", d=2)
            Lm = ssdc[:, C_L:C_L + 256].rearrange("p (d i) -> p d i", d=2)
            m01 = ssdc[:, C_M01:C_M01 + 256].rearrange("p (d i) -> p d i", d=2)
            ones = ssdc[:, C_ONES:C_ONES + 128]
            proj_i = [0]

            def next_proj():
                i = proj_i[0]
                proj_i[0] = (i + 1) % 4
                return bank(i), pb[i]

            wdtb = Buf("wdt")
            load_w(wst_ring, wbf_ring, w_in_v, 12288, 64, dst=wdt, dstbuf=wdtb)
            dtb = Buf("dtv")
            for t in range(18):
                psp, pspb = next_proj()

                def mmd(e, t=t, psp=psp):
                    ins = None
                    for kc in range(8):
                        ins = e.matmul(psp[:, 0:64], lhsT=hT[:, kc, t * 128:(t + 1) * 128], rhs=wdt[:, kc, :], start=(kc == 0), stop=(kc == 7))
                    return ins
                P.op("tensor", mmd, reads=[wdtb], writes=[pspb])
                P.op("vector", lambda e, psp=psp: e.tensor_tensor(out=tmp64[:], in0=psp[:, 0:64], in1=rowss[:, R_DTB - RS:R_DTB - RS + 64], op=ALU.add),
                     reads=[pspb], writes=[dtb])
                P.op("scalar", lambda e: e.activation(out=tmp64[:], in_=tmp64[:], func=AF.Exp), reads=[dtb], writes=[dtb])
                P.op("scalar", lambda e, t=t: e.activation(out=dtv[:, t, :], in_=tmp64[:], func=AF.Ln, bias=1.0), reads=[dtb], writes=[dtb])

            gb = Buf("grp")
            szb = [Buf("sz") for _ in range(16)]
            xsb_ = Buf("xs_tm")
            btb = Buf("B_tm")
            bctb = Buf("BCT")
            eb = Buf("eall")
            xwb2 = [Buf("xw_all_f"), Buf("xw_all_b")]
            acb = Buf("acum")
            ddb = Buf("dD")
            s32b = [Buf("S32f"), Buf("S32b")]
            stb = [[Buf("Sst") for _ in range(16)] for _ in range(2)]
            for g in range(8):
                P.op("vector", lambda e, g=g: e.tensor_copy(
                    out=dt8[:], in_=dtv[:].rearrange("p t (d h) -> p d t h", d=2)[:, :, :, g * 4:(g + 1) * 4]), reads=[dtb], writes=[gb])
                P.op("vector", lambda e, g=g: e.tensor_tensor(
                    out=a8[:], in0=dt8[:],
                    in1=Aneg[:].rearrange("p (d h) -> p d h", d=2)[:, :, g * 4:(g + 1) * 4].unsqueeze(2).broadcast_to([128, 2, 18, 4]),
                    op=ALU.mult), reads=[gb], writes=[gb])
                for rv, rb in zip(raw_ring.views, raw_ring.bufs):
                    for (p0, p1) in ((0, 2), (258, 262), (2310, 2312)):
                        P.op("gpsimd", lambda e, rv=rv, p0=p0, p1=p1: e.memset(rv[:, p0:p1], 0.0), writes=[rb])
                for r in range(4):
                    dcol = rowss[:, R_D - RS + g * 4 + r:R_D - RS + g * 4 + r + 1]
                    P.op("gpsimd", lambda e, dcol=dcol: e.tensor_scalar(out=dDt[:, 0, :], in0=identf[:], scalar1=dcol, scalar2=None, op0=ALU.mult),
                         reads=[cbS], writes=[ddb])
                    P.op("gpsimd", lambda e, r=r: e.tensor_copy(out=dD[:, 0, r, :], in_=dDt[:, 0, :]), reads=[ddb], writes=[ddb])
                    P.op("gpsimd", lambda e, r=r: e.tensor_tensor(out=dDt[:, 1, :], in0=dDt[:, 0, :], in1=dD[:, 0, r, :], op=ALU.subtract), reads=[ddb], writes=[ddb])
                    P.op("gpsimd", lambda e, r=r: e.tensor_copy(out=dD[:, 1, r, :], in_=dDt[:, 1, :]), reads=[ddb], writes=[ddb])
                chunks = [("x", 8192 + g * 256, 0, g * 2), ("x", 8192 + g * 256 + 128, 1, g * 2 + 1),
                          ("B", 8192 + 2048 + g * 128, 0, 16 + g), ("C", 8192 + 3072 + g * 128, 0, 24 + g)]

                def proj(k):
                    (kind, c0, j, cch) = chunks[k]
                    wv, wb = load_w(wst_ring, wbf_ring, w_in_v, c0, 128, cast_eng="scalar")
                    raw, rawb = raw_ring.next()
                    blocks = ([] if kind == "C" else [(0, 256, 2)]) + [(256 + q * 512, 512, 262 + q * 512) for q in range(4)]
                    for (t0, n, r0) in blocks:
                        psp, pspb = next_proj()

                        def mmx(e, wv=wv, t0=t0, n=n, psp=psp):
                            ins = None
                            for kc in range(8):
                                ins = e.matmul(psp[:, 0:n], lhsT=wv[:, kc, 0:128], rhs=hT[:, kc, t0:t0 + n], start=(kc == 0), stop=(kc == 7))
                            return ins
                        P.op("tensor", mmx, reads=[wb], writes=[pspb])
                        P.op("scalar", lambda e, psp=psp, raw=raw, n=n, r0=r0: e.activation(out=raw[:, r0:r0 + n], in_=psp[:, 0:n], func=AF.Copy),
                             reads=[pspb], writes=[rawb])
                    return raw, rawb

                def conv_tr(k, raw, rawb):
                    (kind, c0, j, cch) = chunks[k]
                    acc, accb = acc_ring.next()
                    for kk in range(5):
                        wk = vecs[:, V_CONVW + kk * 32 + cch:V_CONVW + kk * 32 + cch + 1]
                        if kk == 0:
                            P.op("vector", lambda e, acc=acc, raw=raw, wk=wk: e.tensor_scalar(out=acc[:], in0=raw[:, 0:2308], scalar1=wk, scalar2=None, op0=ALU.mult),
                                 reads=[rawb], writes=[accb])
                        else:
                            P.op("vector", lambda e, acc=acc, raw=raw, wk=wk, kk=kk: e.scalar_tensor_tensor(
                                out=acc[:], in0=raw[:, kk:kk + 2308], scalar=wk, in1=acc[:], op0=ALU.mult, op1=ALU.add),
                                reads=[rawb], writes=[accb])
                    bcol = vecs[:, V_CONVB + cch:V_CONVB + cch + 1]
                    if kind == "x":
                        dst, dstb = xsT_ring.next()
                        dsta = dst[:]
                    else:
                        dsta, dstb = BCT[:, 0 if kind == "B" else 1, :], bctb
                    P.op("scalar", lambda e, acc=acc, dsta=dsta, bcol=bcol: e.activation(out=dsta, in_=acc[:], func=AF.Silu, bias=bcol),
                         reads=[accb], writes=[dstb])
                    if kind == "C":
                        return
                    for t0 in range(0, 18, 6):
                        pst = bank(7).bitcast(BF16)

                        def tr6(e, dsta=dsta, t0=t0, pst=pst):
                            ins = None
                            for i in range(6):
                                t = t0 + i
                                u0 = t * 128 if t < 2 else 260 + (t - 2) * 128
                                ins = e.transpose(pst[:, i * 128:(i + 1) * 128], dsta[:, u0:u0 + 128], identb[:])
                            return ins
                        P.op("tensor", tr6, reads=[dstb], writes=[pb[7]])
                        if kind == "x":
                            o = xs_tm[:, t0:t0 + 6, j * 128:(j + 1) * 128]
                            ob_ = xsb_
                        else:
                            o = B_tm[:, t0:t0 + 6, :]
                            ob_ = btb
                        P.op("scalar", lambda e, o=o, pst=pst: e.activation(out=o, in_=pst[:, 0:768].rearrange("p (a b) -> p a b", b=128), func=AF.Copy),
                             reads=[pb[7]], writes=[ob_])

                def z_load():
                    return load_w(wst_ring, wbf_ring, w_in_v, 6144 + g * 256, 256, cast_eng="scalar")

                def z_mm(c, wv, wb):
                    if True:
                        psp, pspb = next_proj()

                        def mmz(e, wv=wv, c=c, psp=psp):
                            ins = None
                            for kc in range(8):
                                ins = e.matmul(psp[:, 0:256], lhsT=hT[:, kc, (2 + c) * 128:(3 + c) * 128], rhs=wv[:, kc, 0:256], start=(kc == 0), stop=(kc == 7))
                            return ins
                        P.op("tensor", mmz, reads=[wb], writes=[pspb])
                        P.op("scalar", lambda e, psp=psp, c=c: e.activation(out=sz[:, c, :], in_=psp[:, 0:256], func=AF.Silu),
                             reads=[pspb], writes=[szb[c]])

                nxt = proj(0)
                for k in range(4):
                    cur = nxt
                    nxt = proj(k + 1) if k < 3 else None
                    if k == 2:
                        zw = z_load()
                    conv_tr(k, *cur)
                if g == 0 and ck("s1"):
                    return


                def mmb(e):
                    ins = None
                    for d in range(2):
                        rhs = a8[:, d, :, :].rearrange("p t r -> p (t r)")
                        e.matmul(bank(0)[:, d * 72:(d + 1) * 72], lhsT=Lm[:, d, :], rhs=rhs, start=True, stop=True)
                        e.matmul(bank(0)[:, 144 + d * 72:144 + (d + 1) * 72], lhsT=ones, rhs=rhs, start=True, stop=True)
                        ins = e.matmul(bank(0)[:, 288 + d * 72:288 + (d + 1) * 72], lhsT=tri[:, d, :], rhs=rhs, start=True, stop=True)
                    return ins
                P.op("tensor", mmb, reads=[gb], writes=[pb[0]])
                P.op("scalar", lambda e: e.activation(out=eall[:].rearrange("p a b -> p (a b)"), in_=bank(0)[:, 0:432], func=AF.Exp), reads=[pb[0]], writes=[eb])
                P.op("vector", lambda e: e.tensor_tensor(out=wall[:], in0=eall[:, 0:2, :], in1=dt8[:].rearrange("p d t r -> p d (t r)"), op=ALU.mult),
                     reads=[eb, gb], writes=[eb])
                P.op("vector", lambda e: e.tensor_copy(out=acum[:].rearrange("p t (d r) -> p d t r", d=2),
                                                       in_=bank(0)[:, 288:432].rearrange("p (d t r) -> p d t r", d=2, t=18)), reads=[pb[0]], writes=[acb])
                P.op("vector", lambda e: e.tensor_scalar(out=nacum[:], in0=acum[:], scalar1=-1.0, scalar2=None, op0=ALU.mult), reads=[acb], writes=[acb])
                P._deps("gpsimd", (), raw_ring.bufs)
                for d in (0, 1):
                    P.op("vector" if d == 1 else "gpsimd", lambda e, d=d: e.tensor_tensor(
                        out=xw_all[:, d, :, :].rearrange("p t (r q) -> p t r q", q=64), in0=xs_tm[:].rearrange("p t (r q) -> p t r q", q=64),
                        in1=wall[:, d, :].rearrange("p (t r) -> p t r", r=4).unsqueeze(3).broadcast_to([128, 18, 4, 64]), op=ALU.mult),
                        reads=[eb, xsb_], writes=[xwb2[d]] + (raw_ring.bufs if d == 1 else []))
                etot = eall[:, 2:4, :].rearrange("p d (t r) -> p d t r", r=4)
                ecum = eall[:, 4:6, :].rearrange("p d (t r) -> p d t r", r=4)

                P.op("gpsimd", lambda e: e.memset(S32[:], 0.0), writes=s32b)
                fwd_tiles = list(range(0, 17))
                bwd_tiles = [1, 0] + list(range(17, 2, -1))
                step = [0]

                def chain_step(d, t, slot):
                    bi = 4 + (step[0] % 3)
                    step[0] += 1
                    P.op("tensor", lambda e, d=d, t=t, bi=bi: e.matmul(bank(bi)[:, 0:256], lhsT=B_tm[:, t, :], rhs=xw_all[:, d, t, :], start=True, stop=True),
                         reads=[xwb2[d], btb], writes=[pb[bi]])
                    P.op("vector", lambda e, d=d, t=t: e.tensor_tensor(
                        out=S32[:, d, :].rearrange("p (r q) -> p r q", q=64), in0=S32[:, d, :].rearrange("p (r q) -> p r q", q=64),
                        in1=etot[:, d, t, :].unsqueeze(2).broadcast_to([128, 4, 64]), op=ALU.mult), reads=[eb], writes=[s32b[d]])
                    P.op("vector", lambda e, d=d, bi=bi: e.tensor_tensor(out=S32[:, d, :], in0=S32[:, d, :], in1=bank(bi)[:, 0:256], op=ALU.add),
                         reads=[pb[bi]], writes=[s32b[d]])
                    if slot is not None:
                        P.op("scalar", lambda e, d=d, slot=slot: e.activation(out=S_store[:, d, slot, :], in_=S32[:, d, :], func=AF.Copy),
                             reads=[s32b[d]], writes=[stb[d][slot]])
                for s in range(17):
                    tf = fwd_tiles[s]
                    chain_step(0, tf, tf - 1 if tf >= 1 else None)
                    tb_ = bwd_tiles[s]
                    if tb_ == 1:
                        slot = None
                    elif tb_ == 0:
                        slot = 15
                    else:
                        slot = tb_ - 3
                    chain_step(1, tb_, slot)
                    if s < 16:
                        z_mm(s, *zw)
                if g == 0 and ck("s2"):
                    return

                ysT, ysTb = ysT_ring.next()

                def sbk_of(c):
                    return (0, 1) if c % 2 == 0 else (5, 6)

                def sc_of(c):
                    return bank(2)[:, (c % 2) * 128:(c % 2) * 128 + 128]

                def prepA(c):
                    t = 2 + c
                    u0 = 260 + c * 128
                    sbk = sbk_of(c)

                    def mmseg(e, t=t, sbk=sbk):
                        ins = None
                        for d in range(2):
                            e.matmul(bank(sbk[d])[:], lhsT=identb[:], rhs=negm[:, d, :, :].rearrange("p r i -> p (r i)"), start=True, stop=False,
                                     skip_group_check=True)
                            for r in range(4):
                                ins = e.matmul(bank(sbk[d])[:, r * 128:(r + 1) * 128], lhsT=a8[:, d, t, r:r + 1].broadcast_to([128, 128]),
                                               rhs=tri[:, d, :], start=False, stop=True, skip_group_check=True)
                        return ins
                    P.op("tensor", mmseg, reads=[gb, cbS], writes=[pb[sbk[0]], pb[sbk[1]]])
                    P.op("tensor", lambda e, u0=u0, c=c: e.matmul(sc_of(c), lhsT=BCT[:, 0, u0:u0 + 128], rhs=BCT[:, 1, u0:u0 + 128], start=True, stop=True),
                         reads=[bctb], writes=[pb[2]])

                def prepB_act(c):
                    t = 2 + c
                    sbk = sbk_of(c)
                    Dec, Decb = Dec_ring.next()
                    for d in range(2):
                        for r in range(4):
                            q = d * 4 + r
                            P.op("scalar", lambda e, Dec=Dec, d=d, r=r, q=q, t=t, sbk=sbk: e.activation(
                                out=Dec[:, d, r, :], in_=bank(sbk[d])[:, r * 128:(r + 1) * 128], func=AF.Exp, bias=nacum[:, t, q:q + 1], scale=1.0),
                                reads=[pb[sbk[d]], acb], writes=[Decb])
                    return Dec, Decb

                def prepB_mv(c, Dec, Decb):
                    t = 2 + c
                    Mv, Mb = M_ring.next()
                    P.op("vector", lambda e, Mv=Mv, Dec=Dec, c=c: e.tensor_tensor(
                        out=Mv[:].rearrange("p d r i -> p (d r) i"), in0=sc_of(c).unsqueeze(1).broadcast_to([128, 8, 128]),
                        in1=Dec[:].rearrange("p d r i -> p (d r) i"), op=ALU.mult), reads=[pb[2], Decb], writes=[Mb])
                    vv, vb = v_ring.next()
                    P.op("gpsimd", lambda e, vv=vv, t=t: e.tensor_tensor(
                        out=vv[:], in0=xs_tm[:, t, :].rearrange("p (r q) -> p r q", q=64).unsqueeze(1).broadcast_to([128, 2, 4, 64]),
                        in1=dt8[:, :, t, :].unsqueeze(3).broadcast_to([128, 2, 4, 64]), op=ALU.mult), reads=[gb, xsb_], writes=[vb])
                    return (Mv, Mb, vv, vb)

                def fin_1(c, Mv, Mb, vv, vb):
                    t = 2 + c
                    u0 = 260 + c * 128

                    def mmy(e, Mv=Mv, vv=vv):
                        ins = None
                        for r in range(4):
                            o = bank(3)[:, r * 64:(r + 1) * 64]
                            e.matmul(o, lhsT=Mv[:, 0, r, :], rhs=vv[:, 0, r, :], start=True, stop=False)
                            e.matmul(o, lhsT=Mv[:, 1, r, :], rhs=vv[:, 1, r, :], start=False, stop=False)
                            e.matmul(o, lhsT=dD[:, 0, r, :], rhs=xs_tm[:, t, r * 64:(r + 1) * 64], start=False, stop=False)
                            ins = e.matmul(o, lhsT=dD[:, 1, r, :], rhs=xs_tm[:, t, r * 64:(r + 1) * 64], start=False, stop=True)
                        return ins
                    P.op("tensor", mmy, reads=[Mb, vb, ddb, xsb_], writes=[pb[3]])
                    ib = 4

                    def mmi(e, c=c, u0=u0, ib=ib):
                        ins = None
                        for d in range(2):
                            ins = e.matmul(bank(ib)[:, d * 256:(d + 1) * 256], lhsT=BCT[:, 1, u0:u0 + 128], rhs=S_store[:, d, c, :], start=True, stop=True)
                        return ins
                    P.op("tensor", mmi, reads=[bctb, stb[0][c], stb[1][c]], writes=[pb[ib]])
                    gi, gib = gi_ring.next()
                    P.op("vector", lambda e, gi=gi, ib=ib, t=t: e.tensor_tensor(
                        out=gi[:].rearrange("p d (r q) -> p d r q", q=64), in0=bank(ib)[:].rearrange("p (d r q) -> p d r q", d=2, q=64),
                        in1=ecum[:, :, t, :].unsqueeze(3).broadcast_to([128, 2, 4, 64]), op=ALU.mult), reads=[pb[ib], eb], writes=[gib])
                    t1, t1b = t1_ring.next()
                    P.op("vector", lambda e, t1=t1, gi=gi: e.tensor_tensor(out=t1[:], in0=gi[:, 0, :], in1=gi[:, 1, :], op=ALU.add), reads=[gib], writes=[t1b])
                    P.op("vector", lambda e, t1=t1: e.tensor_tensor(out=t1[:], in0=bank(3)[:, 0:256], in1=t1[:], op=ALU.add), reads=[pb[3]], writes=[t1b])
                    t2, t2b = t2_ring.next()
                    P.op("vector", lambda e, t1=t1, t2=t2, c=c: e.tensor_tensor(out=t2[:], in0=t1[:], in1=sz[:, c, :], op=ALU.mult),
                         reads=[t1b, szb[c]], writes=[t2b])
                    ss1, sb1 = ss_ring.next()
                    P.op("scalar", lambda e, t2=t2, ss1=ss1: e.activation(out=junk[:], in_=t2[:], func=AF.Square, accum_out=ss1[:, 0:1]), reads=[t2b], writes=[jb, sb1])
                    P.op("scalar", lambda e, ss1=ss1: e.activation(out=ss1[:, 1:2], in_=ss1[:, 0:1], func=AF.Ln, scale=1.0 / 256.0, bias=epsc[:, 0:1]), reads=[sb1], writes=[sb1])
                    P.op("scalar", lambda e, ss1=ss1: e.activation(out=ss1[:, 1:2], in_=ss1[:, 1:2], func=AF.Exp, scale=-0.5), reads=[sb1], writes=[sb1])
                    return (t2, t2b, ss1, sb1)

                def fin_2(c, t2, t2b, ss1, sb1):
                    yo, yob = yo_ring.next()
                    P.op("vector", lambda e, yo=yo, t2=t2, ss1=ss1: e.tensor_scalar(out=yo[:], in0=t2[:], scalar1=ss1[:, 1:2], scalar2=None, op0=ALU.mult),
                         reads=[t2b, sb1], writes=[yob])
                    return (yo, yob)

                def fin_b(c, yo, yob):
                    pst = bank(7).bitcast(BF16)

                    def tr2(e, yo=yo, pst=pst):
                        e.transpose(pst[:, 0:128], yo[:, 0:128], identb[:])
                        return e.transpose(pst[:, 128:256], yo[:, 128:256], identb[:])
                    P.op("tensor", tr2, reads=[yob], writes=[pb[7]])
                    P.op("vector", lambda e, ysT=ysT, c=c, pst=pst: e.tensor_copy(
                        out=ysT[:, :, c * 128:(c + 1) * 128], in_=pst[:, 0:256].rearrange("p (a b) -> p a b", b=128)), reads=[pb[7]], writes=[ysTb])

                hnds = {}
                prepA(0)
                prepA(1)
                hnds[0] = prepB_mv(0, *prepB_act(0))
                prepA(2)
                hnds[1] = prepB_mv(1, *prepB_act(1))
                prev = None
                for c in range(16):
                    if c + 3 < 16:
                        prepA(c + 3)
                    dec = prepB_act(c + 2) if c + 2 < 16 else None
                    f1 = fin_1(c, *hnds.pop(c))
                    if dec is not None:
                        hnds[c + 2] = prepB_mv(c + 2, *dec)
                    cur = fin_2(c, *f1)
                    if prev is not None:
                        fin_b(c - 1, *prev)
                    prev = cur
                fin_b(15, *prev)
                dstd = ysT_d[g * 256:(g + 1) * 256, :].rearrange("(fc p) t -> p fc t", p=128)
                P.dma("gpsimd", lambda e, ysT=ysT, dstd=dstd: e.dma_start(out=dstd, in_=ysT[:]), reads=[ysTb], writes=[ysb])
                if g == 0 and ck("s4"):
                    return

        def phase_F(ph):
            pb = [Buf("bank%d" % i, True) for i in range(8)]
            mT = sb("mT", [128, 8, NLAT], BF16, ph)
            wA = sb("wA", [128, 16, 1024], BF16, ph)
            wG = sb("wG", [128, 8, 1024], BF16, ph)
            yb_ring = Ring([sb("yblk%d" % i, [128, 16, 512], BF16, ph) for i in range(2)], "yblk")
            sg_ring = Ring([sb("sg%d" % i, [128, 512], F32, ph) for i in range(2)], "sg")
            tm_ring = Ring([sb("tm%d" % i, [128, 512], F32, ph) for i in range(2)], "tm")
            xt_ring = Ring([sb("xF%d" % i, [128, 1024], F32, ph) for i in range(3)], "xF")
            ot_ring = Ring([sb("oF%d" % i, [128, 1024], F32, ph) for i in range(2)], "oF")
            junk = sb("junkF", [128, 512], BF16, ph)
            jb = Buf("junkF")
            ssf = sb("ssf", [128, 4], F32, ph)
            wst_ring = Ring([sb("wstF%d" % i, [128, 8, 256], F32, ph) for i in range(2)], "wstF")
            mTb = [Buf("mT%d" % i) for i in range(4)]
            wAb = [Buf("wA%d" % i) for i in range(2)]
            wGb = [Buf("wG%d" % i) for i in range(2)]
            for br in range(2):
                src_o = (w_ret_o_d if br == 0 else w_ssd_o_d).rearrange("(kc p) n -> p kc n", p=128)
                scol = V_GNW if br == 0 else V_SNW
                for cu in range(2):
                    for kq in range(4):
                        alt = (br == 0 and cu == 0)
                        load_w(wst_ring, None, src_o, cu * 512, 512, dst=wA[:, kq * 4:(kq + 1) * 4, cu * 512:(cu + 1) * 512], dstbuf=wAb[cu],
                               scale_col=scol, kc0=kq * 4, nkc=4, cast_eng="scalar" if alt else "gpsimd")
                    for kq in range(2):
                        load_w(wst_ring, None, w_in_v, 12352 + br * 1024 + cu * 512, 512, dst=wG[:, kq * 4:(kq + 1) * 4, cu * 512:(cu + 1) * 512],
                               dstbuf=wGb[cu], kc0=kq * 4, nkc=4, cast_eng="scalar", q="gpsimd")
                scr = yrT_d if br == 0 else ysT_d
                scrb = yrb if br == 0 else ysb
                for tb in range(4):
                    yblk, yblkb = yb_ring.next()
                    P.dma("sync", lambda e, yblk=yblk, tb=tb, scr=scr: e.dma_start(
                        out=yblk[:], in_=scr[:, tb * 512:(tb + 1) * 512].rearrange("(kc p) t -> p kc t", p=128)), reads=[scrb], writes=[yblkb])
                    for fo in range(8):
                        pa, pab = bank(fo % 2), pb[fo % 2]
                        pg, pgb = bank(2 + fo % 2), pb[2 + fo % 2]

                        def mma(e, yblk=yblk, fo=fo, pa=pa):
                            ins = None
                            for kc in range(16):
                                ins = e.matmul(pa[:], lhsT=wA[:, kc, fo * 128:(fo + 1) * 128], rhs=yblk[:, kc, :], start=(kc == 0), stop=(kc == 15))
                            return ins
                        P.op("tensor", mma, reads=[yblkb, wAb[fo // 4]], writes=[pab])

                        def mmg(e, fo=fo, tb=tb, pg=pg):
                            ins = None
                            for kc in range(8):
                                ins = e.matmul(pg[:], lhsT=wG[:, kc, fo * 128:(fo + 1) * 128], rhs=hT[:, kc, 256 + tb * 512:256 + (tb + 1) * 512],
                                               start=(kc == 0), stop=(kc == 7))
                            return ins
                        P.op("tensor", mmg, reads=[wGb[fo // 4]], writes=[pgb])
                        sg, sgb = sg_ring.next()
                        P.op("scalar", lambda e, sg=sg, pg=pg: e.activation(out=sg[:], in_=pg[:], func=AF.Sigmoid), reads=[pgb], writes=[sgb])
                        mdst = mT[:, fo, tb * 512:(tb + 1) * 512]
                        if br == 0:
                            P.op("vector", lambda e, mdst=mdst, pa=pa, sg=sg: e.tensor_tensor(out=mdst, in0=pa[:], in1=sg[:], op=ALU.mult),
                                 reads=[pab, sgb], writes=[mTb[tb]])
                        else:
                            tm, tmb = tm_ring.next()
                            P.op("vector", lambda e, tm=tm, pa=pa, sg=sg: e.tensor_tensor(out=tm[:], in0=pa[:], in1=sg[:], op=ALU.mult),
                                 reads=[pab, sgb], writes=[tmb])
                            P.op("gpsimd", lambda e, mdst=mdst, tm=tm: e.tensor_tensor(out=mdst, in0=mdst, in1=tm[:], op=ALU.add),
                                 reads=[tmb], writes=[mTb[tb]])
                if br == 0 and ck("f1"):
                    return
            src_o = w_out_d.rearrange("(kc p) n -> p kc n", p=128)
            for cu in range(2):
                for kq in range(2):
                    load_w(wst_ring, None, src_o, cu * 512, 512, dst=wA[:, kq * 4:(kq + 1) * 4, cu * 512:(cu + 1) * 512], dstbuf=wAb[cu],
                           kc0=kq * 4, nkc=4, cast_eng="scalar")
            outb = Buf("out")
            ssf3 = [sb("ssf3_%d" % i, [128, 4], F32, ph) for i in range(3)]

            def wo_a(t):
                po = bank(4 + 2 * (t % 2), 2)
                pob = [pb[4 + 2 * (t % 2)], pb[5 + 2 * (t % 2)]]
                ssf_ = ssf3[t % 3]

                def mmo(e, t=t, po=po):
                    ins = None
                    for half in range(2):
                        for kc in range(8):
                            ins = e.matmul(po[:, half * 512:(half + 1) * 512], lhsT=mT[:, kc, t * 128:(t + 1) * 128], rhs=wA[:, kc, half * 512:(half + 1) * 512],
                                           start=(kc == 0), stop=(kc == 7))
                    return ins
                P.op("tensor", mmo, reads=wAb + [mTb[t // 4]], writes=pob)
                return (po, pob, ssf_)

            def wo_sq(t, po, pob, ssf_):
                sfb = Buf("ssf")
                for half in range(2):
                    P.op("scalar", lambda e, po=po, half=half, ssf_=ssf_: e.activation(out=junk[:], in_=po[:, half * 512:(half + 1) * 512], func=AF.Square,
                                                                                       accum_out=ssf_[:, half:half + 1]), reads=[pob[half]], writes=[jb, sfb])
                xt, xtb = xt_ring.next()
                P.dma("sync", lambda e, xt=xt, t=t: e.dma_start(out=xt[:], in_=x_d[t * 128:(t + 1) * 128, :]), writes=[xtb])
                return (po, pob, ssf_, sfb, xt, xtb)

            def wo_b(t, po, pob, ssf_, sfb, xt, xtb):
                P.op("vector", lambda e: e.tensor_tensor(out=ssf_[:, 2:3], in0=ssf_[:, 0:1], in1=ssf_[:, 1:2], op=ALU.add), reads=[sfb], writes=[sfb])
                P.op("scalar", lambda e: e.activation(out=ssf_[:, 3:4], in_=ssf_[:, 2:3], func=AF.Sqrt, scale=1.0 / 1024.0, bias=EPS), reads=[sfb], writes=[sfb])
                P.op("vector", lambda e: e.reciprocal(out=ssf_[:, 3:4], in_=ssf_[:, 3:4]), reads=[sfb], writes=[sfb])
                ot, otb = ot_ring.next()
                P.op("vector", lambda e, ot=ot: e.scalar_tensor_tensor(out=ot[:], in0=po[:], scalar=ssf_[:, 3:4], in1=Gt[:], op0=ALU.mult, op1=ALU.mult),
                     reads=pob + [sfb], writes=[otb])
                P.op("gpsimd", lambda e, ot=ot: e.tensor_tensor(out=ot[:], in0=ot[:], in1=xt[:], op=ALU.add), reads=[xtb], writes=[otb])
                P.dma("gpsimd", lambda e, ot=ot: e.dma_start(out=out_d[t * 128:(t + 1) * 128, :], in_=ot[:]), reads=[otb], writes=[outb])

            pend = wo_sq(0, *wo_a(0))
            for t in range(16):
                mm_next = wo_a(t + 1) if t < 15 else None
                wo_b(t, *pend)
                pend = wo_sq(t + 1, *mm_next) if mm_next is not None else None
            P.barrier()
            if dbg_spec is not None and stop_after == 3:
                dump(mT[:, 0, 0:512], 512)
                dump(mT[:, 5, 1024:1536], 512)

        yrb = Buf("yrT_d")
        ysb = Buf("ysT_d")
        if stop_after is None or stop_after in (1, 3, 5):
            with ExitStack() as ph:
                phase_R(ph)
            P.barrier()
        if stop_after is None or stop_after in (2, 3, 5, 6):
            with ExitStack() as ph:
                phase_S(ph)
            P.barrier()
        if stop_after is None or stop_after in (3, 4, 6):
            with ExitStack() as ph:
                phase_F(ph)
            P.barrier()

        if dbg_spec is not None and stop_after == 1:
            with nc.sbuf_tensor("dbl", [128, 2048], BF16) as dbl:
                lb = Buf()
                for r in range(2):
                    P.dma("sync", lambda e, r=r: e.dma_start(out=dbl[:], in_=yrT_d[r * 1024:r * 1024 + 128, :]), writes=[lb])
                    dump(dbl[:], 2048, reads=[lb])

        if dbg_spec is not None and stop_after == 2:
            with nc.sbuf_tensor("dbl2", [128, 2048], BF16) as dbl:
                lb = Buf()
                for r in range(2):
                    P.dma("sync", lambda e, r=r: e.dma_start(out=dbl[:], in_=ysT_d[r * 128:r * 128 + 128, :]), writes=[lb])
                    dump(dbl[:], 2048, reads=[lb])

        if dbg_spec is not None and stop_after == 3:
            with nc.sbuf_tensor("dbl3", [128, 512], BF16) as dbl:
                lb = Buf()
                for r in range(4):
                    P.dma("sync", lambda e, r=r: e.dma_start(out=dbl[:], in_=yrT_d[r * 512:r * 512 + 128, 512:1024]), writes=[lb])
                    dump(dbl[:], 512, reads=[lb])
                for r in range(8):
                    P.dma("sync", lambda e, r=r: e.dma_start(out=dbl[:], in_=ysT_d[r * 256 + 128:r * 256 + 256, 512:1024]), writes=[lb])
                    dump(dbl[:], 512, reads=[lb])

        if stop_after is not None and stop_after not in (3, 4, 6):
            with nc.sbuf_tensor("zt", [128, 1024], F32) as zt:
                zb = Buf()
                P.op("vector", lambda e: e.memset(zt[:], 0.0), writes=[zb])
                ob = Buf()
                for t in range(16):
                    P.dma("sync", lambda e, t=t: e.dma_start(out=out_d[t * 128:(t + 1) * 128, :], in_=zt[:]), reads=[zb], writes=[ob])
                P.barrier()
            return nc

        P.barrier()
    return nc


def host_constants():
    bf = ml_dtypes.bfloat16
    identb = np.eye(128, dtype=np.float32).astype(bf)
    swap = np.zeros((128, 128), np.float32)
    for m in range(64):
        swap[m + 64, m] = 1.0
        swap[m, m + 64] = 1.0
    swapb = swap.astype(bf)
    inv_freq = (10000.0 ** (-np.arange(64, dtype=np.float32) / 64.0)).astype(np.float32)
    fr = np.concatenate([inv_freq, inv_freq])
    sign = np.concatenate([-np.ones(64), np.ones(64)]).astype(np.float32)
    rows = np.arange(32, dtype=np.float32)
    cols = np.arange(64, dtype=np.float32)
    a0 = (rows[None, :] * fr[:, None]).astype(np.float32)
    a1 = (cols[None, :] * fr[:, None]).astype(np.float32)
    ropec = np.concatenate([np.cos(a0), np.cos(a1), np.sin(a0) * sign[:, None], np.sin(a1) * sign[:, None]], axis=1).astype(np.float32)
    j = np.arange(128, dtype=np.float32)[:, None]
    xx = np.arange(STRIP, dtype=np.float32)[None, :]
    dlt = (xx - XOFF - j).astype(np.float32)
    k = np.arange(128)[:, None]
    i = np.arange(128)[None, :]
    tri = np.stack([(k <= i), (k >= i)], axis=1).astype(np.float32).reshape(128, 256)
    L = np.stack([(k > i), (k < i)], axis=1).astype(np.float32).reshape(128, 256)
    m01 = np.stack([(i >= k), (i < k)], axis=1).astype(np.float32).reshape(128, 256)
    ones = np.ones((128, 128), np.float32)
    ssdc = np.concatenate([tri, L, m01, ones], axis=1).astype(np.float32)
    NEG = -30000.0
    negm = np.stack([np.where(i >= k, 0.0, NEG), np.where(i < k, 0.0, NEG)], axis=1)
    negm = np.repeat(negm[:, :, None, :], 4, axis=2).reshape(128, 1024).astype(np.float32).astype(bf)
    sel = np.zeros((8, 8, 128), np.float32)
    for q in range(8):
        sel[q, q, :] = 1.0
    return dict(identb=identb, swapb=swapb, ropec=ropec, dlt=dlt, ssdc=ssdc, negm=negm, self=sel.reshape(8, 1024),
                identf=np.eye(128, dtype=np.float32))


def host_inputs(inputs):
    f = np.float32
    col = lambda v: np.ascontiguousarray(np.asarray(v, f).reshape(-1, 128).T)
    conv_w = np.asarray(inputs["ssd_conv_w"][0], f)
    vecs = np.concatenate([
        col(inputs["norm_pre_w"][0]), col(inputs["b_mod"][0]),
        np.concatenate([col(conv_w[kk]) for kk in range(5)], axis=1),
        col(inputs["ssd_conv_b"][0]), col(inputs["ret_gn_w"][0]), col(inputs["ssd_norm_w"][0])], axis=1).astype(f)
    assert vecs.shape == (128, NV)
    rows = np.concatenate([
        np.asarray(inputs["norm_post_w"][0], f), np.asarray(inputs["b_mod"][0], f)[2048:3072],
        np.asarray(inputs["ssd_D"][0], f), np.asarray(inputs["ssd_dt_bias"][0], f).reshape(-1),
        np.asarray(inputs["ssd_a_log"][0], f).reshape(-1), np.asarray(inputs["ret_decay"][0], f).reshape(-1)])[None, :].astype(f)
    assert rows.shape == (1, NR)
    shared = dict(
        w_mod=np.ascontiguousarray(inputs["w_mod"][0], f), w_in=np.ascontiguousarray(inputs["w_in"][0], f),
        w_ret_o=np.ascontiguousarray(inputs["w_ret_o"][0], f), w_ssd_o=np.ascontiguousarray(inputs["w_ssd_o"][0], f),
        w_out=np.ascontiguousarray(inputs["w_out"][0], f), vecs=vecs, rows=rows)
    shared.update(host_constants())
    maps = []
    for b in range(8):
        cc = np.stack([np.asarray(inputs["c"][b], f), np.asarray(inputs["c_ctx"], f)], axis=1)
        cct = np.ascontiguousarray(cc.reshape(8, 128, 2).transpose(1, 0, 2).reshape(128, 16))
        m = dict(shared)
        m.update(x=np.ascontiguousarray(inputs["x"][b], f), ctx=np.ascontiguousarray(inputs["ctx"][b], f), cct=cct)
        maps.append(m)
    return maps


def kernel(**inputs):
    maps = host_inputs(inputs)
    nc = build_program()
    res = run_bass_kernel_spmd(nc, maps, core_ids=list(range(8)))
    return np.stack([np.asarray(r["out"], np.float32) for r in res.results], axis=0)
```
